# Optimizing a Trainium2 kernel written in Bass

```python
import jax, jax.numpy as jnp
from jax import lax
import numpy as np

D_MODEL = 2048
BATCH = 2
SEQ = 16384
DEPTH = 2

CHUNK = 64
N_MEM = 256
EPS = 1e-6
D_FF = 256 * ((8 * D_MODEL // 3 + 255) // 256)
CONV_K = 4
BRANCH_W = D_MODEL // 2
N_BRANCH = 3

ML_HEADS = 4
ML_V = BRANCH_W // ML_HEADS
ML_QK = ML_V // 2
ML_QK_W = ML_HEADS * ML_QK
ML_V_W = ML_HEADS * ML_V

DSA_HEADS = 8
DSA_HD = BRANCH_W // DSA_HEADS
DSA_LAT = D_MODEL // 4
IDX_HEADS = 16
IDX_D = 64
INDEX_TOPK = 256
Q_BLOCK = 128

SSM_P = 64
SSM_HEADS = BRANCH_W // SSM_P
SSM_G = 2
SSM_N = 128
SSM_XBC = BRANCH_W + 2 * SSM_G * SSM_N

XA_HEADS = 4
XA_HD = 128
XA_W = XA_HEADS * XA_HD

IN_SIZES = (ML_QK_W, ML_QK_W, ML_V_W, ML_V_W, ML_HEADS, ML_HEADS,
            BRANCH_W, DSA_LAT, IDX_HEADS * IDX_D, IDX_D, IDX_HEADS,
            BRANCH_W, SSM_XBC, SSM_HEADS,
            N_BRANCH * D_MODEL)
IN_SPLITS = tuple(int(s) for s in np.cumsum(IN_SIZES)[:-1])
IN_W = sum(IN_SIZES)

kernel_name = 'hybrid_mlstm_dsa_ssd_macaron'


def rmsnorm(x, g):
    xf = x.astype(jnp.float32)
    y = xf * lax.rsqrt(jnp.mean(xf * xf, axis=-1, keepdims=True) + EPS)
    return (y * g.astype(jnp.float32)).astype(x.dtype)


def swiglu(h, w_up, w_down):
    a, b = jnp.split(h @ w_up, 2, axis=-1)
    return (jax.nn.silu(a) * b) @ w_down


def causal_dwconv(x, w):
    return lax.conv_general_dilated(
        x, w[:, None, :].astype(x.dtype), (1,), [(w.shape[0] - 1, 0)],
        dimension_numbers=('NWC', 'WIO', 'NWC'), feature_group_count=x.shape[-1])


def to_chunks(a):
    return a.reshape((a.shape[0], a.shape[1] // CHUNK, CHUNK) + a.shape[2:]).swapaxes(0, 1)


def from_chunks(a):
    return a.swapaxes(0, 1).reshape((a.shape[1], a.shape[0] * a.shape[2]) + a.shape[3:])


def mlstm_chunkwise(q, k, v, i_pre, f_pre):
    bsz, _, nh, dk = q.shape
    dv = v.shape[-1]
    causal = jnp.tril(jnp.ones((CHUNK, CHUNK), bool))[None, :, :, None]
    b_cum = jnp.cumsum(to_chunks(jax.nn.log_sigmoid(f_pre)), axis=2)

    def step(carry, inp):
        C, n, m = carry
        qc, kc, vc, ic, bc = inp
        dmat = bc[:, :, None, :] - bc[:, None, :, :] + ic[:, None, :, :]
        dmat = jnp.where(causal, dmat, -jnp.inf)
        inter = bc + m[:, None, :]
        m_t = jnp.maximum(inter, dmat.max(axis=2))
        s = jnp.einsum('blhd,bshd->blsh', qc, kc) * jnp.exp(dmat - m_t[:, :, None, :])
        decay = jnp.exp(inter - m_t)
        num = jnp.einsum('blsh,bshv->blhv', s, vc) + decay[..., None] * jnp.einsum('bhvd,blhd->blhv', C, qc)
        den = s.sum(axis=2) + decay * jnp.einsum('bhd,blhd->blh', n, qc)
        h = num / jnp.maximum(jnp.abs(den), jnp.exp(-m_t))[..., None]
        b_end = bc[:, -1]
        g = b_end[:, None, :] - bc + ic
        m_new = jnp.maximum(b_end + m, g.max(axis=1))
        wk = jnp.exp(g - m_new[:, None, :])
        sdec = jnp.exp(b_end + m - m_new)
        C_new = sdec[..., None, None] * C + jnp.einsum('blh,blhv,blhd->bhvd', wk, vc, kc)
        n_new = sdec[..., None] * n + jnp.einsum('blh,blhd->bhd', wk, kc)
        return (C_new, n_new, m_new), h

    init = (jnp.zeros((bsz, nh, dv, dk), jnp.float32),
            jnp.zeros((bsz, nh, dk), jnp.float32),
            jnp.zeros((bsz, nh), jnp.float32))
    _, h = lax.scan(step, init, (to_chunks(q), to_chunks(k), to_chunks(v), to_chunks(i_pre), b_cum))
    return from_chunks(h)


def ssd_chunked(xs, Bm, Cm, dt, A):
    bsz, _, nh, p = xs.shape
    rep = nh // Bm.shape[2]
    n = Bm.shape[-1]
    causal = jnp.tril(jnp.ones((CHUNK, CHUNK), bool))[None, :, :, None]
    seg_all = jnp.cumsum(to_chunks(dt * A), axis=2)

    def step(state, inp):
        xc, bc, cc, dtc, seg = inp
        Bh = jnp.repeat(bc, rep, axis=2)
        Ch = jnp.repeat(cc, rep, axis=2)
        diff = seg[:, :, None, :] - seg[:, None, :, :]
        lmat = jnp.exp(jnp.where(causal, diff, -jnp.inf))
        scores = jnp.einsum('blhn,bshn->blsh', Ch, Bh) * lmat * dtc[:, None, :, :]
        y = (jnp.einsum('blsh,bshp->blhp', scores, xc)
             + jnp.einsum('blhn,bhpn->blhp', Ch, state) * jnp.exp(seg)[..., None])
        w_end = jnp.exp(seg[:, -1:, :] - seg) * dtc
        state = (jnp.exp(seg[:, -1])[:, :, None, None] * state
                 + jnp.einsum('bsh,bshn,bshp->bhpn', w_end, Bh, xc))
        return state, y

    init = jnp.zeros((bsz, nh, p, n), jnp.float32)
    _, y = lax.scan(step, init, (to_chunks(xs), to_chunks(Bm), to_chunks(Cm), to_chunks(dt), seg_all))
    return from_chunks(y)


def dsa_attention(q, k, v, q_idx, k_idx, w_idx):
    bsz, s_len, nh, dh = q.shape
    top = min(INDEX_TOPK, s_len // 4)
    key_pos = jnp.arange(s_len)

    def block(start):
        qb = lax.dynamic_slice_in_dim(q, start, Q_BLOCK, axis=1)
        qib = lax.dynamic_slice_in_dim(q_idx, start, Q_BLOCK, axis=1)
        wib = lax.dynamic_slice_in_dim(w_idx, start, Q_BLOCK, axis=1)
        pos = start + jnp.arange(Q_BLOCK)
        limit = (pos // CHUNK + 1) * CHUNK
        admissible = key_pos[None, :] < limit[:, None]
        raw = jnp.einsum('bqhd,bsd->bqhs', qib, k_idx)
        score = jnp.einsum('bqhs,bqh->bqs', jax.nn.relu(raw), wib).astype(jnp.float32)
        score = jnp.where(admissible[None], score, -jnp.inf)
        _, idx = lax.top_k(score, top)
        valid = idx < limit[None, :, None]
        k_sel = jax.vmap(lambda kb, ib: kb[ib])(k, idx)
        v_sel = jax.vmap(lambda vb, ib: vb[ib])(v, idx)
        logits = jnp.einsum('bqhd,bqjhd->bhqj', qb, k_sel).astype(jnp.float32)
        logits = jnp.where(valid[:, None], logits, -jnp.inf)
        p = jax.nn.softmax(logits, axis=-1).astype(v.dtype)
        return jnp.einsum('bhqj,bqjhd->bqhd', p, v_sel)

    out = lax.map(block, jnp.arange(s_len // Q_BLOCK) * Q_BLOCK)
    return out.swapaxes(0, 1).reshape(bsz, s_len, nh * dh)


def hybrid_mixer(h, w_in, ml_conv, ml_i_bias, ml_f_bias, ml_out_norm,
                 dsa_q_norm, dsa_k_norm, dsa_kv_norm, dsa_w_uk, dsa_w_uv, idx_k_norm,
                 ssm_conv, ssm_conv_b, ssm_dt_bias, ssm_a_log, ssm_d, ssm_norm,
                 w_branch, w_out):
    bsz, s, _ = h.shape
    f32 = jnp.float32
    (ml_q, ml_k, ml_v, ml_o, ml_i, ml_f,
     d_q, d_kv, i_q, i_k, i_w,
     s_z, s_xbc, s_dt, gates) = jnp.split(h @ w_in, IN_SPLITS, axis=-1)

    qk = jax.nn.silu(causal_dwconv(jnp.concatenate([ml_q, ml_k], axis=-1), ml_conv))
    q_m = qk[..., :ML_QK_W].reshape(bsz, s, ML_HEADS, ML_QK).astype(f32) * ML_QK ** -0.5
    k_m = qk[..., ML_QK_W:].reshape(bsz, s, ML_HEADS, ML_QK).astype(f32)
    v_m = ml_v.reshape(bsz, s, ML_HEADS, ML_V).astype(f32)
    h_m = mlstm_chunkwise(q_m, k_m, v_m, (ml_i + ml_i_bias).astype(f32), (ml_f + ml_f_bias).astype(f32))
    h_m = rmsnorm(h_m, ml_out_norm.reshape(ML_HEADS, ML_V)).reshape(bsz, s, ML_V_W).astype(h.dtype)
    y_a = jax.nn.sigmoid(ml_o) * h_m

    q_d = rmsnorm(d_q.reshape(bsz, s, DSA_HEADS, DSA_HD), dsa_q_norm) * DSA_HD ** -0.5
    c_kv = rmsnorm(d_kv, dsa_kv_norm)
    k_d = rmsnorm((c_kv @ dsa_w_uk).reshape(bsz, s, DSA_HEADS, DSA_HD), dsa_k_norm)
    v_d = (c_kv @ dsa_w_uv).reshape(bsz, s, DSA_HEADS, DSA_HD)
    y_b = dsa_attention(q_d, k_d, v_d,
                        i_q.reshape(bsz, s, IDX_HEADS, IDX_D) * IDX_D ** -0.5,
                        rmsnorm(i_k, idx_k_norm),
                        i_w * IDX_HEADS ** -0.5)

    xbc = jax.nn.silu(causal_dwconv(s_xbc, ssm_conv) + ssm_conv_b)
    x_s = xbc[..., :BRANCH_W].reshape(bsz, s, SSM_HEADS, SSM_P).astype(f32)
    b_s = xbc[..., BRANCH_W:BRANCH_W + SSM_G * SSM_N].reshape(bsz, s, SSM_G, SSM_N).astype(f32)
    c_s = xbc[..., BRANCH_W + SSM_G * SSM_N:].reshape(bsz, s, SSM_G, SSM_N).astype(f32)
    dt = jax.nn.softplus((s_dt + ssm_dt_bias).astype(f32))
    a = -jnp.exp(ssm_a_log.astype(f32))
    y_s = ssd_chunked(x_s, b_s, c_s, dt, a) + ssm_d.astype(f32)[:, None] * x_s
    y_c = rmsnorm(y_s.reshape(bsz, s, BRANCH_W).astype(h.dtype) * jax.nn.silu(s_z), ssm_norm)

    g = jax.nn.sigmoid(gates).reshape(bsz, s, N_BRANCH, D_MODEL)
    merged = (g[:, :, 0] * (y_a @ w_branch[0])
              + g[:, :, 1] * (y_b @ w_branch[1])
              + g[:, :, 2] * (y_c @ w_branch[2]))
    return merged @ w_out


def memory_cross_attention(h, m, wq, wkv, q_norm, k_norm, wo):
    bsz, s, _ = h.shape
    n_mem = m.shape[1]
    q = rmsnorm((h @ wq).reshape(bsz, s, XA_HEADS, XA_HD), q_norm) * XA_HD ** -0.5
    kv = m @ wkv
    k = rmsnorm(kv[..., :XA_W].reshape(bsz, n_mem, XA_HEADS, XA_HD), k_norm)
    v = kv[..., XA_W:].reshape(bsz, n_mem, XA_HEADS, XA_HD)
    p = jax.nn.softmax(jnp.einsum('bshd,bmhd->bhsm', q, k).astype(jnp.float32), axis=-1).astype(v.dtype)
    o = jnp.einsum('bhsm,bmhd->bshd', p, v).reshape(bsz, s, XA_W)
    return o @ wo


def setup_inputs(seed: int = 0) -> dict:
    key = jax.random.key(seed)
    keys = iter(jax.random.split(key, 64))
    f32 = jnp.float32
    L = (DEPTH,)

    def nrm(shape, scale):
        return scale * jax.random.normal(next(keys), shape, f32)

    def gain(shape):
        return 1.0 + 0.05 * jax.random.normal(next(keys), shape, f32)

    dt0 = jnp.exp(jax.random.uniform(next(keys), L + (SSM_HEADS,), f32,
                                     np.log(1e-3).astype(np.float32), np.log(1e-1).astype(np.float32)))
    return {
        'x': nrm((BATCH, SEQ, D_MODEL), 1.0),
        'mem': nrm((BATCH, N_MEM, D_MODEL), 1.0),
        'ffn1_norm': gain(L + (D_MODEL,)),
        'ffn1_w_up': nrm(L + (D_MODEL, 2 * D_FF), D_MODEL ** -0.5),
        'ffn1_w_down': nrm(L + (D_FF, D_MODEL), D_FF ** -0.5),
        'mix_norm': gain(L + (D_MODEL,)),
        'w_in': nrm(L + (D_MODEL, IN_W), D_MODEL ** -0.5),
        'ml_conv': nrm(L + (CONV_K, 2 * ML_QK_W), CONV_K ** -0.5),
        'ml_i_bias': nrm(L + (ML_HEADS,), 0.1),
        'ml_f_bias': 3.0 + nrm(L + (ML_HEADS,), 0.5),
        'ml_out_norm': gain(L + (ML_V_W,)),
        'dsa_q_norm': gain(L + (DSA_HD,)),
        'dsa_k_norm': gain(L + (DSA_HD,)),
        'dsa_kv_norm': gain(L + (DSA_LAT,)),
        'dsa_w_uk': nrm(L + (DSA_LAT, BRANCH_W), DSA_LAT ** -0.5),
        'dsa_w_uv': nrm(L + (DSA_LAT, BRANCH_W), DSA_LAT ** -0.5),
        'idx_k_norm': gain(L + (IDX_D,)),
        'ssm_conv': nrm(L + (CONV_K, SSM_XBC), CONV_K ** -0.5),
        'ssm_conv_b': nrm(L + (SSM_XBC,), 0.02),
        'ssm_dt_bias': dt0 + jnp.log(-jnp.expm1(-dt0)),
        'ssm_a_log': jnp.log(jax.random.uniform(next(keys), L + (SSM_HEADS,), f32, 1.0, 16.0)),
        'ssm_d': gain(L + (SSM_HEADS,)),
        'ssm_norm': gain(L + (BRANCH_W,)),
        'w_branch': nrm(L + (N_BRANCH, BRANCH_W, D_MODEL), BRANCH_W ** -0.5),
        'w_out': nrm(L + (D_MODEL, D_MODEL), D_MODEL ** -0.5),
        'xa_norm': gain(L + (D_MODEL,)),
        'xa_mem_norm': gain(L + (D_MODEL,)),
        'xa_wq': nrm(L + (D_MODEL, XA_W), D_MODEL ** -0.5),
        'xa_wkv': nrm(L + (D_MODEL, 2 * XA_W), D_MODEL ** -0.5),
        'xa_q_norm': gain(L + (XA_HD,)),
        'xa_k_norm': gain(L + (XA_HD,)),
        'xa_wo': nrm(L + (XA_W, D_MODEL), XA_W ** -0.5),
        'ffn2_norm': gain(L + (D_MODEL,)),
        'ffn2_w_up': nrm(L + (D_MODEL, 2 * D_FF), D_MODEL ** -0.5),
        'ffn2_w_down': nrm(L + (D_FF, D_MODEL), D_FF ** -0.5),
    }


def reference(x, mem, ffn1_norm, ffn1_w_up, ffn1_w_down, mix_norm, w_in,
              ml_conv, ml_i_bias, ml_f_bias, ml_out_norm,
              dsa_q_norm, dsa_k_norm, dsa_kv_norm, dsa_w_uk, dsa_w_uv, idx_k_norm,
              ssm_conv, ssm_conv_b, ssm_dt_bias, ssm_a_log, ssm_d, ssm_norm,
              w_branch, w_out, xa_norm, xa_mem_norm, xa_wq, xa_wkv, xa_q_norm, xa_k_norm, xa_wo,
              ffn2_norm, ffn2_w_up, ffn2_w_down):
    for l in range(DEPTH):
        x = x + 0.5 * swiglu(rmsnorm(x, ffn1_norm[l]), ffn1_w_up[l], ffn1_w_down[l])
        x = x + hybrid_mixer(rmsnorm(x, mix_norm[l]), w_in[l],
                             ml_conv[l], ml_i_bias[l], ml_f_bias[l], ml_out_norm[l],
                             dsa_q_norm[l], dsa_k_norm[l], dsa_kv_norm[l], dsa_w_uk[l], dsa_w_uv[l], idx_k_norm[l],
                             ssm_conv[l], ssm_conv_b[l], ssm_dt_bias[l], ssm_a_log[l], ssm_d[l], ssm_norm[l],
                             w_branch[l], w_out[l])
        x = x + memory_cross_attention(rmsnorm(x, xa_norm[l]), rmsnorm(mem, xa_mem_norm[l]),
                                       xa_wq[l], xa_wkv[l], xa_q_norm[l], xa_k_norm[l], xa_wo[l])
        x = x + 0.5 * swiglu(rmsnorm(x, ffn2_norm[l]), ffn2_w_up[l], ffn2_w_down[l])
    return x
```

```python
from concourse.bass_utils import run_bass_kernel_spmd
import contextlib
import numpy as np
import concourse.bass as bass
import concourse.mybir as mybir

F32 = mybir.dt.float32
BF16 = mybir.dt.bfloat16
AF = mybir.ActivationFunctionType
ALU = mybir.AluOpType
AX = mybir.AxisListType

SEM_MAX = 30000
N_DMA_SEMS = 12


class Buf:
    __slots__ = ("name", "last_w", "readers")

    def __init__(self, name=""):
        self.name = name
        self.last_w = None
        self.readers = []


class Op:
    __slots__ = ("eng", "fn", "deps", "signal", "sem", "val", "is_dma", "idx")


class Prog:
    ENGS = ("pe", "act", "dve", "pool", "sp")

    def __init__(self):
        self.nc = bass.Bass("TRN2", target_bir_lowering=False)
        self.ops = []
        self.stack = contextlib.ExitStack()
        self.eng_ops = {e: [] for e in self.ENGS}
        self.dma_rr = {"sp": 0, "pool": 0, "act": 0}

    def dram_in(self, name, shape, dt=F32):
        return self.nc.dram_tensor(name, list(shape), dt, kind="ExternalInput").ap()

    def dram_out(self, name, shape, dt=F32):
        return self.nc.dram_tensor(name, list(shape), dt, kind="ExternalOutput").ap()

    def dram_tmp(self, name, shape, dt=F32):
        return self.nc.dram_tensor(name, list(shape), dt, kind="Internal").ap()

    def sbuf(self, name, shape, dt=F32):
        return self.stack.enter_context(self.nc.sbuf_tensor("sb_" + name, list(shape), dt))

    def psum(self, name, shape, dt=F32):
        return self.stack.enter_context(self.nc.psum_tensor("pp_" + name, list(shape), dt))

    def op(self, eng, fn, reads=(), writes=(), is_dma=False):
        o = Op()
        o.eng = eng
        o.fn = fn
        o.is_dma = is_dma
        o.signal = False
        o.sem = None
        o.val = None
        o.idx = len(self.ops)
        deps = set()
        for b in reads:
            if b.last_w is not None:
                deps.add(b.last_w)
        for b in writes:
            if b.last_w is not None:
                deps.add(b.last_w)
            deps.update(b.readers)
        for b in reads:
            b.readers.append(o.idx)
        for b in writes:
            b.last_w = o.idx
            b.readers = []
        o.deps = deps
        self.ops.append(o)
        self.eng_ops[eng].append(o)
        return o

    def dma(self, out, in_, reads=(), writes=(), eng="sp"):
        return self.op(eng, lambda e: e.dma_start(out=out, in_=in_), reads, writes, is_dma=True)

    def emit(self):
        nc = self.nc
        ops = self.ops
        for o in ops:
            for d in o.deps:
                p = ops[d]
                if p.eng == "pe" and o.eng == "pe" and not p.is_dma:
                    continue
                p.signal = True
        for o in ops:
            if o.is_dma:
                o.signal = True
        sems = []

        def new_sem(nm):
            s = self.stack.enter_context(nc.semaphore(nm))
            sems.append(s)
            return s

        cur = {}
        cnt = {}
        dma_sems = {}
        dma_cnt = {}
        dma_prev = {}
        extra_wait = {}
        for e in self.ENGS:
            k = 0
            for o in self.eng_ops[e]:
                if o.is_dma:
                    if e not in dma_sems:
                        dma_sems[e] = [new_sem(f"dma_{e}_{i}") for i in range(N_DMA_SEMS)]
                        dma_cnt[e] = [0] * N_DMA_SEMS
                    slot = k % N_DMA_SEMS
                    k += 1
                    if dma_cnt[e][slot] + 16 > SEM_MAX:
                        dma_sems[e][slot] = new_sem(f"dma_{e}_{slot}_n")
                        dma_cnt[e][slot] = 0
                    prev = dma_prev.get((e, slot))
                    if prev is not None:
                        extra_wait[o.idx] = [(prev.sem, prev.val)]
                    dma_cnt[e][slot] += 16
                    o.sem = dma_sems[e][slot]
                    o.val = dma_cnt[e][slot]
                    dma_prev[(e, slot)] = o
                elif o.signal:
                    if e not in cur or cnt[e] + 1 > SEM_MAX:
                        cur[e] = new_sem(f"c_{e}_{len(sems)}")
                        cnt[e] = 0
                    cnt[e] += 1
                    o.sem = cur[e]
                    o.val = cnt[e]
        self.n_sems = len(sems)

        def run_engine(ename, eng):
            waited = {}
            for o in self.eng_ops[ename]:
                need = {}
                for d in o.deps:
                    p = ops[d]
                    if p.sem is None:
                        continue
                    if p.eng == "pe" and ename == "pe" and not p.is_dma:
                        continue
                    key = id(p.sem)
                    if key not in need or need[key][1] < p.val:
                        need[key] = (p.sem, p.val)
                for (s, v) in extra_wait.get(o.idx, ()):
                    key = id(s)
                    if key not in need or need[key][1] < v:
                        need[key] = (s, v)
                for key, (s, v) in need.items():
                    if waited.get(key, 0) >= v:
                        continue
                    eng.wait_ge(s, v)
                    waited[key] = v
                ins = o.fn(eng)
                if o.sem is not None:
                    ins.then_inc(o.sem, 16 if o.is_dma else 1)
            if ename in dma_sems:
                for (e2, slot), p in dma_prev.items():
                    if e2 == ename:
                        eng.wait_ge(p.sem, p.val)

        with nc.Block() as block:
            @block.sync
            def _(e):
                run_engine("sp", e)

            @block.tensor
            def _(e):
                run_engine("pe", e)

            @block.scalar
            def _(e):
                run_engine("act", e)

            @block.vector
            def _(e):
                run_engine("dve", e)

            @block.gpsimd
            def _(e):
                run_engine("pool", e)
        self.stack.close()
        return nc


D = 2048
DFF = 5632
EPS = 1e-6
TT = 512
KC = D // 128
FC = DFF // 128


class Ctx:
    def __init__(self, P):
        self.P = P
        nc = P.nc
        self.ps = [P.psum(f"ps{i}", [128, 512], F32) for i in range(8)]
        self.psb = [Buf(f"ps{i}") for i in range(8)]
        self.ps_rr = 0
        self.ones = P.sbuf("ones", [128, 128], F32)
        self.ones_b = Buf("ones")
        self.eps = P.sbuf("epsc", [128, 1], F32)
        P.op("dve", lambda e: e.memset(self.ones[:], 1.0), writes=[self.ones_b])
        P.op("dve", lambda e: e.memset(self.eps[:], EPS), writes=[self.ones_b])

    def next_ps(self):
        i = self.ps_rr % 7
        self.ps_rr += 1
        return self.ps[i], self.psb[i]


def cast_weight_tiled(P, w_f32, w_bf, kc, blk, nblk, name):
    b = Buf(name)
    src = w_f32.rearrange("(kc p) n -> p kc n", p=128)
    for j in range(nblk):
        P.dma(w_bf[j], src[:, :, j * blk:(j + 1) * blk], writes=[b], eng="pool")
    return b


def rms_rstd(P, C, x_sb, x_b, rstd, rstd_b, sq, sq_b, nch, width, dtot):
    ps, psb = C.next_ps()
    for c in range(nch):
        k = c % 2
        P.op("act", lambda e, c=c, k=k: e.activation(out=sq[k][:, :width], in_=x_sb[:, c, :width], func=AF.Square),
             reads=[x_b], writes=[sq_b[k]])
        P.op("pe", lambda e, c=c, k=k: e.matmul(ps[:, :width], C.ones[:], sq[k][:, :width], start=(c == 0), stop=(c == nch - 1)),
             reads=[sq_b[k], C.ones_b], writes=[psb])
    P.op("act", lambda e: e.activation(out=rstd[:, :width], in_=ps[:, :width], func=AF.Sqrt, bias=C.eps[:, 0:1], scale=1.0 / dtot),
         reads=[psb, C.ones_b], writes=[rstd_b])
    P.op("dve", lambda e: e.reciprocal(out=rstd[:, :width], in_=rstd[:, :width]), reads=[rstd_b], writes=[rstd_b])


def ffn_tile(P, C, S, gs, gs_b, wup_bf, wup_b, wdn_bf, wdn_b):
    x_sb, x_b = S["x"], S["x_b"]
    h_sb, h_b = S["h"], S["h_b"]
    u_sb, u_b = S["u"], S["u_b"]
    norm_to_h(P, C, S, gs, gs_b)
    for j in range(22):
        k = S["wup_rr"] % 2
        S["wup_rr"] += 1
        wa, wb, w_b, wb_b = S["wa"][k], S["wb"][k], S["wab_b"][k], S["wbb_b"][k]
        P.dma(wa[:, :, :], wup_bf[j], reads=[wup_b], writes=[w_b])
        P.dma(wb[:, :, :], wup_bf[22 + j], reads=[wup_b], writes=[wb_b])
        for s in range(2):
            pa, pa_b = C.next_ps()
            pb, pb_b = C.next_ps()
            for kc in range(KC):
                P.op("pe", lambda e, kc=kc, s=s, pa=pa, wa=wa: e.matmul(pa[:, :], wa[:, kc, s * 128:(s + 1) * 128], h_sb[:, kc, :],
                                                                     start=(kc == 0), stop=(kc == KC - 1)),
                     reads=[w_b, h_b[kc]], writes=[pa_b])
            for kc in range(KC):
                P.op("pe", lambda e, kc=kc, s=s, pb=pb, wb=wb: e.matmul(pb[:, :], wb[:, kc, s * 128:(s + 1) * 128], h_sb[:, kc, :],
                                                                     start=(kc == 0), stop=(kc == KC - 1)),
                     reads=[wb_b, h_b[kc]], writes=[pb_b])
            q = S["sa_rr"] % 2
            S["sa_rr"] += 1
            sa, sa_b = S["sa"][q], S["sa_b"][q]
            fc = j * 2 + s
            P.op("act", lambda e, pa=pa, sa=sa: e.activation(out=sa[:, :], in_=pa[:, :], func=AF.Silu),
                 reads=[pa_b], writes=[sa_b])
            P.op("dve", lambda e, pb=pb, sa=sa, fc=fc: e.tensor_tensor(out=u_sb[:, fc, :], in0=sa[:, :], in1=pb[:, :], op=ALU.mult),
                 reads=[sa_b, pb_b], writes=[u_b[fc]])
    for i in range(KC):
        k = S["wdn_rr"] % 2
        S["wdn_rr"] += 1
        wd, wd_b = S["wd"][k], S["wd_b"][k]
        P.dma(wd[:, :, :], wdn_bf[i], reads=[wdn_b], writes=[wd_b])
        po, po_b = C.next_ps()
        for kc in range(FC):
            P.op("pe", lambda e, kc=kc, po=po, wd=wd: e.matmul(po[:, :], wd[:, kc, :], u_sb[:, kc, :],
                                                             start=(kc == 0), stop=(kc == FC - 1)),
                 reads=[wd_b, u_b[kc]], writes=[po_b])
        P.op("dve", lambda e, i=i, po=po: e.scalar_tensor_tensor(out=x_sb[:, i, :], in0=po[:, :], scalar=0.5, in1=x_sb[:, i, :],
                                                               op0=ALU.mult, op1=ALU.add),
             reads=[po_b, x_b], writes=[x_b])


def norm_to_h(P, C, S, gs, gs_b):
    x_sb, x_b = S["x"], S["x_b"]
    h_sb, h_b = S["h"], S["h_b"]
    rms_rstd(P, C, x_sb, x_b, S["rstd"], S["rstd_b"], S["sq"], S["sq_b"], KC, TT, D)
    for c in range(KC):
        P.op("dve", lambda e, c=c: e.scalar_tensor_tensor(out=h_sb[:, c, :], in0=x_sb[:, c, :], scalar=gs[:, c:c + 1],
                                                        in1=S["rstd"][:, :], op0=ALU.mult, op1=ALU.mult),
             reads=[x_b, gs_b, S["rstd_b"]], writes=[h_b[c]])


def load_vec(P, name, dram_ap, shape):
    t = P.sbuf(name, shape, F32)
    b = Buf(name)
    P.dma(t[:], dram_ap, writes=[b])
    return t, b


def alloc_shared(P):
    S = {}
    S["x"] = P.sbuf("x_sb", [128, KC, TT], F32)
    S["x_b"] = Buf("x")
    S["h"] = P.sbuf("h_sb", [128, KC, TT], BF16)
    S["h_b"] = [Buf(f"h{c}") for c in range(KC)]
    S["u"] = P.sbuf("u_sb", [128, FC, TT], BF16)
    S["u_b"] = [Buf(f"u{c}") for c in range(FC)]
    S["rstd"] = P.sbuf("rstd", [128, TT], F32)
    S["rstd_b"] = Buf("rstd")
    S["sq"] = [P.sbuf(f"sq{i}", [128, TT], F32) for i in range(2)]
    S["sq_b"] = [Buf(f"sq{i}") for i in range(2)]
    S["sa"] = [P.sbuf(f"sa{i}", [128, TT], F32) for i in range(2)]
    S["sa_b"] = [Buf(f"sa{i}") for i in range(2)]
    S["wa"] = [P.sbuf(f"wa{i}", [128, KC, 256], BF16) for i in range(2)]
    S["wb"] = [P.sbuf(f"wb{i}", [128, KC, 256], BF16) for i in range(2)]
    S["wab_b"] = [Buf(f"wab{i}") for i in range(2)]
    S["wbb_b"] = [Buf(f"wbb{i}") for i in range(2)]
    S["wd"] = [P.sbuf(f"wd{i}", [128, FC, 128], BF16) for i in range(2)]
    S["wd_b"] = [Buf(f"wd{i}") for i in range(2)]
    S["wup_rr"] = 0
    S["wdn_rr"] = 0
    S["sa_rr"] = 0
    return S


NA_COLS = 6144
N_PT = 60


def alloc_A_extra(P, S, resident_kv=True):
    S["ob"] = [P.sbuf(f"ob{i}", [128, TT], BF16) for i in range(3)]
    S["ob_b"] = [Buf(f"ob{i}") for i in range(3)]
    S["ob_rr"] = 0
    if "tmp" not in S:
        S["tmp"] = [P.sbuf(f"tmpf{i}", [128, TT], F32) for i in range(2)]
        S["tmp_b"] = [Buf(f"tmpf{i}") for i in range(2)]
        S["tmp_rr"] = 0
        S["rstd2"] = P.sbuf("rstd2", [128, TT], F32)
        S["rstd2_b"] = Buf("rstd2")
    S["ckv"] = P.sbuf("ckv", [128, 4, TT], F32)
    S["ckv_b"] = Buf("ckv")
    S["ckvn"] = P.sbuf("ckvn", [128, 4, TT], BF16)
    S["ckvn_b"] = Buf("ckvn")
    if resident_kv:
        S["wuk"] = P.sbuf("wuk", [128, 4, 1024], BF16)
        S["wuv"] = P.sbuf("wuv", [128, 4, 1024], BF16)
        S["wukv_b"] = Buf("wukv")
    S["wsm"] = P.sbuf("wsm", [128, KC, 104], BF16)
    S["wsm_b"] = Buf("wsm")
    S["osm"] = P.sbuf("osm", [128, TT], F32)
    S["osm_b"] = Buf("osm")


def next_ob(S):
    k = S["ob_rr"] % 3
    S["ob_rr"] += 1
    return S["ob"][k], S["ob_b"][k]


def next_tmp(S):
    k = S["tmp_rr"] % 2
    S["tmp_rr"] += 1
    return S["tmp"][k], S["tmp_b"][k]


def head_norm(P, C, S, ps, ps_b, np_, gain, gain_b, out_ap, out_b):
    t, t_b = next_tmp(S)
    sq, sq_b = S["sq"][0], S["sq_b"][0]
    P.op("act", lambda e: e.activation(out=t[:np_, :], in_=ps[:np_, :], func=AF.Copy), reads=[ps_b], writes=[t_b])
    P.op("act", lambda e: e.activation(out=sq[:np_, :], in_=ps[:np_, :], func=AF.Square), reads=[ps_b], writes=[sq_b])
    p2, p2_b = C.next_ps()
    P.op("pe", lambda e: e.matmul(p2[:np_, :], C.ones[:np_, :np_], sq[:np_, :], start=True, stop=True),
         reads=[sq_b, C.ones_b], writes=[p2_b])
    r, r_b = S["rstd2"], S["rstd2_b"]
    P.op("act", lambda e: e.activation(out=r[:np_, :], in_=p2[:np_, :], func=AF.Sqrt, bias=C.eps[:np_, 0:1], scale=1.0 / np_),
         reads=[p2_b, C.ones_b], writes=[r_b])
    P.op("dve", lambda e: e.reciprocal(out=r[:np_, :], in_=r[:np_, :]), reads=[r_b], writes=[r_b])
    P.op("dve", lambda e: e.scalar_tensor_tensor(out=out_ap, in0=t[:np_, :], scalar=gain, in1=r[:np_, :], op0=ALU.mult, op1=ALU.mult),
         reads=[t_b, r_b, gain_b], writes=[out_b])


def setup_A(P, C, S, ntok, pre="", with_x_in=True, resident_kv=True):
    xT = P.dram_in(pre + "xT", [D, ntok]) if with_x_in else None
    g1 = P.dram_in(pre + "g1", [128, KC])
    wup = P.dram_in(pre + "wup", [D, 2 * DFF])
    wdn = P.dram_in(pre + "wdn", [DFF, D])
    gmix = P.dram_in(pre + "gmix", [128, KC])
    winA = P.dram_in(pre + "winA", [D, NA_COLS])
    winS = P.dram_in(pre + "winS", [D, 104])
    gq = P.dram_in(pre + "gq", [128, 1])
    gk = P.dram_in(pre + "gk", [128, 1])
    gkv = P.dram_in(pre + "gkv", [128, 4])
    gik = P.dram_in(pre + "gik", [64, 1])
    wuk = P.dram_in(pre + "wuk", [512, 1024])
    wuv = P.dram_in(pre + "wuv", [512, 1024])
    x1T = P.dram_out(pre + "x1T", [D, ntok])
    PT = P.dram_out(pre + "PT", [N_PT * 128, ntok], BF16)
    kiT = P.dram_out(pre + "kiT", [64, ntok], BF16)
    smT = P.dram_out(pre + "smT", [40, ntok])
    wup_bf = P.dram_tmp(pre + "wup_bf", [44, 128, KC, 256], BF16)
    wdn_bf = P.dram_tmp(pre + "wdn_bf", [16, 128, FC, 128], BF16)
    winA_bf = P.dram_tmp(pre + "winA_bf", [24, 128, KC, 256], BF16)
    alloc_A_extra(P, S, resident_kv)
    wup_b = cast_weight_tiled(P, wup, wup_bf, KC, 256, 44, "wup")
    wdn_b = cast_weight_tiled(P, wdn, wdn_bf, FC, 128, 16, "wdn")
    winA_b = cast_weight_tiled(P, winA, winA_bf, KC, 256, 24, "winA")
    if resident_kv:
        P.dma(S["wuk"][:, :, :], wuk.rearrange("(kc p) n -> p kc n", p=128), writes=[S["wukv_b"]], eng="pool")
        P.dma(S["wuv"][:, :, :], wuv.rearrange("(kc p) n -> p kc n", p=128), writes=[S["wukv_b"]], eng="pool")
    else:
        wukv_bf = P.dram_tmp(pre + "wukv_bf", [2, 128, 4, 1024], BF16)
        wukv_bf_b = Buf("wukv_bf")
        P.dma(wukv_bf[0], wuk.rearrange("(kc p) n -> p kc n", p=128), writes=[wukv_bf_b], eng="pool")
        P.dma(wukv_bf[1], wuv.rearrange("(kc p) n -> p kc n", p=128), writes=[wukv_bf_b], eng="pool")
    P.dma(S["wsm"][:, :, :], winS.rearrange("(kc p) n -> p kc n", p=128), writes=[S["wsm_b"]], eng="pool")
    g1_t, g1_b = load_vec(P, pre + "g1", g1, [128, KC])
    gm_t, gm_b = load_vec(P, pre + "gmix", gmix, [128, KC])
    gq_t, gq_b = load_vec(P, pre + "gq", gq, [128, 1])
    gk_t, gk_b = load_vec(P, pre + "gk", gk, [128, 1])
    gkv_t, gkv_b = load_vec(P, pre + "gkv", gkv, [128, 4])
    gik_t, gik_b = load_vec(P, pre + "gik", gik, [64, 1])
    P.op("dve", lambda e: e.tensor_scalar(out=gq_t[:, :], in0=gq_t[:, :], scalar1=float(128 ** -0.5), scalar2=None, op0=ALU.mult),
         reads=[gq_b], writes=[gq_b])

    return dict(locals())


def tile_A(P, C, S, A, t, ntok):
    g1_t, g1_b, gm_t, gm_b, gq_t, gq_b, gk_t, gk_b, gkv_t, gkv_b, gik_t, gik_b = (A[k] for k in ("g1_t", "g1_b", "gm_t", "gm_b", "gq_t", "gq_b", "gk_t", "gk_b", "gkv_t", "gkv_b", "gik_t", "gik_b"))
    wup_bf, wup_b, wdn_bf, wdn_b, winA_bf, winA_b, x1T, PT, kiT, smT = (A[k] for k in ("wup_bf", "wup_b", "wdn_bf", "wdn_b", "winA_bf", "winA_b", "x1T", "PT", "kiT", "smT"))
    xout_v = x1T.rearrange("(c p) n -> p c n", p=128)
    x_sb, x_b = S["x"], S["x_b"]
    h_sb, h_b = S["h"], S["h_b"]
    tok = slice(t * TT, (t + 1) * TT)
    ffn_tile(P, C, S, g1_t, g1_b, wup_bf, wup_b, wdn_bf, wdn_b)
    P.dma(xout_v[:, :, tok], x_sb[:, :, :], reads=[x_b])
    norm_to_h(P, C, S, gm_t, gm_b)

    def out_slot(slot, ob, ob_b):
        P.dma(PT[slot * 128:(slot + 1) * 128, tok], ob[:, :], reads=[ob_b])

    def proj_chunk(w_ap_fn, w_b, m=128):
        ps, ps_b = C.next_ps()
        for kc in range(KC):
            P.op("pe", lambda e, kc=kc: e.matmul(ps[:m, :], w_ap_fn(kc), h_sb[:, kc, :], start=(kc == 0), stop=(kc == KC - 1)),
                 reads=[w_b, h_b[kc]], writes=[ps_b])
        return ps, ps_b

    for blk in range(24):
        k = S["wup_rr"] % 2
        S["wup_rr"] += 1
        wa, w_b = S["wa"][k], S["wab_b"][k]
        P.dma(wa[:, :, :], winA_bf[blk], reads=[winA_b], writes=[w_b])
        for s in range(2):
            ch = blk * 2 + s
            ps, ps_b = proj_chunk(lambda kc, s=s, wa=wa: wa[:, kc, s * 128:(s + 1) * 128], w_b)
            if ch < 16 or ch >= 36:
                slot = ch if ch < 16 else ch + 12
                ob, ob_b = next_ob(S)
                if ch % 2 == 0:
                    P.op("act", lambda e, ob=ob, ps=ps: e.activation(out=ob[:, :], in_=ps[:, :], func=AF.Copy), reads=[ps_b], writes=[ob_b])
                else:
                    P.op("dve", lambda e, ob=ob, ps=ps: e.tensor_copy(out=ob[:, :], in_=ps[:, :]), reads=[ps_b], writes=[ob_b])
                out_slot(slot, ob, ob_b)
            elif ch < 24:
                ob, ob_b = next_ob(S)
                head_norm(P, C, S, ps, ps_b, 128, gq_t[:, 0:1], gq_b, ob[:, :], ob_b)
                out_slot(ch, ob, ob_b)
            elif ch < 28:
                c4 = ch - 24
                P.op("act", lambda e, c4=c4, ps=ps: e.activation(out=S["ckv"][:, c4, :], in_=ps[:, :], func=AF.Copy),
                     reads=[ps_b], writes=[S["ckv_b"]])
            else:
                ob, ob_b = next_ob(S)
                P.op("act", lambda e, ob=ob, ps=ps: e.activation(out=ob[:, :], in_=ps[:, :], func=AF.Copy, scale=0.125),
                     reads=[ps_b], writes=[ob_b])
                out_slot(ch + 12, ob, ob_b)
    ps, ps_b = proj_chunk(lambda kc: S["wsm"][:, kc, :], S["wsm_b"], m=104)
    P.op("act", lambda e, ps=ps: e.activation(out=S["osm"][:104, :], in_=ps[:104, :], func=AF.Copy), reads=[ps_b], writes=[S["osm_b"]])
    P.dma(smT[:, tok], S["osm"][64:104, :], reads=[S["osm_b"]])
    ob, ob_b = next_ob(S)
    head_norm(P, C, S, ps, ps_b, 64, gik_t[:, 0:1], gik_b, ob[:64, :], ob_b)
    P.dma(kiT[:, tok], ob[:64, :], reads=[ob_b])
    ckv, ckv_b = S["ckv"], S["ckv_b"]
    rms_rstd(P, C, ckv, ckv_b, S["rstd2"], S["rstd2_b"], S["sq"], S["sq_b"], 4, TT, 512)
    for c4 in range(4):
        P.op("dve", lambda e, c4=c4: e.scalar_tensor_tensor(out=S["ckvn"][:, c4, :], in0=ckv[:, c4, :], scalar=gkv_t[:, c4:c4 + 1],
                                                        in1=S["rstd2"][:, :], op0=ALU.mult, op1=ALU.mult),
             reads=[ckv_b, gkv_b, S["rstd2_b"]], writes=[S["ckvn_b"]])
    if "wuk" in S:
        wuk_t, wuv_t, wuk_tb, wuv_tb = S["wuk"], S["wuv"], S["wukv_b"], S["wukv_b"]
    else:
        vw = lambda w: w[:, :, :].rearrange("p a b -> p (a b)").rearrange("p (k n) -> p k n", n=1024)
        wuk_t, wuv_t, wuk_tb, wuv_tb = vw(S["wa"][0]), vw(S["wa"][1]), S["wab_b"][0], S["wab_b"][1]
        P.dma(wuk_t, A["wukv_bf"][0], reads=[A["wukv_bf_b"]], writes=[wuk_tb])
        P.dma(wuv_t, A["wukv_bf"][1], reads=[A["wukv_bf_b"]], writes=[wuv_tb])
    for hh in range(8):
        pk, pk_b = C.next_ps()
        for kc in range(4):
            P.op("pe", lambda e, kc=kc, hh=hh, pk=pk: e.matmul(pk[:, :], wuk_t[:, kc, hh * 128:(hh + 1) * 128], S["ckvn"][:, kc, :],
                                                             start=(kc == 0), stop=(kc == 3)),
                 reads=[wuk_tb, S["ckvn_b"]], writes=[pk_b])
        ob, ob_b = next_ob(S)
        head_norm(P, C, S, pk, pk_b, 128, gk_t[:, 0:1], gk_b, ob[:, :], ob_b)
        out_slot(24 + hh, ob, ob_b)
        pv, pv_b = C.next_ps()
        for kc in range(4):
            P.op("pe", lambda e, kc=kc, hh=hh, pv=pv: e.matmul(pv[:, :], wuv_t[:, kc, hh * 128:(hh + 1) * 128], S["ckvn"][:, kc, :],
                                                             start=(kc == 0), stop=(kc == 3)),
                 reads=[wuv_tb, S["ckvn_b"]], writes=[pv_b])
        ob, ob_b = next_ob(S)
        P.op("dve", lambda e, ob=ob, pv=pv: e.tensor_copy(out=ob[:, :], in_=pv[:, :]), reads=[pv_b], writes=[ob_b])
        out_slot(32 + hh, ob, ob_b)


def build_A(ntok):
    P = Prog()
    C = Ctx(P)
    S = alloc_shared(P)
    A = setup_A(P, C, S, ntok, pre="a_")
    xin_v = A["xT"].rearrange("(c p) n -> p c n", p=128)
    for t in range(ntok // TT):
        P.dma(S["x"][:, :, :], xin_v[:, :, t * TT:(t + 1) * TT], writes=[S["x_b"]])
        tile_A(P, C, S, A, t, ntok)
    return P


def perm_cols_A():
    import numpy as np
    sizes = (512, 512, 1024, 1024, 4, 4, 1024, 512, 1024, 64, 16, 1024, 1536, 16, 6144)
    off = np.concatenate([[0], np.cumsum(sizes)])
    names = ["ml_q", "ml_k", "ml_v", "ml_o", "ml_i", "ml_f", "d_q", "d_kv", "i_q", "i_k", "i_w", "s_z", "s_xbc", "s_dt", "gates"]
    r = {n: np.arange(off[i], off[i + 1]) for i, n in enumerate(names)}
    main = np.concatenate([r["ml_q"], r["ml_k"], r["ml_v"], r["d_q"], r["d_kv"], r["i_q"], r["s_xbc"]])
    small = np.concatenate([r["i_k"], r["ml_i"], r["ml_f"], r["i_w"], r["s_dt"]])
    later = np.concatenate([r["ml_o"], r["s_z"], r["gates"]])
    return main, small, later


EPS = 1e-6
SEG = 1024
NEGV = -30000.0


class CtxB:
    def __init__(self, P):
        self.P = P
        self.ps = [P.psum(f"ps{i}", [128, 512], F32) for i in range(7)]
        self.psb = [Buf(f"ps{i}") for i in range(7)]
        self.ps_rr = 0
        self.pt = P.psum("pst", [128, 1024], BF16)
        self.ptb = [Buf(f"pst{i}") for i in range(8)]
        self.pt_rr = 0
        mk = lambda n, dt=F32: (P.sbuf(n, [128, 128], dt), Buf(n))
        self.ones, self.ones_b = mk("ones")
        self.tri, self.tri_b = mk("tri")
        self.negm, self.negm_b = mk("negm")
        self.ident, self.ident_b = mk("ident")
        self.identb, self.identb_b = mk("identb", BF16)
        self.zeros, self.zeros_b = mk("zeros")
        self.eps = P.sbuf("epsc", [128, 1], F32)
        self.eps_b = Buf("eps")
        P.op("dve", lambda e: e.memset(self.ones[:], 1.0), writes=[self.ones_b])
        P.op("dve", lambda e: e.memset(self.zeros[:], 0.0), writes=[self.zeros_b])
        P.op("dve", lambda e: e.memset(self.eps[:], EPS), writes=[self.eps_b])
        P.op("pool", lambda e: e.affine_select(out=self.tri[:], in_=self.ones[:], pattern=[[1, 128]], compare_op=ALU.is_ge,
                                               fill=0.0, base=0, channel_multiplier=-1),
             reads=[self.ones_b], writes=[self.tri_b])
        P.op("pool", lambda e: e.affine_select(out=self.negm[:], in_=self.zeros[:], pattern=[[1, 128]], compare_op=ALU.is_ge,
                                               fill=NEGV, base=0, channel_multiplier=-1),
             reads=[self.zeros_b], writes=[self.negm_b])
        P.op("pool", lambda e: e.affine_select(out=self.ident[:], in_=self.ones[:], pattern=[[1, 128]], compare_op=ALU.is_equal,
                                               fill=0.0, base=0, channel_multiplier=-1),
             reads=[self.ones_b], writes=[self.ident_b])
        P.op("dve", lambda e: e.tensor_copy(out=self.identb[:], in_=self.ident[:]), reads=[self.ident_b], writes=[self.identb_b])

    def next_ps(self):
        i = (0, 1, 2, 4, 5, 6)[self.ps_rr % 6]
        self.ps_rr += 1
        return self.ps[i], self.psb[i]

    def next_pt(self):
        i = self.pt_rr % 8
        self.pt_rr += 1
        return self.pt[:, i * 128:(i + 1) * 128], self.ptb[i]


def conv_silu(P, raw, raw_b, cw, cw_b, j0, acc, acc_b, width, n_part=128, bias=None):
    P.op("dve", lambda e: e.tensor_scalar(out=acc[:n_part, :width], in0=raw[:n_part, 0:width], scalar1=cw[:n_part, j0:j0 + 1], scalar2=None, op0=ALU.mult),
         reads=[raw_b, cw_b], writes=[acc_b])
    for j in range(1, 4):
        P.op("dve", lambda e, j=j: e.scalar_tensor_tensor(out=acc[:n_part, :width], in0=raw[:n_part, j:j + width], scalar=cw[:n_part, j0 + j:j0 + j + 1],
                                                       in1=acc[:n_part, :width], op0=ALU.mult, op1=ALU.add),
             reads=[raw_b, cw_b, acc_b], writes=[acc_b])


def local_cumsum_cols(P, C, src, src_b, dst, dst_b, ncol):
    ps, ps_b = C.next_ps()
    P.op("pe", lambda e: e.matmul(ps[:, :ncol], C.tri[:, :], src, start=True, stop=True), reads=[src_b, C.tri_b], writes=[ps_b])
    P.op("dve", lambda e: e.tensor_copy(out=dst, in_=ps[:, :ncol]), reads=[ps_b], writes=[dst_b])


def decay_exp_tile(P, C, lcol_ap, lcol_b, bias_ap, bias_b, ET, ET_b, lfb, lfb_b, sdec=None, sdec_b=None):
    P.op("dve", lambda e: e.tensor_scalar(out=lfb[:, :], in0=C.ones[:, :], scalar1=lcol_ap, scalar2=None, op0=ALU.mult),
         reads=[C.ones_b, lcol_b], writes=[lfb_b])
    pD, pD_b = C.next_ps()
    P.op("pe", lambda e: e.matmul(pD[:, :128], lfb[:, :], C.tri[:, :], start=True, stop=False), reads=[lfb_b, C.tri_b], writes=[pD_b])
    P.op("pe", lambda e: e.matmul(pD[:, :128], C.ident[:, :], C.negm[:, :], start=False, stop=True), reads=[C.ident_b, C.negm_b], writes=[pD_b])
    P.op("act", lambda e: e.activation(out=ET[:, :], in_=pD[:, :128], func=AF.Exp, bias=bias_ap), reads=[pD_b, bias_b], writes=[ET_b])
    if sdec is not None:
        P.op("act", lambda e: e.activation(out=sdec, in_=pD[:, 127:128], func=AF.Exp), reads=[pD_b], writes=[sdec_b])


def mlstm_part(P, C, S_len, pre=""):
    nch = S_len // 128
    nseg = max(1, S_len // SEG)
    seg = min(SEG, S_len)
    cps = seg // 128
    qk = P.dram_in(pre + "ml_qk", [2, 128, S_len], BF16)
    v = P.dram_in(pre + "ml_v", [S_len, 256], BF16)
    gif = P.dram_in(pre + "ml_if", [128, 2, nch])
    cw_d = P.dram_in(pre + "ml_cw", [128, 8])
    bias_d = P.dram_in(pre + "ml_bias", [128, 2])
    gn_d = P.dram_in(pre + "ml_gn", [128, 256])
    out = P.dram_out(pre + "ml_h", [S_len, 256], BF16)

    def T(name, shape, dt=F32):
        return P.sbuf(pre + name, shape, dt), Buf(pre + name)

    cw, cw_b = T("cw", [128, 8])
    bias, bias_b = T("bias", [128, 2])
    gn, gn_b = T("gn", [128, 256])
    gi, gi_b = T("gif", [128, 2, nch])
    P.dma(cw[:], cw_d, writes=[cw_b])
    P.dma(bias[:], bias_d, writes=[bias_b])
    P.dma(gn[:], gn_d, writes=[gn_b])
    P.dma(gi[:], gif, writes=[gi_b])
    lf, lf_b = T("lf", [128, nch])
    Bc, Bc_b = T("Bc", [128, nch])
    bc, bc_b = T("biascol", [128, nch])
    dc, dc_b = T("decaycol", [128, nch])
    nfb, nfb_b = T("nfb", [128, 1])
    P.op("dve", lambda e: e.tensor_scalar(out=nfb[:, :], in0=bias[:, 1:2], scalar1=-1.0, scalar2=None, op0=ALU.mult), reads=[bias_b], writes=[nfb_b])
    P.op("act", lambda e: e.activation(out=lf[:, :], in_=gi[:, 1, :], func=AF.Exp, bias=nfb[:, 0:1], scale=-1.0), reads=[gi_b, nfb_b], writes=[lf_b])
    P.op("act", lambda e: e.activation(out=lf[:, :], in_=lf[:, :], func=AF.Ln, bias=C.ones[:, 0:1], scale=1.0), reads=[lf_b, C.ones_b], writes=[lf_b])
    P.op("dve", lambda e: e.tensor_scalar(out=lf[:, :], in0=lf[:, :], scalar1=-1.0, scalar2=None, op0=ALU.mult), reads=[lf_b], writes=[lf_b])
    local_cumsum_cols(P, C, lf[:, :], lf_b, Bc[:, :], Bc_b, nch)
    P.op("dve", lambda e: e.scalar_tensor_tensor(out=bc[:, :], in0=gi[:, 0, :], scalar=bias[:, 0:1], in1=Bc[:, :], op0=ALU.add, op1=ALU.subtract),
         reads=[gi_b, bias_b, Bc_b], writes=[bc_b])
    P.op("act", lambda e: e.activation(out=dc[:, :], in_=Bc[:, :], func=AF.Exp), reads=[Bc_b], writes=[dc_b])

    raw = [T(f"raw{i}", [128, 2, seg + 3], BF16) for i in range(1)]
    acc, acc_b = T("acc", [128, seg])
    qT, qT_b = T("qT", [128, seg], BF16)
    kT, kT_b = T("kT", [128, seg], BF16)
    vx = [T(f"vx{i}", [128, cps, 257], BF16) for i in range(2)]
    for i in range(2):
        P.op("pool", lambda e, i=i: e.memset(vx[i][0][:, :, 256:257], 1.0), writes=[vx[i][1]])
    CT, CT_b = T("CT", [128, 257])
    CTb, CTb_b = T("CTb", [128, 257], BF16)
    P.op("dve", lambda e: e.memset(CT[:], 0.0), writes=[CT_b])
    P.op("pool", lambda e: e.memset(CTb[:], 0.0), writes=[CTb_b])
    ET, ET_b = T("ET", [128, 128])
    lfb, lfb_b = T("lfb", [128, 128])
    PTm, PTm_b = T("PTm", [128, 128], BF16)
    kw, kw_b = T("kw", [128, 128], BF16)
    sdec, sdec_b = T("sdec", [128, 1])
    tmpi, tmpi_b = T("tmpi", [128, 257])
    nd, nd_b = T("nd", [128, 257])
    sc, sc_b = T("sc", [128, 4])
    junk, junk_b = T("junk", [128, 256])
    ho = [T(f"ho{i}", [128, 256], BF16) for i in range(2)]
    v_v = v.rearrange("(c p) d -> p c d", p=128)
    out_v = out.rearrange("(c p) d -> p c d", p=128)
    for sg in range(nseg):
        rw, rw_b = raw[0]
        vxt, vx_b = vx[sg % 2]
        t0 = sg * seg
        if sg == 0:
            P.op("pool", lambda e, rw=rw: e.memset(rw[:, :, 0:3], 0.0), writes=[rw_b])
            for j in range(2):
                P.dma(rw[:, j, 3:seg + 3], qk[j, :, 0:seg], writes=[rw_b])
        else:
            for j in range(2):
                P.dma(rw[:, j, :], qk[j, :, t0 - 3:t0 + seg], writes=[rw_b])
        P.dma(vxt[:, :, 0:256], v_v[:, sg * cps:(sg + 1) * cps, :], writes=[vx_b])
        conv_silu(P, rw[:, 0, :], rw_b, cw, cw_b, 0, acc, acc_b, seg)
        P.op("act", lambda e: e.activation(out=acc[:, :], in_=acc[:, :], func=AF.Silu), reads=[acc_b], writes=[acc_b])
        P.op("dve", lambda e: e.tensor_scalar(out=qT[:, :], in0=acc[:, :], scalar1=float(128 ** -0.5), scalar2=None, op0=ALU.mult),
             reads=[acc_b], writes=[qT_b])
        conv_silu(P, rw[:, 1, :], rw_b, cw, cw_b, 4, acc, acc_b, seg)
        P.op("act", lambda e: e.activation(out=kT[:, :], in_=acc[:, :], func=AF.Silu), reads=[acc_b], writes=[kT_b])
        for cc in range(cps):
            c = sg * cps + cc
            cs = slice(cc * 128, (cc + 1) * 128)
            decay_exp_tile(P, C, lf[:, c:c + 1], lf_b, bc[:, c:c + 1], bc_b, ET, ET_b, lfb, lfb_b, sdec[:, 0:1], sdec_b)
            pS, pS_b = C.next_ps()
            P.op("pe", lambda e, cs=cs, pS=pS: e.matmul(pS[:, :128], kT[:, cs], qT[:, cs], start=True, stop=True), reads=[kT_b, qT_b], writes=[pS_b])
            P.op("dve", lambda e, pS=pS: e.tensor_tensor(out=PTm[:, :], in0=pS[:, :128], in1=ET[:, :], op=ALU.mult), reads=[pS_b, ET_b], writes=[PTm_b])
            pI, pI_b = C.next_ps()
            P.op("pe", lambda e, cc=cc, pI=pI, vxt=vxt: e.matmul(pI[:, :257], PTm[:, :], vxt[:, cc, :], start=True, stop=True),
                 reads=[PTm_b, vx_b], writes=[pI_b])
            pN, pN_b = C.next_ps()
            P.op("pe", lambda e, cs=cs, pN=pN: e.matmul(pN[:, :257], qT[:, cs], CTb[:, :], start=True, stop=True), reads=[qT_b, CTb_b], writes=[pN_b])
            P.op("act", lambda e, c=c, pN=pN: e.activation(out=tmpi[:, :], in_=pN[:, :257], func=AF.Copy, scale=dc[:, c:c + 1]),
                 reads=[pN_b, dc_b], writes=[tmpi_b])
            P.op("dve", lambda e, pI=pI: e.tensor_tensor(out=nd[:, :], in0=tmpi[:, :], in1=pI[:, :257], op=ALU.add), reads=[tmpi_b, pI_b], writes=[nd_b])
            P.op("act", lambda e: e.activation(out=sc[:, 0:1], in_=nd[:, 256:257], func=AF.Abs), reads=[nd_b], writes=[sc_b])
            P.op("dve", lambda e: e.tensor_scalar(out=sc[:, 0:1], in0=sc[:, 0:1], scalar1=1.0, scalar2=None, op0=ALU.max), reads=[sc_b], writes=[sc_b])
            P.op("dve", lambda e: e.reciprocal(out=sc[:, 0:1], in_=sc[:, 0:1]), reads=[sc_b], writes=[sc_b])
            P.op("act", lambda e: e.activation(out=junk[:, :], in_=nd[:, 0:256], func=AF.Square, accum_out=sc[:, 1:2]), reads=[nd_b, sc_b], writes=[junk_b, sc_b])
            P.op("dve", lambda e: e.tensor_tensor(out=sc[:, 2:3], in0=sc[:, 0:1], in1=sc[:, 0:1], op=ALU.mult), reads=[sc_b], writes=[sc_b])
            P.op("dve", lambda e: e.tensor_tensor(out=sc[:, 2:3], in0=sc[:, 2:3], in1=sc[:, 1:2], op=ALU.mult), reads=[sc_b], writes=[sc_b])
            P.op("act", lambda e: e.activation(out=sc[:, 2:3], in_=sc[:, 2:3], func=AF.Sqrt, bias=C.eps[:, 0:1], scale=1.0 / 256), reads=[sc_b, C.eps_b], writes=[sc_b])
            P.op("dve", lambda e: e.reciprocal(out=sc[:, 2:3], in_=sc[:, 2:3]), reads=[sc_b], writes=[sc_b])
            P.op("dve", lambda e: e.tensor_tensor(out=sc[:, 3:4], in0=sc[:, 2:3], in1=sc[:, 0:1], op=ALU.mult), reads=[sc_b], writes=[sc_b])
            hot, ho_b = ho[c % 2]
            P.op("dve", lambda e, hot=hot: e.scalar_tensor_tensor(out=hot[:, :], in0=nd[:, 0:256], scalar=sc[:, 3:4], in1=gn[:, :], op0=ALU.mult, op1=ALU.mult),
                 reads=[nd_b, sc_b, gn_b], writes=[ho_b])
            P.dma(out_v[:, c, :], hot[:, :], reads=[ho_b])
            pT, pT_b = C.next_pt()
            P.op("pe", lambda e, cs=cs, pT=pT: e.transpose(pT, kT[:, cs], C.identb[:, :]), reads=[kT_b, C.identb_b], writes=[pT_b])
            P.op("dve", lambda e, pT=pT: e.tensor_scalar(out=kw[:, :], in0=pT, scalar1=ET[:, 127:128], scalar2=None, op0=ALU.mult),
                 reads=[pT_b, ET_b], writes=[kw_b])
            pU, pU_b = C.next_ps()
            P.op("pe", lambda e, cc=cc, pU=pU, vxt=vxt: e.matmul(pU[:, :257], kw[:, :], vxt[:, cc, :], start=True, stop=True), reads=[kw_b, vx_b], writes=[pU_b])
            P.op("dve", lambda e, pU=pU: e.scalar_tensor_tensor(out=CT[:, :], in0=CT[:, :], scalar=sdec[:, 0:1], in1=pU[:, :257], op0=ALU.mult, op1=ALU.add),
                 reads=[CT_b, sdec_b, pU_b], writes=[CT_b])
            P.op("act", lambda e: e.activation(out=CTb[:, :], in_=CT[:, :], func=AF.Copy), reads=[CT_b], writes=[CTb_b])


def ssd_part(P, C, S_len, pre=""):
    nch = S_len // 128
    nseg = max(1, S_len // SEG)
    seg = min(SEG, S_len)
    cps = seg // 128
    raw_d = P.dram_in(pre + "ss_raw", [4, 128, S_len], BF16)
    dt_d = P.dram_in(pre + "ss_dt", [128, 4, nch])
    cw_d = P.dram_in(pre + "ss_cw", [128, 4, 4])
    cb_d = P.dram_in(pre + "ss_cb", [128, 4])
    par_d = P.dram_in(pre + "ss_par", [128, 3, 4])
    out = P.dram_out(pre + "ss_y", [S_len, 256], BF16)

    def T(name, shape, dt=F32):
        return P.sbuf(pre + name, shape, dt), Buf(pre + name)

    cw, cw_b = T("scw", [128, 16])
    cb, cb_b = T("scb", [128, 4])
    par, par_b = T("spar", [128, 3, 4])
    dtr, dtr_b = T("sdt", [128, 4, nch])
    P.dma(cw[:], cw_d.rearrange("p a b -> p (a b)"), writes=[cw_b])
    P.dma(cb[:], cb_d, writes=[cb_b])
    P.dma(par[:], par_d, writes=[par_b])
    P.dma(dtr[:], dt_d, writes=[dtr_b])
    dA, dA_b = T("sdA", [128, 4, nch])
    sg_, sg_b = T("ssegc", [128, 4, nch])
    dc, dc_b = T("sdecay", [128, 4, nch])
    An, An_b = T("sAn", [128, 4])
    for h in range(4):
        P.op("act", lambda e, h=h: e.activation(out=dtr[:, h, :], in_=dtr[:, h, :], func=AF.Exp, bias=par[:, 0, h:h + 1]), reads=[dtr_b, par_b], writes=[dtr_b])
    P.op("act", lambda e: e.activation(out=dtr[:, :, :], in_=dtr[:, :, :], func=AF.Ln, bias=C.ones[:, 0:1], scale=1.0), reads=[dtr_b, C.ones_b], writes=[dtr_b])
    P.op("act", lambda e: e.activation(out=An[:, :], in_=par[:, 1, :], func=AF.Exp), reads=[par_b], writes=[An_b])
    P.op("dve", lambda e: e.tensor_scalar(out=An[:, :], in0=An[:, :], scalar1=-1.0, scalar2=None, op0=ALU.mult), reads=[An_b], writes=[An_b])
    for h in range(4):
        P.op("dve", lambda e, h=h: e.tensor_scalar(out=dA[:, h, :], in0=dtr[:, h, :], scalar1=An[:, h:h + 1], scalar2=None, op0=ALU.mult),
             reads=[dtr_b, An_b], writes=[dA_b])
    local_cumsum_cols(P, C, dA[:, :, :].rearrange("p a b -> p (a b)"), dA_b, sg_[:, :, :].rearrange("p a b -> p (a b)"), sg_b, 4 * nch)
    P.op("act", lambda e: e.activation(out=dc[:, :, :], in_=sg_[:, :, :], func=AF.Exp), reads=[sg_b], writes=[dc_b])
    P.op("dve", lambda e: e.tensor_scalar(out=sg_[:, :, :], in0=sg_[:, :, :], scalar1=-1.0, scalar2=None, op0=ALU.mult), reads=[sg_b], writes=[sg_b])

    raw = [T(f"sraw{i}", [128, 4, seg + 3], BF16) for i in range(1)]
    acc, acc_b = T("sacc", [128, seg])
    fT = [T(f"sfT{i}", [128, seg], BF16) for i in range(4)]
    ST, ST_b = T("sST", [128, 256])
    STb, STb_b = T("sSTb", [128, 256], BF16)
    P.op("dve", lambda e: e.memset(ST[:], 0.0), writes=[ST_b])
    P.op("pool", lambda e: e.memset(STb[:], 0.0), writes=[STb_b])
    ETs = [T(f"sET{i}", [128, 128]) for i in range(4)]
    sdecs = [T(f"ssdec{i}", [128, 1]) for i in range(4)]
    lfb, lfb_b = T("slfb", [128, 128])
    PTm = [T(f"sPT{i}", [128, 128], BF16) for i in range(2)]
    xtm, xtm_b = T("sxtm", [128, 256], BF16)
    xdt, xdt_b = T("sxdt", [128, 256], BF16)
    xw, xw_b = T("sxw", [128, 256], BF16)
    btm, btm_b = T("sbtm", [128, 128], BF16)
    wcol, wcol_b = T("swcol", [128, 4])
    tmpi, tmpi_b = T("stmpi", [128, 256])
    y1, y1_b = T("sy1", [128, 256])
    yo = [T(f"syo{i}", [128, 256], BF16) for i in range(2)]
    out_v = out.rearrange("(c p) d -> p c d", p=128)
    for sgi in range(nseg):
        rw, rw_b = raw[0]
        t0 = sgi * seg
        if sgi == 0:
            P.op("pool", lambda e, rw=rw: e.memset(rw[:, :, 0:3], 0.0), writes=[rw_b])
            for j in range(4):
                P.dma(rw[:, j, 3:seg + 3], raw_d[j, :, 0:seg], writes=[rw_b])
        else:
            for j in range(4):
                P.dma(rw[:, j, :], raw_d[j, :, t0 - 3:t0 + seg], writes=[rw_b])
        for j in range(4):
            conv_silu(P, rw[:, j, :], rw_b, cw, cw_b, 4 * j, acc, acc_b, seg)
            P.op("act", lambda e, j=j: e.activation(out=fT[j][0][:, :], in_=acc[:, :], func=AF.Silu, bias=cb[:, j:j + 1]), reads=[acc_b, cb_b], writes=[fT[j][1]])
        for cc in range(cps):
            c = sgi * cps + cc
            cs = slice(cc * 128, (cc + 1) * 128)
            for j in range(2):
                pT, pT_b = C.next_pt()
                P.op("pe", lambda e, j=j, pT=pT, c=c, cs=cs: e.transpose(pT, fT[j][0][:, cs], C.identb[:, :]), reads=[fT[j][1], C.identb_b], writes=[pT_b])
                P.op("act", lambda e, j=j, pT=pT, c=c, cs=cs: e.activation(out=xtm[:, j * 128:(j + 1) * 128], in_=pT, func=AF.Copy), reads=[pT_b], writes=[xtm_b])
            pT, pT_b = C.next_pt()
            P.op("pe", lambda e, pT=pT, c=c, cs=cs: e.transpose(pT, fT[2][0][:, cs], C.identb[:, :]), reads=[fT[2][1], C.identb_b], writes=[pT_b])
            P.op("act", lambda e, pT=pT, c=c, cs=cs: e.activation(out=btm[:, :], in_=pT, func=AF.Copy), reads=[pT_b], writes=[btm_b])
            for h in range(4):
                hs = slice(h * 64, (h + 1) * 64)
                P.op("dve", lambda e, h=h, hs=hs, c=c, cs=cs: e.tensor_scalar(out=xdt[:, hs], in0=xtm[:, hs], scalar1=dtr[:, h, c:c + 1], scalar2=None, op0=ALU.mult),
                     reads=[xtm_b, dtr_b], writes=[xdt_b])
            pSc, pSc_b = C.next_ps()
            P.op("pe", lambda e, pSc=pSc, c=c, cs=cs: e.matmul(pSc[:, :128], fT[2][0][:, cs], fT[3][0][:, cs], start=True, stop=True), reads=[fT[2][1], fT[3][1]], writes=[pSc_b])
            pY, pY_b = C.next_ps()
            for h in range(4):
                hs = slice(h * 64, (h + 1) * 64)
                ET, ET_b = ETs[h]
                sd, sd_b = sdecs[h]
                decay_exp_tile(P, C, dA[:, h, c:c + 1], dA_b, sg_[:, h, c:c + 1], sg_b, ET, ET_b, lfb, lfb_b, sd[:, 0:1], sd_b)
                pm, pm_b = PTm[h % 2]
                P.op("dve", lambda e, pm=pm, ET=ET, pSc=pSc, c=c, cs=cs: e.tensor_tensor(out=pm[:, :], in0=pSc[:, :128], in1=ET[:, :], op=ALU.mult), reads=[pSc_b, ET_b], writes=[pm_b])
                P.op("pe", lambda e, pm=pm, hs=hs, pY=pY, c=c, cs=cs: e.matmul(pY[:, hs], pm[:, :], xdt[:, hs], start=True, stop=True), reads=[pm_b, xdt_b], writes=[pY_b])
                P.op("dve", lambda e, h=h, ET=ET, c=c, cs=cs: e.tensor_tensor(out=wcol[:, h:h + 1], in0=ET[:, 127:128], in1=dtr[:, h, c:c + 1], op=ALU.mult),
                     reads=[ET_b, dtr_b], writes=[wcol_b])
            pYi, pYi_b = C.next_ps()
            P.op("pe", lambda e, pYi=pYi, c=c, cs=cs: e.matmul(pYi[:, :256], fT[3][0][:, cs], STb[:, :], start=True, stop=True), reads=[fT[3][1], STb_b], writes=[pYi_b])
            for h in range(4):
                hs = slice(h * 64, (h + 1) * 64)
                P.op("act", lambda e, h=h, hs=hs, pYi=pYi, c=c, cs=cs: e.activation(out=tmpi[:, hs], in_=pYi[:, hs], func=AF.Copy, scale=dc[:, h, c:c + 1]),
                     reads=[pYi_b, dc_b], writes=[tmpi_b])
            P.op("dve", lambda e, pY=pY, c=c, cs=cs: e.tensor_tensor(out=y1[:, :], in0=tmpi[:, :], in1=pY[:, :256], op=ALU.add), reads=[tmpi_b, pY_b], writes=[y1_b])
            yot, yo_b = yo[c % 2]
            for h in range(4):
                hs = slice(h * 64, (h + 1) * 64)
                P.op("dve", lambda e, h=h, hs=hs, yot=yot, c=c, cs=cs: e.scalar_tensor_tensor(out=yot[:, hs], in0=xtm[:, hs], scalar=par[:, 2, h:h + 1], in1=y1[:, hs],
                                                                             op0=ALU.mult, op1=ALU.add),
                     reads=[xtm_b, par_b, y1_b], writes=[yo_b])
            P.dma(out_v[:, c, :], yot[:, :], reads=[yo_b])
            for h in range(4):
                hs = slice(h * 64, (h + 1) * 64)
                P.op("dve", lambda e, h=h, hs=hs, c=c, cs=cs: e.tensor_scalar(out=xw[:, hs], in0=xtm[:, hs], scalar1=wcol[:, h:h + 1], scalar2=None, op0=ALU.mult),
                     reads=[xtm_b, wcol_b], writes=[xw_b])
            pU, pU_b = C.next_ps()
            P.op("pe", lambda e, pU=pU, c=c, cs=cs: e.matmul(pU[:, :256], btm[:, :], xw[:, :], start=True, stop=True), reads=[btm_b, xw_b], writes=[pU_b])
            for h in range(4):
                hs = slice(h * 64, (h + 1) * 64)
                P.op("dve", lambda e, h=h, hs=hs, pU=pU, c=c, cs=cs: e.scalar_tensor_tensor(out=ST[:, hs], in0=ST[:, hs], scalar=sdecs[h][0][:, 0:1], in1=pU[:, hs],
                                                                           op0=ALU.mult, op1=ALU.add),
                     reads=[ST_b, sdecs[h][1], pU_b], writes=[ST_b])
            P.op("act", lambda e, c=c, cs=cs: e.activation(out=STb[:, :], in_=ST[:, :], func=AF.Copy), reads=[ST_b], writes=[STb_b])


def dsa_part(P, C, S_len, nqb, blk_sadm, pre="", debug=False):
    NIT = 27
    LO0, W0 = -512.0, 1024.0
    dq_d = P.dram_in(pre + "d_q", [nqb, 128, 8, 128], BF16)
    diq_d = P.dram_in(pre + "d_iq", [nqb, 128, 8, 128], BF16)
    dw_d = P.dram_in(pre + "d_w", [128, nqb, 16])
    dlim_d = P.dram_in(pre + "d_lim", [128, nqb])
    kT_d = P.dram_in(pre + "d_kT", [128, 8, S_len], BF16)
    v_d = P.dram_in(pre + "d_v", [S_len, 1024], BF16)
    ki_d = P.dram_in(pre + "d_ki2", [128, S_len], BF16)
    out = P.dram_out(pre + "d_y", [nqb, 128, 1024], BF16)
    if debug:
        dbg_score = P.dram_out(pre + "dbg_score", [nqb, 128, max(blk_sadm)])
        dbg_lo = P.dram_out(pre + "dbg_lo", [nqb, 2, 128, 1])

    def T(name, shape, dt=F32):
        return P.sbuf(pre + name, shape, dt), Buf(pre + name)

    smax = max(blk_sadm)
    score, score_b = T("dscore", [128, smax])
    JW = min(2048, smax)
    junk, junk_b = T("djunk", [128, JW], BF16)
    kidx, kidx_b = T("dkidx", [128, 512])
    P.op("pool", lambda e: e.iota(kidx[:, :], pattern=[[1, 512]], base=0, channel_multiplier=0, allow_small_or_imprecise_dtypes=True), writes=[kidx_b])
    dw, dw_b = T("ddw", [128, nqb, 16])
    aw, aw_b = T("daw", [128, nqb, 16])
    sw, sw_b = T("dsw", [128, nqb, 16])
    lim, lim_b = T("dlim", [128, nqb])
    P.dma(dw[:], dw_d, writes=[dw_b])
    P.dma(lim[:], dlim_d, writes=[lim_b])
    P.op("act", lambda e: e.activation(out=aw[:, :, :], in_=dw[:, :, :], func=AF.Abs, scale=0.25), reads=[dw_b], writes=[aw_b])
    P.op("act", lambda e: e.activation(out=sw[:, :, :], in_=dw[:, :, :], func=AF.Sign), reads=[dw_b], writes=[sw_b])
    qb = [T(f"dqb{i}", [128, 8, 128], BF16) for i in range(2)]
    iqb = [T(f"diqb{i}", [128, 8, 128], BF16) for i in range(2)]
    kib = [T(f"dkib{i}", [128, 1024], BF16) for i in range(2)]
    rt = [T(f"drt{i}", [128, 512]) for i in range(2)]
    KG = 256
    kbuf = [T(f"dkbuf{i}", [128, 8, KG], BF16) for i in range(2)]
    vbuf = [T(f"dvbuf{i}", [128, KG // 128, 8, 129], BF16) for i in range(2)]
    for i in range(2):
        P.op("pool", (lambda vb: (lambda e: e.memset(vb[:, :, :, 128:129], 1.0)))(vbuf[i][0]), writes=[vbuf[i][1]])
    ngm = [T(f"dngm{i}", [128, KG], BF16) for i in range(2)]
    ngT = [T(f"dngT{i}", [128, KG // 128, 128], BF16) for i in range(2)]
    pex = [T(f"dpex{i}", [128, 128], BF16) for i in range(4)]
    lo, lo_b = T("dlo", [128, 1])
    th, th_b = T("dth", [128, 1])
    cnt, cnt_b = T("dcnt", [128, 1])
    ge, ge_b = T("dge", [128, 1])
    nth, nth_b = T("dnth", [128, 1])
    sgs, sgs_b = T("dsgs", [128, 8])
    sg1, sg1_b = T("dsg1", [128, 1])
    junk2, junk2_b = T("djunk2", [128, JW], BF16)
    DVE_FRAC = 0.5
    pxw = [T(f"dpxw{i}", [128, 512], BF16) for i in range(3)]
    limk, limk_b = T("dlimk", [128, 1])
    rden, rden_b = T("drden", [128, 8])
    yo = [T(f"dyo{i}", [128, 1024], BF16) for i in range(2)]
    negc, negc_b = T("dnegc", [128, 1])
    P.op("dve", lambda e: e.memset(negc[:], -12.0), writes=[negc_b])
    acc_banks = [(C.ps[4], C.psb[4]), (C.ps[5], C.psb[5]), (C.ps[6], C.psb[6])]
    v_v = v_d.rearrange("s (h d) -> s h d", h=8)
    rr = {"ps": 0, "q": 0, "pex": 0, "kv": 0, "rt": 0, "ng": 0}

    qslots = [Buf(f"dqslot{i}") for i in range(4)]

    def rot_ps():
        i = rr["ps"] % 2
        rr["ps"] += 1
        return C.ps[i], C.psb[i]

    def rot_q():
        i = rr["q"] % 4
        rr["q"] += 1
        return C.ps[3][:, i * 128:(i + 1) * 128], qslots[i]

    def index_tile(b, kt0, kib_t, kib_bf, off, iq_t, iq_bf):
        for hh in range(16):
            j, half = hh // 2, hh % 2
            prt = slice(half * 64, half * 64 + 64)
            pr, pr_b = rot_ps()
            P.op("pe", lambda e, pr=pr, prt=prt, j=j: e.matmul(pr[:, :], iq_t[prt, j, :], kib_t[prt, off:off + 512], start=True, stop=True),
                 reads=[iq_bf, kib_bf], writes=[pr_b])
            r, r_b = rt[rr["rt"] % 2]
            rr["rt"] += 1
            P.op("act", lambda e, r=r, pr=pr, hh=hh: e.activation(out=r[:, :], in_=pr[:, :], func=AF.Relu, scale=aw[:, b, hh:hh + 1]), reads=[pr_b, aw_b], writes=[r_b])
            P.op("dve", lambda e, r=r, hh=hh: e.scalar_tensor_tensor(out=score[:, kt0:kt0 + 512], in0=r[:, :], scalar=sw[:, b, hh:hh + 1], in1=score[:, kt0:kt0 + 512],
                                                                op0=ALU.mult, op1=ALU.add),
                 reads=[r_b, sw_b, score_b], writes=[score_b])

    def pen_tile(b, kt0):
        P.op("dve", lambda e: e.tensor_scalar(out=limk[:, :], in0=lim[:, b:b + 1], scalar1=float(-kt0), scalar2=None, op0=ALU.add), reads=[lim_b], writes=[limk_b])
        P.op("dve", lambda e: e.tensor_scalar(out=score[:, kt0:kt0 + 512], in0=kidx[:, :], scalar1=limk[:, 0:1], scalar2=-1.0e4, op0=ALU.is_ge, op1=ALU.mult),
             reads=[kidx_b, limk_b], writes=[score_b])

    def count_piece(a, bnd, first):
        if first:
            P.op("dve", lambda e: e.tensor_scalar(out=junk[:, :bnd - a], in0=score[:, a:bnd], scalar1=th[:, 0:1], scalar2=None, op0=ALU.is_ge, op1=ALU.add,
                                                  accum_out=cnt[:, 0:1]),
                 reads=[score_b, th_b], writes=[junk_b, cnt_b])
        else:
            P.op("dve", lambda e: e.tensor_scalar(out=junk[:, :bnd - a], in0=score[:, a:bnd], scalar1=th[:, 0:1], scalar2=cnt[:, 0:1], op0=ALU.is_ge, op1=ALU.add,
                                                  accum_out=cnt[:, 0:1]),
                 reads=[score_b, th_b, cnt_b], writes=[junk_b, cnt_b])

    def count_piece_act(a, bnd, col):
        P.op("act", lambda e: e.activation(out=junk2[:, :bnd - a], in_=score[:, a:bnd], func=AF.Sign, bias=nth[:, 0:1], scale=1.0, accum_out=sgs[:, col:col + 1]),
             reads=[score_b, nth_b], writes=[junk2_b, sgs_b])

    def bisect_step(sadm, half):
        P.op("dve", lambda e: e.tensor_scalar(out=th[:, :], in0=lo[:, :], scalar1=float(half), scalar2=None, op0=ALU.add), reads=[lo_b], writes=[th_b])
        P.op("dve", lambda e: e.tensor_scalar(out=nth[:, :], in0=lo[:, :], scalar1=-1.0, scalar2=float(-half), op0=ALU.mult, op1=ALU.add), reads=[lo_b], writes=[nth_b])
        nd = min(sadm, max(512, int(sadm * DVE_FRAC) // 512 * 512))
        pieces = [(a, min(nd, a + JW)) for a in range(0, nd, JW)]
        for i, (a, bnd) in enumerate(pieces):
            count_piece(a, bnd, i == 0)
        apieces = [(a, min(sadm, a + JW)) for a in range(nd, sadm, JW)]
        for i, (a, bnd) in enumerate(apieces):
            count_piece_act(a, bnd, i)
        if apieces:
            na = len(apieces)
            P.op("dve", lambda e: e.tensor_reduce(out=sg1[:, 0:1], in_=sgs[:, 0:na], axis=AX.X, op=ALU.add), reads=[sgs_b], writes=[sg1_b])
            P.op("dve", lambda e: e.scalar_tensor_tensor(out=cnt[:, :], in0=sg1[:, :], scalar=0.5, in1=cnt[:, :], op0=ALU.mult, op1=ALU.add),
                 reads=[sg1_b, cnt_b], writes=[cnt_b])
            P.op("dve", lambda e: e.tensor_scalar(out=ge[:, :], in0=cnt[:, :], scalar1=256.0 - 0.5 * (sadm - nd), scalar2=float(half), op0=ALU.is_ge, op1=ALU.mult),
                 reads=[cnt_b], writes=[ge_b])
        else:
            P.op("dve", lambda e: e.tensor_scalar(out=ge[:, :], in0=cnt[:, :], scalar1=256.0, scalar2=float(half), op0=ALU.is_ge, op1=ALU.mult), reads=[cnt_b], writes=[ge_b])
        P.op("dve", lambda e: e.tensor_tensor(out=lo[:, :], in0=lo[:, :], in1=ge[:, :], op=ALU.add), reads=[lo_b, ge_b], writes=[lo_b])

    def bisect(sadm):
        P.op("dve", lambda e: e.memset(lo[:], LO0), writes=[lo_b])
        w = W0
        for it in range(NIT):
            bisect_step(sadm, w / 2)
            w = w / 2

    def attn_hgroup_tile(t, hg, kb, kb_b, vb, vb_b, nT, nT_b, q_t, q_bf, start, stop):
        i = rr["q"] % 2
        rr["q"] += 1
        pL, pL_b = C.ps[2 + i], C.psb[2 + i]
        P.op("pe", lambda e: e.matmul(pL[:, :].rearrange("p (h q) -> p h q", h=4), C.identb[:, :], nT[:, t, :].unsqueeze(1).broadcast_to([128, 4, 128]),
                                      start=True, stop=False, skip_group_check=True),
             reads=[C.identb_b, nT_b], writes=[pL_b])
        for h4 in range(4):
            h = hg * 4 + h4
            P.op("pe", lambda e, h=h, h4=h4: e.matmul(pL[:, h4 * 128:(h4 + 1) * 128], kb[:, h, t * 128:(t + 1) * 128], q_t[:, h, :], start=False, stop=(h4 == 3),
                                                  skip_group_check=True),
                 reads=[kb_b, q_bf], writes=[pL_b])
        px, px_b = pxw[rr["pex"] % 3]
        rr["pex"] += 1
        P.op("act", lambda e: e.activation(out=px[:, :], in_=pL[:, :], func=AF.Exp, bias=negc[:, 0:1]), reads=[pL_b, negc_b], writes=[px_b])
        for h4 in range(4):
            h = hg * 4 + h4
            ab, ab_b = acc_banks[h // 3]
            col = (h % 3) * 129
            P.op("pe", lambda e, h=h, h4=h4, ab=ab, col=col: e.matmul(ab[:, col:col + 129], px[:, h4 * 128:(h4 + 1) * 128], vb[:, t, h, :],
                                                                  start=(start and h % 3 == 0), stop=stop, skip_group_check=True),
                 reads=[px_b, vb_b], writes=[ab_b])

    def transpose_ng(t, ng, ng_b, nT, nT_b):
        pT, pT_b = C.next_pt()
        P.op("pe", lambda e: e.transpose(pT, ng[:, t * 128:(t + 1) * 128], C.identb[:, :]), reads=[ng_b, C.identb_b], writes=[pT_b])
        P.op("act", lambda e: e.activation(out=nT[:, t, :], in_=pT, func=AF.Copy), reads=[pT_b], writes=[nT_b])

    def attn_group(b, g0, first, last, q_t, q_bf):
        k = rr["kv"] % 2
        rr["kv"] += 1
        kb, kb_b = kbuf[k]
        vb, vb_b = vbuf[k]
        ng, ng_b = ngm[k]
        nT, nT_b = ngT[k]
        P.dma(kb[:, :, :], kT_d[:, :, g0:g0 + KG], writes=[kb_b])
        for t in range(KG // 128):
            P.dma(vb[:, t, :, 0:128], v_v[g0 + t * 128:g0 + (t + 1) * 128], writes=[vb_b])
        P.op("dve", lambda e: e.tensor_scalar(out=ng[:, :], in0=score[:, g0:g0 + KG], scalar1=lo[:, 0:1], scalar2=NEGV, op0=ALU.is_lt, op1=ALU.mult),
             reads=[score_b, lo_b], writes=[ng_b])
        nt = KG // 128
        for t in range(nt):
            transpose_ng(t, ng, ng_b, nT, nT_b)
        for t in range(nt):
            for hg in range(2):
                attn_hgroup_tile(t, hg, kb, kb_b, vb, vb_b, nT, nT_b, q_t, q_bf, first and t == 0, last and t == nt - 1)

    def finalize_head(h, yot, yo_b):
        ab, ab_b = acc_banks[h // 3]
        col = (h % 3) * 129
        P.op("dve", lambda e: e.reciprocal(out=rden[:, h:h + 1], in_=ab[:, col + 128:col + 129]), reads=[ab_b], writes=[rden_b])
        P.op("act", lambda e: e.activation(out=yot[:, h * 128:(h + 1) * 128], in_=ab[:, col:col + 128], func=AF.Copy, scale=rden[:, h:h + 1]),
             reads=[ab_b, rden_b], writes=[yo_b])

    def do_block(b):
        sadm = blk_sadm[b]
        q_t, q_bf = qb[b % 2]
        iq_t, iq_bf = iqb[b % 2]
        P.dma(q_t[:, :, :], dq_d[b], writes=[q_bf])
        P.dma(iq_t[:, :, :], diq_d[b], writes=[iq_bf])
        for kt0 in range(0, sadm, 512):
            pen_tile(b, kt0)
        for k0 in range(0, sadm, 1024):
            kw_ = min(1024, sadm - k0)
            kib_t, kib_bf = kib[(k0 // 1024) % 2]
            P.dma(kib_t[:, :kw_], ki_d[:, k0:k0 + kw_], writes=[kib_bf])
            for off in range(0, kw_, 512):
                index_tile(b, k0 + off, kib_t, kib_bf, off, iq_t, iq_bf)
        bisect(sadm)
        if debug:
            P.dma(dbg_score[b, :, :sadm], score[:, :sadm], reads=[score_b])
            P.dma(dbg_lo[b, 0], lo[:, :], reads=[lo_b])
            P.dma(dbg_lo[b, 1], cnt[:, :], reads=[cnt_b])
        ng_ = sadm // KG
        for gi in range(ng_):
            attn_group(b, gi * KG, gi == 0, gi == ng_ - 1, q_t, q_bf)
        yot, yo_b = yo[b % 2]
        for h in range(8):
            finalize_head(h, yot, yo_b)
        P.dma(out[b], yot[:, :], reads=[yo_b])

    for b in range(nqb):
        do_block(b)


def build_B(S_len, nqb, blk_sadm):
    P = Prog()
    C = CtxB(P)
    mlstm_part(P, C, S_len)
    ssd_part(P, C, S_len)
    dsa_part(P, C, S_len, nqb, blk_sadm)
    return P


NMEM = 256


def setup_C(P, C, S, ntok, pre="", with_out=True):
    x1T = P.dram_in(pre + "x1T", [D, ntok])
    hmT = P.dram_in(pre + "hmT", [1024, ntok], BF16)
    ybT = P.dram_in(pre + "ybT", [1024, ntok], BF16)
    ysT = P.dram_in(pre + "ysT", [1024, ntok], BF16)
    gmix = P.dram_in(pre + "gmix", [128, KC])
    gssm = P.dram_in(pre + "gssm", [128, 8])
    winC = P.dram_in(pre + "winC", [D, 8192])
    wbr = P.dram_in(pre + "wbr", [3, 1024, D])
    wout = P.dram_in(pre + "wout", [D, D])
    gxa = P.dram_in(pre + "gxa", [128, KC])
    gxm = P.dram_in(pre + "gxm", [128, KC])
    gxq = P.dram_in(pre + "gxq", [128, 1])
    gxk = P.dram_in(pre + "gxk", [128, 1])
    wq = P.dram_in(pre + "wq", [D, 512])
    wkv = P.dram_in(pre + "wkv", [D, 1024])
    wo = P.dram_in(pre + "wo", [512, D])
    memT = P.dram_in(pre + "memT", [D, NMEM])
    g2 = P.dram_in(pre + "g2", [128, KC])
    wup = P.dram_in(pre + "wup", [D, 2 * DFF])
    wdn = P.dram_in(pre + "wdn", [DFF, D])
    xoT = P.dram_out(pre + "xoT", [D, ntok]) if with_out else None
    wup_bf = P.dram_tmp(pre + "wup_bf", [44, 128, KC, 256], BF16)
    wdn_bf = P.dram_tmp(pre + "wdn_bf", [16, 128, FC, 128], BF16)
    winC_bf = P.dram_tmp(pre + "winC_bf", [32, 128, KC, 256], BF16)
    wbr_bf = [P.dram_tmp(pre + f"wbr_bf{b}", [8, 128, 8, 256], BF16) for b in range(3)]
    wout_bf = P.dram_tmp(pre + "wout_bf", [8, 128, KC, 256], BF16)
    wq_bf = P.dram_tmp(pre + "wq_bf", [2, 128, KC, 256], BF16)
    wk_bf = P.dram_tmp(pre + "wk_bf", [2, 128, KC, 256], BF16)
    wo_bf = P.dram_tmp(pre + "wo_bf", [8, 128, 4, 256], BF16)
    S["tmp"] = [P.sbuf(f"tmpf{i}", [128, TT], F32) for i in range(2)]
    S["tmp_b"] = [Buf(f"tmpf{i}") for i in range(2)]
    S["tmp_rr"] = 0
    S["rstd2"] = P.sbuf("rstd2", [128, TT], F32)
    S["rstd2_b"] = Buf("rstd2")
    macc = [P.sbuf(f"macc{i}", [128, TT], F32) for i in range(2)]
    macc_b = [Buf(f"macc{i}") for i in range(2)]
    gsb = [P.sbuf(f"gsb{i}", [128, TT], F32) for i in range(2)]
    gsb_b = [Buf(f"gsb{i}") for i in range(2)]
    ytmp = P.sbuf("ytmp", [128, TT], BF16)
    ytmp_b = Buf("ytmp")
    kx = P.sbuf("kx", [128, 4, NMEM], BF16)
    kx_b = Buf("kx")
    vx = P.sbuf("vx", [128, 2, 512], BF16)
    vx_b = Buf("vx")
    mn = P.sbuf("mn", [128, KC, NMEM], BF16)
    mn_b = Buf("mn")
    onesb = P.sbuf("onesb", [128, 128], BF16)
    onesb_b = Buf("onesb")
    negc = P.sbuf("negc", [128, 1], F32)
    negc_b = Buf("negc")
    P.op("dve", lambda e: e.memset(onesb[:], 1.0), writes=[onesb_b])
    P.op("dve", lambda e: e.memset(negc[:], -12.0), writes=[negc_b])

    wup_b = cast_weight_tiled(P, wup, wup_bf, KC, 256, 44, "wup")
    wdn_b = cast_weight_tiled(P, wdn, wdn_bf, FC, 128, 16, "wdn")
    winC_b = cast_weight_tiled(P, winC, winC_bf, KC, 256, 32, "winC")
    wbr_b = [cast_weight_tiled(P, wbr[b], wbr_bf[b], 8, 256, 8, f"wbr{b}") for b in range(3)]
    wout_b = cast_weight_tiled(P, wout, wout_bf, KC, 256, 8, "wout")
    wq_b = cast_weight_tiled(P, wq, wq_bf, KC, 256, 2, "wq")
    wk_b = cast_weight_tiled(P, wkv[:, 0:512], wk_bf, KC, 256, 2, "wk")
    wo_b = cast_weight_tiled(P, wo, wo_bf, 4, 256, 8, "wo")
    gm_t, gm_b = load_vec(P, pre + "gmix", gmix, [128, KC])
    gs_t, gs_b = load_vec(P, pre + "gssm", gssm, [128, 8])
    gxa_t, gxa_b = load_vec(P, pre + "gxa", gxa, [128, KC])
    gxm_t, gxm_b = load_vec(P, pre + "gxm", gxm, [128, KC])
    gxq_t, gxq_b = load_vec(P, pre + "gxq", gxq, [128, 1])
    gxk_t, gxk_b = load_vec(P, pre + "gxk", gxk, [128, 1])
    g2_t, g2_b = load_vec(P, pre + "g2", g2, [128, KC])
    P.op("dve", lambda e: e.tensor_scalar(out=gxq_t[:, :], in0=gxq_t[:, :], scalar1=float(128 ** -0.5), scalar2=None, op0=ALU.mult),
         reads=[gxq_b], writes=[gxq_b])

    x_sb, x_b = S["x"], S["x_b"]
    h_sb, h_b = S["h"], S["h_b"]
    u_sb, u_b = S["u"], S["u_b"]
    wa, wab_b = S["wa"], S["wab_b"]
    wbufs = [(S["wa"][0], S["wab_b"][0]), (S["wa"][1], S["wab_b"][1]), (S["wb"][0], S["wbb_b"][0]), (S["wb"][1], S["wbb_b"][1])]
    rr = {"w": 0}

    def next_w():
        k = rr["w"] % 4
        rr["w"] += 1
        return wbufs[k]

    def mm_acc(ps, ps_b, lhs_fn, rhs_fn, nk, reads, m=128, n=TT):
        for kc in range(nk):
            P.op("pe", lambda e, kc=kc: e.matmul(ps[:m, :n], lhs_fn(kc), rhs_fn(kc), start=(kc == 0), stop=(kc == nk - 1)),
                 reads=reads(kc), writes=[ps_b])

    P.dma(x_sb[:, :, 0:NMEM], memT.rearrange("(c p) n -> p c n", p=128), writes=[x_b])
    rms_rstd(P, C, x_sb, x_b, S["rstd"], S["rstd_b"], S["sq"], S["sq_b"], KC, NMEM, D)
    for c in range(KC):
        P.op("dve", lambda e, c=c: e.scalar_tensor_tensor(out=mn[:, c, :], in0=x_sb[:, c, 0:NMEM], scalar=gxm_t[:, c:c + 1],
                                                        in1=S["rstd"][:, 0:NMEM], op0=ALU.mult, op1=ALU.mult),
             reads=[x_b, gxm_b, S["rstd_b"]], writes=[mn_b])
    P.dma(u_sb[:, 0:16, :], wkv.rearrange("(kc p) n -> p kc n", p=128)[:, :, 512:1024], writes=[u_b[i] for i in range(16)], eng="pool")
    for blk in range(2):
        w_t, w_bf = next_w()
        P.dma(w_t[:, :, :], wk_bf[blk], reads=[wk_b], writes=[w_bf])
        for s in range(2):
            hh = blk * 2 + s
            pk, pk_b = C.next_ps()
            mm_acc(pk, pk_b, lambda kc, w_t=w_t, s=s: w_t[:, kc, s * 128:(s + 1) * 128], lambda kc: mn[:, kc, :], KC,
                   lambda kc, w_bf=w_bf: [w_bf, mn_b], n=NMEM)
            _hn_small(P, C, S, pk, pk_b, gxk_t, gxk_b, kx, kx_b, hh)
    for mc in range(2):
        pv, pv_b = C.next_ps()
        mm_acc(pv, pv_b, lambda kc, mc=mc: mn[:, kc, mc * 128:(mc + 1) * 128], lambda kc: u_sb[:, kc, :], KC,
               lambda kc: [mn_b, u_b[kc]])
        P.op("act", lambda e, mc=mc, pv=pv: e.activation(out=vx[:, mc, :], in_=pv[:, :], func=AF.Copy), reads=[pv_b], writes=[vx_b])

    xin_v = x1T.rearrange("(c p) n -> p c n", p=128)
    xout_v = xoT.rearrange("(c p) n -> p c n", p=128) if with_out else None
    YA, YB, YC, MG = 0, 8, 16, 24
    def tile(t, store=True):
        tok = slice(t * TT, (t + 1) * TT)
        P.dma(x_sb[:, :, :], xin_v[:, :, tok], writes=[x_b])
        norm_to_h(P, C, S, gm_t, gm_b)
        P.dma(u_sb[:, YB:YB + 8, :], ybT.rearrange("(c p) n -> p c n", p=128)[:, :, tok], writes=[u_b[YB + j] for j in range(8)])
        pss, pss_b = C.ps[7], C.psb[7]
        for blk in range(8):
            w_t, w_bf = next_w()
            P.dma(w_t[:, :, :], winC_bf[blk], reads=[winC_b], writes=[w_bf])
            for s in range(2):
                j = (blk % 4) * 2 + s
                ps, ps_b = C.next_ps()
                mm_acc(ps, ps_b, lambda kc, w_t=w_t, s=s: w_t[:, kc, s * 128:(s + 1) * 128], lambda kc: h_sb[:, kc, :], KC,
                       lambda kc, w_bf=w_bf: [w_bf, h_b[kc]])
                src = hmT if blk < 4 else ysT
                P.dma(ytmp[:, :], src[j * 128:(j + 1) * 128, tok], writes=[ytmp_b])
                g_t, g_bf = gsb[j % 2], gsb_b[j % 2]
                if blk < 4:
                    P.op("act", lambda e, ps=ps, g_t=g_t: e.activation(out=g_t[:, :], in_=ps[:, :], func=AF.Sigmoid), reads=[ps_b], writes=[g_bf])
                    P.op("dve", lambda e, j=j, g_t=g_t: e.tensor_tensor(out=u_sb[:, YA + j, :], in0=g_t[:, :], in1=ytmp[:, :], op=ALU.mult),
                         reads=[g_bf, ytmp_b], writes=[u_b[YA + j]])
                else:
                    P.op("act", lambda e, ps=ps, g_t=g_t: e.activation(out=g_t[:, :], in_=ps[:, :], func=AF.Silu), reads=[ps_b], writes=[g_bf])
                    P.op("dve", lambda e, j=j, g_t=g_t: e.tensor_tensor(out=g_t[:, :], in0=g_t[:, :], in1=ytmp[:, :], op=ALU.mult),
                         reads=[g_bf, ytmp_b], writes=[g_bf])
                    P.op("pool", lambda e, j=j, g_t=g_t: e.tensor_copy(out=u_sb[:, YC + j, :], in_=g_t[:, :]), reads=[g_bf], writes=[u_b[YC + j]])
                    sq, sq_b = S["sq"][j % 2], S["sq_b"][j % 2]
                    P.op("act", lambda e, g_t=g_t, sq=sq: e.activation(out=sq[:, :], in_=g_t[:, :], func=AF.Square), reads=[g_bf], writes=[sq_b])
                    P.op("pe", lambda e, j=j, sq=sq: e.matmul(pss[:, :], C.ones[:], sq[:, :], start=(j == 0), stop=(j == 7)),
                         reads=[sq_b, C.ones_b], writes=[pss_b])
        P.op("act", lambda e: e.activation(out=S["rstd2"][:, :], in_=pss[:, :], func=AF.Sqrt, bias=C.eps[:, 0:1], scale=1.0 / 1024),
             reads=[pss_b, C.ones_b], writes=[S["rstd2_b"]])
        P.op("dve", lambda e: e.reciprocal(out=S["rstd2"][:, :], in_=S["rstd2"][:, :]), reads=[S["rstd2_b"]], writes=[S["rstd2_b"]])
        for j in range(8):
            P.op("dve", lambda e, j=j: e.scalar_tensor_tensor(out=u_sb[:, YC + j, :], in0=u_sb[:, YC + j, :], scalar=gs_t[:, j:j + 1], in1=S["rstd2"][:, :],
                                                            op0=ALU.mult, op1=ALU.mult),
                 reads=[u_b[YC + j], gs_b, S["rstd2_b"]], writes=[u_b[YC + j]])
        for ip in range(8):
            for b in range(3):
                wg_t, wg_bf = next_w()
                P.dma(wg_t[:, :, :], winC_bf[8 + b * 8 + ip], reads=[winC_b], writes=[wg_bf])
                wb_t, wb_bf = next_w()
                P.dma(wb_t[:, 0:8, :], wbr_bf[b][ip], reads=[wbr_b[b]], writes=[wb_bf])
                yoff = (YA, YB, YC)[b]
                for s in range(2):
                    i = ip * 2 + s
                    pg, pg_b = C.next_ps()
                    mm_acc(pg, pg_b, lambda kc, wg_t=wg_t, s=s: wg_t[:, kc, s * 128:(s + 1) * 128], lambda kc: h_sb[:, kc, :], KC,
                           lambda kc, wg_bf=wg_bf: [wg_bf, h_b[kc]])
                    pb, pb_b = C.next_ps()
                    mm_acc(pb, pb_b, lambda kc, wb_t=wb_t, s=s: wb_t[:, kc, s * 128:(s + 1) * 128], lambda kc, yoff=yoff: u_sb[:, yoff + kc, :], 8,
                           lambda kc, wb_bf=wb_bf, yoff=yoff: [wb_bf, u_b[yoff + kc]])
                    g_t, g_bf = gsb[s], gsb_b[s]
                    P.op("act", lambda e, pg=pg, g_t=g_t: e.activation(out=g_t[:, :], in_=pg[:, :], func=AF.Sigmoid), reads=[pg_b], writes=[g_bf])
                    if b == 0:
                        P.op("dve", lambda e, pb=pb, g_t=g_t, s=s: e.tensor_tensor(out=macc[s][:, :], in0=g_t[:, :], in1=pb[:, :], op=ALU.mult),
                             reads=[g_bf, pb_b], writes=[macc_b[s]])
                    else:
                        P.op("dve", lambda e, pb=pb, g_t=g_t: e.tensor_tensor(out=g_t[:, :], in0=g_t[:, :], in1=pb[:, :], op=ALU.mult),
                             reads=[g_bf, pb_b], writes=[g_bf])
                        if b == 1:
                            P.op("pool", lambda e, g_t=g_t, s=s: e.tensor_tensor(out=macc[s][:, :], in0=macc[s][:, :], in1=g_t[:, :], op=ALU.add),
                                 reads=[g_bf, macc_b[s]], writes=[macc_b[s]])
                        else:
                            P.op("pool", lambda e, g_t=g_t, s=s, i=i: e.tensor_tensor(out=u_sb[:, MG + i, :], in0=macc[s][:, :], in1=g_t[:, :], op=ALU.add),
                                 reads=[g_bf, macc_b[s]], writes=[u_b[MG + i]])
        for blk in range(8):
            w_t, w_bf = next_w()
            P.dma(w_t[:, :, :], wout_bf[blk], reads=[wout_b], writes=[w_bf])
            for s in range(2):
                i = blk * 2 + s
                ps, ps_b = C.next_ps()
                mm_acc(ps, ps_b, lambda kc, w_t=w_t, s=s: w_t[:, kc, s * 128:(s + 1) * 128], lambda kc: u_sb[:, MG + kc, :], KC,
                       lambda kc, w_bf=w_bf: [w_bf, u_b[MG + kc]])
                P.op("dve", lambda e, i=i, ps=ps: e.tensor_tensor(out=x_sb[:, i, :], in0=x_sb[:, i, :], in1=ps[:, :], op=ALU.add),
                     reads=[ps_b, x_b], writes=[x_b])
        norm_to_h(P, C, S, gxa_t, gxa_b)
        QX, OX, PX = 0, 4, 8
        for blk in range(2):
            w_t, w_bf = next_w()
            P.dma(w_t[:, :, :], wq_bf[blk], reads=[wq_b], writes=[w_bf])
            for s in range(2):
                hh = blk * 2 + s
                ps, ps_b = C.next_ps()
                mm_acc(ps, ps_b, lambda kc, w_t=w_t, s=s: w_t[:, kc, s * 128:(s + 1) * 128], lambda kc: h_sb[:, kc, :], KC,
                       lambda kc, w_bf=w_bf: [w_bf, h_b[kc]])
                head_norm(P, C, S, ps, ps_b, 128, gxq_t[:, 0:1], gxq_b, u_sb[:, QX + hh, :], u_b[QX + hh])
        for hh in range(4):
            for mc in range(2):
                psc, psc_b = C.next_ps()
                P.op("pe", lambda e, hh=hh, mc=mc, psc=psc: e.matmul(psc[:, :], kx[:, hh, mc * 128:(mc + 1) * 128], u_sb[:, QX + hh, :], start=True, stop=True),
                     reads=[kx_b, u_b[QX + hh]], writes=[psc_b])
                P.op("act", lambda e, mc=mc, psc=psc: e.activation(out=u_sb[:, PX + mc, :], in_=psc[:, :], func=AF.Exp, bias=negc[:, 0:1]),
                     reads=[psc_b, negc_b], writes=[u_b[PX + mc]])
            po, po_b = C.next_ps()
            pd, pd_b = C.next_ps()
            for mc in range(2):
                P.op("pe", lambda e, hh=hh, mc=mc, po=po: e.matmul(po[:, :], vx[:, mc, hh * 128:(hh + 1) * 128], u_sb[:, PX + mc, :], start=(mc == 0), stop=(mc == 1)),
                     reads=[vx_b, u_b[PX + mc]], writes=[po_b])
            for mc in range(2):
                P.op("pe", lambda e, mc=mc, pd=pd: e.matmul(pd[:, :], onesb[:, :], u_sb[:, PX + mc, :], start=(mc == 0), stop=(mc == 1)),
                     reads=[onesb_b, u_b[PX + mc]], writes=[pd_b])
            P.op("dve", lambda e, pd=pd: e.reciprocal(out=S["rstd2"][:, :], in_=pd[:, :]), reads=[pd_b], writes=[S["rstd2_b"]])
            P.op("dve", lambda e, hh=hh, po=po: e.tensor_tensor(out=u_sb[:, OX + hh, :], in0=po[:, :], in1=S["rstd2"][:, :], op=ALU.mult),
                 reads=[po_b, S["rstd2_b"]], writes=[u_b[OX + hh]])
        for blk in range(8):
            w_t, w_bf = next_w()
            P.dma(w_t[:, 0:4, :], wo_bf[blk], reads=[wo_b], writes=[w_bf])
            for s in range(2):
                i = blk * 2 + s
                ps, ps_b = C.next_ps()
                mm_acc(ps, ps_b, lambda kc, w_t=w_t, s=s: w_t[:, kc, s * 128:(s + 1) * 128], lambda kc: u_sb[:, OX + kc, :], 4,
                       lambda kc, w_bf=w_bf: [w_bf, u_b[OX + kc]])
                P.op("dve", lambda e, i=i, ps=ps: e.tensor_tensor(out=x_sb[:, i, :], in0=x_sb[:, i, :], in1=ps[:, :], op=ALU.add),
                     reads=[ps_b, x_b], writes=[x_b])
        ffn_tile(P, C, S, g2_t, g2_b, wup_bf, wup_b, wdn_bf, wdn_b)
        if store:
            P.dma(xout_v[:, :, tok], x_sb[:, :, :], reads=[x_b])

    return dict(locals())


def build_C(ntok):
    P = Prog()
    C = Ctx(P)
    S = alloc_shared(P)
    Cc = setup_C(P, C, S, ntok, pre="c_")
    for t in range(ntok // TT):
        Cc["tile"](t, True)
    return P


def build_CA(ntok):
    P = Prog()
    C = Ctx(P)
    S = alloc_shared(P)
    Cc = setup_C(P, C, S, ntok, pre="c_", with_out=False)
    A = setup_A(P, C, S, ntok, pre="a_", with_x_in=False, resident_kv=False)
    for t in range(ntok // TT):
        Cc["tile"](t, False)
        tile_A(P, C, S, A, t, ntok)
    return P


def _hn_small(P, C, S, pk, pk_b, g_t, g_b, kx, kx_b, hh):
    t, t_b = next_tmp(S)
    sq, sq_b = S["sq"][0], S["sq_b"][0]
    n = NMEM
    P.op("act", lambda e: e.activation(out=t[:, :n], in_=pk[:, :n], func=AF.Copy), reads=[pk_b], writes=[t_b])
    P.op("act", lambda e: e.activation(out=sq[:, :n], in_=pk[:, :n], func=AF.Square), reads=[pk_b], writes=[sq_b])
    p2, p2_b = C.next_ps()
    P.op("pe", lambda e: e.matmul(p2[:, :n], C.ones[:, :], sq[:, :n], start=True, stop=True), reads=[sq_b, C.ones_b], writes=[p2_b])
    r, r_b = S["rstd2"], S["rstd2_b"]
    P.op("act", lambda e: e.activation(out=r[:, :n], in_=p2[:, :n], func=AF.Sqrt, bias=C.eps[:, 0:1], scale=1.0 / 128), reads=[p2_b, C.ones_b], writes=[r_b])
    P.op("dve", lambda e: e.reciprocal(out=r[:, :n], in_=r[:, :n]), reads=[r_b], writes=[r_b])
    P.op("dve", lambda e: e.scalar_tensor_tensor(out=kx[:, hh, :], in0=t[:, :n], scalar=g_t[:, 0:1], in1=r[:, :n], op0=ALU.mult, op1=ALU.mult),
         reads=[t_b, r_b, g_b], writes=[kx_b])


B_SZ, SEQ, NCORE = 2, 16384, 8
NTOK = 4096
NQB = 32
BLK_SADM = [512 * (i + 1) for i in range(NQB)]
_PROGS = {}


def _prog(name):
    if name not in _PROGS:
        if name == "A":
            _PROGS[name] = build_A(NTOK).emit()
        elif name == "B":
            _PROGS[name] = build_B(SEQ, NQB, BLK_SADM).emit()
        elif name == "CA":
            _PROGS[name] = build_CA(NTOK).emit()
        else:
            _PROGS[name] = build_C(NTOK).emit()
    return _PROGS[name]


def _pc(v, c):
    return np.ascontiguousarray(np.asarray(v, np.float32).reshape(c, -1).T)


def _rep(v):
    v = np.asarray(v, np.float32)
    return np.ascontiguousarray(np.broadcast_to(v[None], (128,) + v.shape))


def _run(nc, in_maps):
    res = run_bass_kernel_spmd(nc, in_maps, core_ids=list(range(NCORE)))
    return res.results


def kernel(x, mem, ffn1_norm, ffn1_w_up, ffn1_w_down, mix_norm, w_in,
           ml_conv, ml_i_bias, ml_f_bias, ml_out_norm,
           dsa_q_norm, dsa_k_norm, dsa_kv_norm, dsa_w_uk, dsa_w_uv, idx_k_norm,
           ssm_conv, ssm_conv_b, ssm_dt_bias, ssm_a_log, ssm_d, ssm_norm,
           w_branch, w_out, xa_norm, xa_mem_norm, xa_wq, xa_wkv, xa_q_norm, xa_k_norm, xa_wo,
           ffn2_norm, ffn2_w_up, ffn2_w_down):
    A_ = np.ascontiguousarray
    f32 = np.float32
    x = np.asarray(x, f32)
    mem = np.asarray(mem, f32)
    main, small, later = perm_cols_A()
    nch = SEQ // 128
    xT = [A_(x[c // 4, (c % 4) * NTOK:(c % 4 + 1) * NTOK, :].T) for c in range(NCORE)]

    def a_inputs(l):
        w_in_l = np.asarray(w_in[l], f32)
        d = {"g1": _pc(ffn1_norm[l], 16), "wup": A_(ffn1_w_up[l]), "wdn": A_(ffn1_w_down[l]), "gmix": _pc(mix_norm[l], 16),
             "winA": A_(w_in_l[:, main]), "winS": A_(w_in_l[:, small]),
             "gq": A_(np.asarray(dsa_q_norm[l], f32).reshape(128, 1)), "gk": A_(np.asarray(dsa_k_norm[l], f32).reshape(128, 1)),
             "gkv": _pc(dsa_kv_norm[l], 4), "gik": A_(np.asarray(idx_k_norm[l], f32).reshape(64, 1)),
             "wuk": A_(dsa_w_uk[l]), "wuv": A_(dsa_w_uv[l])}
        return {"a_" + k: v for k, v in d.items()}

    def c_inputs(l):
        w_in_l = np.asarray(w_in[l], f32)
        d = {"gmix": _pc(mix_norm[l], 16), "gssm": _pc(ssm_norm[l], 8), "winC": A_(w_in_l[:, later]), "wbr": A_(w_branch[l]), "wout": A_(w_out[l]),
             "gxa": _pc(xa_norm[l], 16), "gxm": _pc(xa_mem_norm[l], 16),
             "gxq": A_(np.asarray(xa_q_norm[l], f32).reshape(128, 1)), "gxk": A_(np.asarray(xa_k_norm[l], f32).reshape(128, 1)),
             "wq": A_(xa_wq[l]), "wkv": A_(xa_wkv[l]), "wo": A_(xa_wo[l]), "g2": _pc(ffn2_norm[l], 16),
             "wup": A_(ffn2_w_up[l]), "wdn": A_(ffn2_w_down[l])}
        return {"c_" + k: v for k, v in d.items()}

    comA = a_inputs(0)
    resA = _run(_prog("A"), [dict(comA, a_xT=xT[c]) for c in range(NCORE)])
    del comA
    for l in range(2):
        x1T = [r["a_x1T"] for r in resA]
        in_B = []
        for b in range(B_SZ):
            PT = np.concatenate([resA[b * 4 + q]["a_PT"] for q in range(4)], axis=1)
            kiT = np.concatenate([resA[b * 4 + q]["a_kiT"] for q in range(4)], axis=1)
            smT = np.concatenate([resA[b * 4 + q]["a_smT"] for q in range(4)], axis=1)
            d_kT = A_(PT[24 * 128:32 * 128].reshape(8, 128, SEQ).transpose(1, 0, 2))
            d_v = A_(PT[32 * 128:40 * 128].T)
            d_ki2 = A_(np.concatenate([kiT, kiT], axis=0))
            qd = PT[16 * 128:24 * 128].reshape(8, 128, nch, 128)
            qi = PT[40 * 128:48 * 128].reshape(8, 128, nch, 128)
            iw = smT[8:24].reshape(16, nch, 128)
            for g in range(4):
                blks = np.arange(NQB) * 4 + g
                pos = blks[None, :] * 128 + np.arange(128)[:, None]
                lim = ((pos // 64 + 1) * 64).astype(f32)
                gr = g // 2
                chs = [np.arange(256 * g, 256 * g + 128), np.arange(256 * g + 128, 256 * g + 256),
                       np.arange(1024 + 128 * gr, 1024 + 128 * gr + 128), np.arange(1280 + 128 * gr, 1280 + 128 * gr + 128)]
                conv_s = np.asarray(ssm_conv[l], f32)
                convb_s = np.asarray(ssm_conv_b[l], f32)
                conv_m = np.asarray(ml_conv[l], f32)
                m = {
                    "ml_qk": A_(np.stack([PT[g * 128:(g + 1) * 128], PT[512 + g * 128:512 + (g + 1) * 128]])),
                    "ml_v": A_(PT[1024 + g * 256:1024 + (g + 1) * 256].T),
                    "ml_if": A_(np.stack([smT[g].reshape(nch, 128).T, smT[4 + g].reshape(nch, 128).T], axis=1)),
                    "ml_cw": A_(np.concatenate([conv_m[:, g * 128:(g + 1) * 128].T, conv_m[:, 512 + g * 128:512 + (g + 1) * 128].T], axis=1)),
                    "ml_bias": _rep(np.array([np.asarray(ml_i_bias[l], f32)[g], np.asarray(ml_f_bias[l], f32)[g]], f32)),
                    "ml_gn": _rep(np.asarray(ml_out_norm[l], f32)[g * 256:(g + 1) * 256]),
                    "ss_raw": A_(np.stack([PT[(48 + 2 * g) * 128:(49 + 2 * g) * 128], PT[(49 + 2 * g) * 128:(50 + 2 * g) * 128],
                                           PT[(56 + gr) * 128:(57 + gr) * 128], PT[(58 + gr) * 128:(59 + gr) * 128]])),
                    "ss_dt": A_(np.stack([smT[24 + 4 * g + h].reshape(nch, 128).T for h in range(4)], axis=1)),
                    "ss_cw": A_(np.stack([conv_s[:, ch].T for ch in chs], axis=1)),
                    "ss_cb": A_(np.stack([convb_s[ch] for ch in chs], axis=1)),
                    "ss_par": _rep(np.stack([np.asarray(ssm_dt_bias[l], f32)[4 * g:4 * g + 4], np.asarray(ssm_a_log[l], f32)[4 * g:4 * g + 4],
                                             np.asarray(ssm_d[l], f32)[4 * g:4 * g + 4]])),
                    "d_q": A_(qd[:, :, blks, :].transpose(2, 1, 0, 3)),
                    "d_iq": A_(qi[:, :, blks, :].transpose(2, 1, 0, 3)),
                    "d_w": A_(iw[:, blks, :].transpose(2, 1, 0)),
                    "d_lim": A_(lim),
                    "d_kT": d_kT, "d_v": d_v, "d_ki2": d_ki2,
                }
                in_B.append(m)
        del resA
        resB = _run(_prog("B"), in_B)
        del in_B
        commonC = c_inputs(l)
        if l == 0:
            commonC.update(a_inputs(1))
        in_C = []
        for b in range(B_SZ):
            hm = np.concatenate([resB[b * 4 + g]["ml_h"] for g in range(4)], axis=1)
            ys = np.concatenate([resB[b * 4 + g]["ss_y"] for g in range(4)], axis=1)
            yb = np.empty((nch, 128, 1024), dtype=resB[0]["d_y"].dtype)
            for g in range(4):
                yb[np.arange(NQB) * 4 + g] = resB[b * 4 + g]["d_y"]
            yb = yb.reshape(SEQ, 1024)
            memT = A_(mem[b].T)
            for q in range(4):
                tok = slice(q * NTOK, (q + 1) * NTOK)
                in_C.append(dict(commonC, c_x1T=x1T[b * 4 + q], c_hmT=A_(hm[tok].T), c_ybT=A_(yb[tok].T), c_ysT=A_(ys[tok].T), c_memT=memT))
        del resB
        if l == 0:
            resA = _run(_prog("CA"), in_C)
        else:
            resC = _run(_prog("C"), in_C)
            xT = [r["c_xoT"] for r in resC]
        del in_C, commonC
    out = np.empty((B_SZ, SEQ, D), f32)
    for c in range(NCORE):
        out[c // 4, (c % 4) * NTOK:(c % 4 + 1) * NTOK, :] = xT[c].T
    return out
```

```python
from concourse.bass_utils import run_bass_kernel_spmd
import contextlib
import numpy as np
import concourse.bass as bass
import concourse.mybir as mybir

F32 = mybir.dt.float32
BF16 = mybir.dt.bfloat16
AF = mybir.ActivationFunctionType
ALU = mybir.AluOpType
AX = mybir.AxisListType

SEM_MAX = 30000
N_DMA_SEMS = 12


class Buf:
    __slots__ = ("name", "last_w", "readers")

    def __init__(self, name=""):
        self.name = name
        self.last_w = None
        self.readers = []


class Op:
    __slots__ = ("eng", "fn", "deps", "signal", "sem", "val", "is_dma", "idx")


class Prog:
    ENGS = ("pe", "act", "dve", "pool", "sp")

    def __init__(self):
        self.nc = bass.Bass("TRN2", target_bir_lowering=False)
        self.ops = []
        self.stack = contextlib.ExitStack()
        self.eng_ops = {e: [] for e in self.ENGS}
        self.dma_rr = {"sp": 0, "pool": 0, "act": 0}

    def dram_in(self, name, shape, dt=F32):
        return self.nc.dram_tensor(name, list(shape), dt, kind="ExternalInput").ap()

    def dram_out(self, name, shape, dt=F32):
        return self.nc.dram_tensor(name, list(shape), dt, kind="ExternalOutput").ap()

    def dram_tmp(self, name, shape, dt=F32):
        return self.nc.dram_tensor(name, list(shape), dt, kind="Internal").ap()

    def sbuf(self, name, shape, dt=F32):
        return self.stack.enter_context(self.nc.sbuf_tensor("sb_" + name, list(shape), dt))

    def psum(self, name, shape, dt=F32):
        return self.stack.enter_context(self.nc.psum_tensor("pp_" + name, list(shape), dt))

    def op(self, eng, fn, reads=(), writes=(), is_dma=False):
        o = Op()
        o.eng = eng
        o.fn = fn
        o.is_dma = is_dma
        o.signal = False
        o.sem = None
        o.val = None
        o.idx = len(self.ops)
        deps = set()
        for b in reads:
            if b.last_w is not None:
                deps.add(b.last_w)
        for b in writes:
            if b.last_w is not None:
                deps.add(b.last_w)
            deps.update(b.readers)
        for b in reads:
            b.readers.append(o.idx)
        for b in writes:
            b.last_w = o.idx
            b.readers = []
        o.deps = deps
        self.ops.append(o)
        self.eng_ops[eng].append(o)
        return o

    def dma(self, out, in_, reads=(), writes=(), eng="sp"):
        return self.op(eng, lambda e: e.dma_start(out=out, in_=in_), reads, writes, is_dma=True)

    def emit(self):
        nc = self.nc
        ops = self.ops
        for o in ops:
            for d in o.deps:
                p = ops[d]
                if p.eng == "pe" and o.eng == "pe" and not p.is_dma:
                    continue
                p.signal = True
        for o in ops:
            if o.is_dma:
                o.signal = True
        sems = []

        def new_sem(nm):
            s = self.stack.enter_context(nc.semaphore(nm))
            sems.append(s)
            return s

        cur = {}
        cnt = {}
        dma_sems = {}
        dma_cnt = {}
        dma_prev = {}
        extra_wait = {}
        for e in self.ENGS:
            k = 0
            for o in self.eng_ops[e]:
                if o.is_dma:
                    if e not in dma_sems:
                        dma_sems[e] = [new_sem(f"dma_{e}_{i}") for i in range(N_DMA_SEMS)]
                        dma_cnt[e] = [0] * N_DMA_SEMS
                    slot = k % N_DMA_SEMS
                    k += 1
                    if dma_cnt[e][slot] + 16 > SEM_MAX:
                        dma_sems[e][slot] = new_sem(f"dma_{e}_{slot}_n")
                        dma_cnt[e][slot] = 0
                    prev = dma_prev.get((e, slot))
                    if prev is not None:
                        extra_wait[o.idx] = [(prev.sem, prev.val)]
                    dma_cnt[e][slot] += 16
                    o.sem = dma_sems[e][slot]
                    o.val = dma_cnt[e][slot]
                    dma_prev[(e, slot)] = o
                elif o.signal:
                    if e not in cur or cnt[e] + 1 > SEM_MAX:
                        cur[e] = new_sem(f"c_{e}_{len(sems)}")
                        cnt[e] = 0
                    cnt[e] += 1
                    o.sem = cur[e]
                    o.val = cnt[e]
        self.n_sems = len(sems)

        def run_engine(ename, eng):
            waited = {}
            for o in self.eng_ops[ename]:
                need = {}
                for d in o.deps:
                    p = ops[d]
                    if p.sem is None:
                        continue
                    if p.eng == "pe" and ename == "pe" and not p.is_dma:
                        continue
                    key = id(p.sem)
                    if key not in need or need[key][1] < p.val:
                        need[key] = (p.sem, p.val)
                for (s, v) in extra_wait.get(o.idx, ()):
                    key = id(s)
                    if key not in need or need[key][1] < v:
                        need[key] = (s, v)
                for key, (s, v) in need.items():
                    if waited.get(key, 0) >= v:
                        continue
                    eng.wait_ge(s, v)
                    waited[key] = v
                ins = o.fn(eng)
                if o.sem is not None:
                    ins.then_inc(o.sem, 16 if o.is_dma else 1)
            if ename in dma_sems:
                for (e2, slot), p in dma_prev.items():
                    if e2 == ename:
                        eng.wait_ge(p.sem, p.val)

        with nc.Block() as block:
            @block.sync
            def _(e):
                run_engine("sp", e)

            @block.tensor
            def _(e):
                run_engine("pe", e)

            @block.scalar
            def _(e):
                run_engine("act", e)

            @block.vector
            def _(e):
                run_engine("dve", e)

            @block.gpsimd
            def _(e):
                run_engine("pool", e)
        self.stack.close()
        return nc


D = 2048
DFF = 5632
EPS = 1e-6
TT = 512
KC = D // 128
FC = DFF // 128


class Ctx:
    def __init__(self, P):
        self.P = P
        nc = P.nc
        self.ps = [P.psum(f"ps{i}", [128, 512], F32) for i in range(8)]
        self.psb = [Buf(f"ps{i}") for i in range(8)]
        self.ps_rr = 0
        self.ones = P.sbuf("ones", [128, 128], F32)
        self.ones_b = Buf("ones")
        self.eps = P.sbuf("epsc", [128, 1], F32)
        P.op("dve", lambda e: e.memset(self.ones[:], 1.0), writes=[self.ones_b])
        P.op("dve", lambda e: e.memset(self.eps[:], EPS), writes=[self.ones_b])

    def next_ps(self):
        i = self.ps_rr % 7
        self.ps_rr += 1
        return self.ps[i], self.psb[i]


def cast_weight_tiled(P, w_f32, w_bf, kc, blk, nblk, name):
    b = Buf(name)
    src = w_f32.rearrange("(kc p) n -> p kc n", p=128)
    for j in range(nblk):
        P.dma(w_bf[j], src[:, :, j * blk:(j + 1) * blk], writes=[b], eng="pool")
    return b


def rms_rstd(P, C, x_sb, x_b, rstd, rstd_b, sq, sq_b, nch, width, dtot):
    ps, psb = C.next_ps()
    for c in range(nch):
        k = c % 2
        P.op("act", lambda e, c=c, k=k: e.activation(out=sq[k][:, :width], in_=x_sb[:, c, :width], func=AF.Square),
             reads=[x_b], writes=[sq_b[k]])
        P.op("pe", lambda e, c=c, k=k: e.matmul(ps[:, :width], C.ones[:], sq[k][:, :width], start=(c == 0), stop=(c == nch - 1)),
             reads=[sq_b[k], C.ones_b], writes=[psb])
    P.op("act", lambda e: e.activation(out=rstd[:, :width], in_=ps[:, :width], func=AF.Sqrt, bias=C.eps[:, 0:1], scale=1.0 / dtot),
         reads=[psb, C.ones_b], writes=[rstd_b])
    P.op("dve", lambda e: e.reciprocal(out=rstd[:, :width], in_=rstd[:, :width]), reads=[rstd_b], writes=[rstd_b])


def ffn_tile(P, C, S, gs, gs_b, wup_bf, wup_b, wdn_bf, wdn_b):
    x_sb, x_b = S["x"], S["x_b"]
    h_sb, h_b = S["h"], S["h_b"]
    u_sb, u_b = S["u"], S["u_b"]
    norm_to_h(P, C, S, gs, gs_b)
    for j in range(22):
        k = S["wup_rr"] % 2
        S["wup_rr"] += 1
        wa, wb, w_b, wb_b = S["wa"][k], S["wb"][k], S["wab_b"][k], S["wbb_b"][k]
        P.dma(wa[:, :, :], wup_bf[j], reads=[wup_b], writes=[w_b])
        P.dma(wb[:, :, :], wup_bf[22 + j], reads=[wup_b], writes=[wb_b])
        for s in range(2):
            pa, pa_b = C.next_ps()
            pb, pb_b = C.next_ps()
            for kc in range(KC):
                P.op("pe", lambda e, kc=kc, s=s, pa=pa, wa=wa: e.matmul(pa[:, :], wa[:, kc, s * 128:(s + 1) * 128], h_sb[:, kc, :],
                                                                     start=(kc == 0), stop=(kc == KC - 1)),
                     reads=[w_b, h_b[kc]], writes=[pa_b])
            for kc in range(KC):
                P.op("pe", lambda e, kc=kc, s=s, pb=pb, wb=wb: e.matmul(pb[:, :], wb[:, kc, s * 128:(s + 1) * 128], h_sb[:, kc, :],
                                                                     start=(kc == 0), stop=(kc == KC - 1)),
                     reads=[wb_b, h_b[kc]], writes=[pb_b])
            q = S["sa_rr"] % 2
            S["sa_rr"] += 1
            sa, sa_b = S["sa"][q], S["sa_b"][q]
            fc = j * 2 + s
            P.op("act", lambda e, pa=pa, sa=sa: e.activation(out=sa[:, :], in_=pa[:, :], func=AF.Silu),
                 reads=[pa_b], writes=[sa_b])
            P.op("dve", lambda e, pb=pb, sa=sa, fc=fc: e.tensor_tensor(out=u_sb[:, fc, :], in0=sa[:, :], in1=pb[:, :], op=ALU.mult),
                 reads=[sa_b, pb_b], writes=[u_b[fc]])
    for i in range(KC):
        k = S["wdn_rr"] % 2
        S["wdn_rr"] += 1
        wd, wd_b = S["wd"][k], S["wd_b"][k]
        P.dma(wd[:, :, :], wdn_bf[i], reads=[wdn_b], writes=[wd_b])
        po, po_b = C.next_ps()
        for kc in range(FC):
            P.op("pe", lambda e, kc=kc, po=po, wd=wd: e.matmul(po[:, :], wd[:, kc, :], u_sb[:, kc, :],
                                                             start=(kc == 0), stop=(kc == FC - 1)),
                 reads=[wd_b, u_b[kc]], writes=[po_b])
        P.op("dve", lambda e, i=i, po=po: e.scalar_tensor_tensor(out=x_sb[:, i, :], in0=po[:, :], scalar=0.5, in1=x_sb[:, i, :],
                                                               op0=ALU.mult, op1=ALU.add),
             reads=[po_b, x_b], writes=[x_b])


def norm_to_h(P, C, S, gs, gs_b):
    x_sb, x_b = S["x"], S["x_b"]
    h_sb, h_b = S["h"], S["h_b"]
    rms_rstd(P, C, x_sb, x_b, S["rstd"], S["rstd_b"], S["sq"], S["sq_b"], KC, TT, D)
    for c in range(KC):
        P.op("dve", lambda e, c=c: e.scalar_tensor_tensor(out=h_sb[:, c, :], in0=x_sb[:, c, :], scalar=gs[:, c:c + 1],
                                                        in1=S["rstd"][:, :], op0=ALU.mult, op1=ALU.mult),
             reads=[x_b, gs_b, S["rstd_b"]], writes=[h_b[c]])


def load_vec(P, name, dram_ap, shape):
    t = P.sbuf(name, shape, F32)
    b = Buf(name)
    P.dma(t[:], dram_ap, writes=[b])
    return t, b


def alloc_shared(P):
    S = {}
    S["x"] = P.sbuf("x_sb", [128, KC, TT], F32)
    S["x_b"] = Buf("x")
    S["h"] = P.sbuf("h_sb", [128, KC, TT], BF16)
    S["h_b"] = [Buf(f"h{c}") for c in range(KC)]
    S["u"] = P.sbuf("u_sb", [128, FC, TT], BF16)
    S["u_b"] = [Buf(f"u{c}") for c in range(FC)]
    S["rstd"] = P.sbuf("rstd", [128, TT], F32)
    S["rstd_b"] = Buf("rstd")
    S["sq"] = [P.sbuf(f"sq{i}", [128, TT], F32) for i in range(2)]
    S["sq_b"] = [Buf(f"sq{i}") for i in range(2)]
    S["sa"] = [P.sbuf(f"sa{i}", [128, TT], F32) for i in range(2)]
    S["sa_b"] = [Buf(f"sa{i}") for i in range(2)]
    S["wa"] = [P.sbuf(f"wa{i}", [128, KC, 256], BF16) for i in range(2)]
    S["wb"] = [P.sbuf(f"wb{i}", [128, KC, 256], BF16) for i in range(2)]
    S["wab_b"] = [Buf(f"wab{i}") for i in range(2)]
    S["wbb_b"] = [Buf(f"wbb{i}") for i in range(2)]
    S["wd"] = [P.sbuf(f"wd{i}", [128, FC, 128], BF16) for i in range(2)]
    S["wd_b"] = [Buf(f"wd{i}") for i in range(2)]
    S["wup_rr"] = 0
    S["wdn_rr"] = 0
    S["sa_rr"] = 0
    return S


NA_COLS = 6144
N_PT = 60


def alloc_A_extra(P, S, resident_kv=True):
    S["ob"] = [P.sbuf(f"ob{i}", [128, TT], BF16) for i in range(3)]
    S["ob_b"] = [Buf(f"ob{i}") for i in range(3)]
    S["ob_rr"] = 0
    if "tmp" not in S:
        S["tmp"] = [P.sbuf(f"tmpf{i}", [128, TT], F32) for i in range(2)]
        S["tmp_b"] = [Buf(f"tmpf{i}") for i in range(2)]
        S["tmp_rr"] = 0
        S["rstd2"] = P.sbuf("rstd2", [128, TT], F32)
        S["rstd2_b"] = Buf("rstd2")
    S["ckv"] = P.sbuf("ckv", [128, 4, TT], F32)
    S["ckv_b"] = Buf("ckv")
    S["ckvn"] = P.sbuf("ckvn", [128, 4, TT], BF16)
    S["ckvn_b"] = Buf("ckvn")
    if resident_kv:
        S["wuk"] = P.sbuf("wuk", [128, 4, 1024], BF16)
        S["wuv"] = P.sbuf("wuv", [128, 4, 1024], BF16)
        S["wukv_b"] = Buf("wukv")
    S["wsm"] = P.sbuf("wsm", [128, KC, 104], BF16)
    S["wsm_b"] = Buf("wsm")
    S["osm"] = P.sbuf("osm", [128, TT], F32)
    S["osm_b"] = Buf("osm")


def next_ob(S):
    k = S["ob_rr"] % 3
    S["ob_rr"] += 1
    return S["ob"][k], S["ob_b"][k]


def next_tmp(S):
    k = S["tmp_rr"] % 2
    S["tmp_rr"] += 1
    return S["tmp"][k], S["tmp_b"][k]


def head_norm(P, C, S, ps, ps_b, np_, gain, gain_b, out_ap, out_b):
    t, t_b = next_tmp(S)
    sq, sq_b = S["sq"][0], S["sq_b"][0]
    P.op("act", lambda e: e.activation(out=t[:np_, :], in_=ps[:np_, :], func=AF.Copy), reads=[ps_b], writes=[t_b])
    P.op("act", lambda e: e.activation(out=sq[:np_, :], in_=ps[:np_, :], func=AF.Square), reads=[ps_b], writes=[sq_b])
    p2, p2_b = C.next_ps()
    P.op("pe", lambda e: e.matmul(p2[:np_, :], C.ones[:np_, :np_], sq[:np_, :], start=True, stop=True),
         reads=[sq_b, C.ones_b], writes=[p2_b])
    r, r_b = S["rstd2"], S["rstd2_b"]
    P.op("act", lambda e: e.activation(out=r[:np_, :], in_=p2[:np_, :], func=AF.Sqrt, bias=C.eps[:np_, 0:1], scale=1.0 / np_),
         reads=[p2_b, C.ones_b], writes=[r_b])
    P.op("dve", lambda e: e.reciprocal(out=r[:np_, :], in_=r[:np_, :]), reads=[r_b], writes=[r_b])
    P.op("dve", lambda e: e.scalar_tensor_tensor(out=out_ap, in0=t[:np_, :], scalar=gain, in1=r[:np_, :], op0=ALU.mult, op1=ALU.mult),
         reads=[t_b, r_b, gain_b], writes=[out_b])


def setup_A(P, C, S, ntok, pre="", with_x_in=True, resident_kv=True):
    xT = P.dram_in(pre + "xT", [D, ntok]) if with_x_in else None
    g1 = P.dram_in(pre + "g1", [128, KC])
    wup = P.dram_in(pre + "wup", [D, 2 * DFF])
    wdn = P.dram_in(pre + "wdn", [DFF, D])
    gmix = P.dram_in(pre + "gmix", [128, KC])
    winA = P.dram_in(pre + "winA", [D, NA_COLS])
    winS = P.dram_in(pre + "winS", [D, 104])
    gq = P.dram_in(pre + "gq", [128, 1])
    gk = P.dram_in(pre + "gk", [128, 1])
    gkv = P.dram_in(pre + "gkv", [128, 4])
    gik = P.dram_in(pre + "gik", [64, 1])
    wuk = P.dram_in(pre + "wuk", [512, 1024])
    wuv = P.dram_in(pre + "wuv", [512, 1024])
    x1T = P.dram_out(pre + "x1T", [D, ntok])
    PT = P.dram_out(pre + "PT", [N_PT * 128, ntok], BF16)
    kiT = P.dram_out(pre + "kiT", [64, ntok], BF16)
    smT = P.dram_out(pre + "smT", [40, ntok])
    wup_bf = P.dram_tmp(pre + "wup_bf", [44, 128, KC, 256], BF16)
    wdn_bf = P.dram_tmp(pre + "wdn_bf", [16, 128, FC, 128], BF16)
    winA_bf = P.dram_tmp(pre + "winA_bf", [24, 128, KC, 256], BF16)
    alloc_A_extra(P, S, resident_kv)
    wup_b = cast_weight_tiled(P, wup, wup_bf, KC, 256, 44, "wup")
    wdn_b = cast_weight_tiled(P, wdn, wdn_bf, FC, 128, 16, "wdn")
    winA_b = cast_weight_tiled(P, winA, winA_bf, KC, 256, 24, "winA")
    if resident_kv:
        P.dma(S["wuk"][:, :, :], wuk.rearrange("(kc p) n -> p kc n", p=128), writes=[S["wukv_b"]], eng="pool")
        P.dma(S["wuv"][:, :, :], wuv.rearrange("(kc p) n -> p kc n", p=128), writes=[S["wukv_b"]], eng="pool")
    else:
        wukv_bf = P.dram_tmp(pre + "wukv_bf", [2, 128, 4, 1024], BF16)
        wukv_bf_b = Buf("wukv_bf")
        P.dma(wukv_bf[0], wuk.rearrange("(kc p) n -> p kc n", p=128), writes=[wukv_bf_b], eng="pool")
        P.dma(wukv_bf[1], wuv.rearrange("(kc p) n -> p kc n", p=128), writes=[wukv_bf_b], eng="pool")
    P.dma(S["wsm"][:, :, :], winS.rearrange("(kc p) n -> p kc n", p=128), writes=[S["wsm_b"]], eng="pool")
    g1_t, g1_b = load_vec(P, pre + "g1", g1, [128, KC])
    gm_t, gm_b = load_vec(P, pre + "gmix", gmix, [128, KC])
    gq_t, gq_b = load_vec(P, pre + "gq", gq, [128, 1])
    gk_t, gk_b = load_vec(P, pre + "gk", gk, [128, 1])
    gkv_t, gkv_b = load_vec(P, pre + "gkv", gkv, [128, 4])
    gik_t, gik_b = load_vec(P, pre + "gik", gik, [64, 1])
    P.op("dve", lambda e: e.tensor_scalar(out=gq_t[:, :], in0=gq_t[:, :], scalar1=float(128 ** -0.5), scalar2=None, op0=ALU.mult),
         reads=[gq_b], writes=[gq_b])

    return dict(locals())


def tile_A(P, C, S, A, t, ntok):
    g1_t, g1_b, gm_t, gm_b, gq_t, gq_b, gk_t, gk_b, gkv_t, gkv_b, gik_t, gik_b = (A[k] for k in ("g1_t", "g1_b", "gm_t", "gm_b", "gq_t", "gq_b", "gk_t", "gk_b", "gkv_t", "gkv_b", "gik_t", "gik_b"))
    wup_bf, wup_b, wdn_bf, wdn_b, winA_bf, winA_b, x1T, PT, kiT, smT = (A[k] for k in ("wup_bf", "wup_b", "wdn_bf", "wdn_b", "winA_bf", "winA_b", "x1T", "PT", "kiT", "smT"))
    xout_v = x1T.rearrange("(c p) n -> p c n", p=128)
    x_sb, x_b = S["x"], S["x_b"]
    h_sb, h_b = S["h"], S["h_b"]
    tok = slice(t * TT, (t + 1) * TT)
    ffn_tile(P, C, S, g1_t, g1_b, wup_bf, wup_b, wdn_bf, wdn_b)
    P.dma(xout_v[:, :, tok], x_sb[:, :, :], reads=[x_b])
    norm_to_h(P, C, S, gm_t, gm_b)

    def out_slot(slot, ob, ob_b):
        P.dma(PT[slot * 128:(slot + 1) * 128, tok], ob[:, :], reads=[ob_b])

    def proj_chunk(w_ap_fn, w_b, m=128):
        ps, ps_b = C.next_ps()
        for kc in range(KC):
            P.op("pe", lambda e, kc=kc: e.matmul(ps[:m, :], w_ap_fn(kc), h_sb[:, kc, :], start=(kc == 0), stop=(kc == KC - 1)),
                 reads=[w_b, h_b[kc]], writes=[ps_b])
        return ps, ps_b

    for blk in range(24):
        k = S["wup_rr"] % 2
        S["wup_rr"] += 1
        wa, w_b = S["wa"][k], S["wab_b"][k]
        P.dma(wa[:, :, :], winA_bf[blk], reads=[winA_b], writes=[w_b])
        for s in range(2):
            ch = blk * 2 + s
            ps, ps_b = proj_chunk(lambda kc, s=s, wa=wa: wa[:, kc, s * 128:(s + 1) * 128], w_b)
            if ch < 16 or ch >= 36:
                slot = ch if ch < 16 else ch + 12
                ob, ob_b = next_ob(S)
                if ch % 2 == 0:
                    P.op("act", lambda e, ob=ob, ps=ps: e.activation(out=ob[:, :], in_=ps[:, :], func=AF.Copy), reads=[ps_b], writes=[ob_b])
                else:
                    P.op("dve", lambda e, ob=ob, ps=ps: e.tensor_copy(out=ob[:, :], in_=ps[:, :]), reads=[ps_b], writes=[ob_b])
                out_slot(slot, ob, ob_b)
            elif ch < 24:
                ob, ob_b = next_ob(S)
                head_norm(P, C, S, ps, ps_b, 128, gq_t[:, 0:1], gq_b, ob[:, :], ob_b)
                out_slot(ch, ob, ob_b)
            elif ch < 28:
                c4 = ch - 24
                P.op("act", lambda e, c4=c4, ps=ps: e.activation(out=S["ckv"][:, c4, :], in_=ps[:, :], func=AF.Copy),
                     reads=[ps_b], writes=[S["ckv_b"]])
            else:
                ob, ob_b = next_ob(S)
                P.op("act", lambda e, ob=ob, ps=ps: e.activation(out=ob[:, :], in_=ps[:, :], func=AF.Copy, scale=0.125),
                     reads=[ps_b], writes=[ob_b])
                out_slot(ch + 12, ob, ob_b)
    ps, ps_b = proj_chunk(lambda kc: S["wsm"][:, kc, :], S["wsm_b"], m=104)
    P.op("act", lambda e, ps=ps: e.activation(out=S["osm"][:104, :], in_=ps[:104, :], func=AF.Copy), reads=[ps_b], writes=[S["osm_b"]])
    P.dma(smT[:, tok], S["osm"][64:104, :], reads=[S["osm_b"]])
    ob, ob_b = next_ob(S)
    head_norm(P, C, S, ps, ps_b, 64, gik_t[:, 0:1], gik_b, ob[:64, :], ob_b)
    P.dma(kiT[:, tok], ob[:64, :], reads=[ob_b])
    ckv, ckv_b = S["ckv"], S["ckv_b"]
    rms_rstd(P, C, ckv, ckv_b, S["rstd2"], S["rstd2_b"], S["sq"], S["sq_b"], 4, TT, 512)
    for c4 in range(4):
        P.op("dve", lambda e, c4=c4: e.scalar_tensor_tensor(out=S["ckvn"][:, c4, :], in0=ckv[:, c4, :], scalar=gkv_t[:, c4:c4 + 1],
                                                        in1=S["rstd2"][:, :], op0=ALU.mult, op1=ALU.mult),
             reads=[ckv_b, gkv_b, S["rstd2_b"]], writes=[S["ckvn_b"]])
    if "wuk" in S:
        wuk_t, wuv_t, wuk_tb, wuv_tb = S["wuk"], S["wuv"], S["wukv_b"], S["wukv_b"]
    else:
        vw = lambda w: w[:, :, :].rearrange("p a b -> p (a b)").rearrange("p (k n) -> p k n", n=1024)
        wuk_t, wuv_t, wuk_tb, wuv_tb = vw(S["wa"][0]), vw(S["wa"][1]), S["wab_b"][0], S["wab_b"][1]
        P.dma(wuk_t, A["wukv_bf"][0], reads=[A["wukv_bf_b"]], writes=[wuk_tb])
        P.dma(wuv_t, A["wukv_bf"][1], reads=[A["wukv_bf_b"]], writes=[wuv_tb])
    for hh in range(8):
        pk, pk_b = C.next_ps()
        for kc in range(4):
            P.op("pe", lambda e, kc=kc, hh=hh, pk=pk: e.matmul(pk[:, :], wuk_t[:, kc, hh * 128:(hh + 1) * 128], S["ckvn"][:, kc, :],
                                                             start=(kc == 0), stop=(kc == 3)),
                 reads=[wuk_tb, S["ckvn_b"]], writes=[pk_b])
        ob, ob_b = next_ob(S)
        head_norm(P, C, S, pk, pk_b, 128, gk_t[:, 0:1], gk_b, ob[:, :], ob_b)
        out_slot(24 + hh, ob, ob_b)
        pv, pv_b = C.next_ps()
        for kc in range(4):
            P.op("pe", lambda e, kc=kc, hh=hh, pv=pv: e.matmul(pv[:, :], wuv_t[:, kc, hh * 128:(hh + 1) * 128], S["ckvn"][:, kc, :],
                                                             start=(kc == 0), stop=(kc == 3)),
                 reads=[wuv_tb, S["ckvn_b"]], writes=[pv_b])
        ob, ob_b = next_ob(S)
        P.op("dve", lambda e, ob=ob, pv=pv: e.tensor_copy(out=ob[:, :], in_=pv[:, :]), reads=[pv_b], writes=[ob_b])
        out_slot(32 + hh, ob, ob_b)


def build_A(ntok):
    P = Prog()
    C = Ctx(P)
    S = alloc_shared(P)
    A = setup_A(P, C, S, ntok, pre="a_")
    xin_v = A["xT"].rearrange("(c p) n -> p c n", p=128)
    for t in range(ntok // TT):
        P.dma(S["x"][:, :, :], xin_v[:, :, t * TT:(t + 1) * TT], writes=[S["x_b"]])
        tile_A(P, C, S, A, t, ntok)
    return P


def perm_cols_A():
    import numpy as np
    sizes = (512, 512, 1024, 1024, 4, 4, 1024, 512, 1024, 64, 16, 1024, 1536, 16, 6144)
    off = np.concatenate([[0], np.cumsum(sizes)])
    names = ["ml_q", "ml_k", "ml_v", "ml_o", "ml_i", "ml_f", "d_q", "d_kv", "i_q", "i_k", "i_w", "s_z", "s_xbc", "s_dt", "gates"]
    r = {n: np.arange(off[i], off[i + 1]) for i, n in enumerate(names)}
    main = np.concatenate([r["ml_q"], r["ml_k"], r["ml_v"], r["d_q"], r["d_kv"], r["i_q"], r["s_xbc"]])
    small = np.concatenate([r["i_k"], r["ml_i"], r["ml_f"], r["i_w"], r["s_dt"]])
    later = np.concatenate([r["ml_o"], r["s_z"], r["gates"]])
    return main, small, later


EPS = 1e-6
SEG = 1024
NEGV = -30000.0


class CtxB:
    def __init__(self, P):
        self.P = P
        self.ps03 = P.psum("ps03", [128, 2048], F32)
        self.ps = [self.ps03[:, i * 512:(i + 1) * 512] for i in range(4)] + [P.psum(f"ps{i}", [128, 512], F32) for i in range(4, 7)]
        self.psb = [Buf(f"ps{i}") for i in range(7)]
        self.ps_rr = 0
        self.banks = (0, 1, 2, 4, 5, 6)
        self.pt = P.psum("pst", [128, 1024], BF16)
        self.ptb = [Buf(f"pst{i}") for i in range(8)]
        self.pt_rr = 0
        mk = lambda n, dt=F32: (P.sbuf(n, [128, 128], dt), Buf(n))
        self.ones, self.ones_b = mk("ones")
        self.tri, self.tri_b = mk("tri")
        self.negm, self.negm_b = mk("negm")
        self.ident, self.ident_b = mk("ident")
        self.identb, self.identb_b = mk("identb", BF16)
        self.zeros, self.zeros_b = mk("zeros")
        self.eps = P.sbuf("epsc", [128, 1], F32)
        self.eps_b = Buf("eps")
        P.op("dve", lambda e: e.memset(self.ones[:], 1.0), writes=[self.ones_b])
        P.op("dve", lambda e: e.memset(self.zeros[:], 0.0), writes=[self.zeros_b])
        P.op("dve", lambda e: e.memset(self.eps[:], EPS), writes=[self.eps_b])
        P.op("pool", lambda e: e.affine_select(out=self.tri[:], in_=self.ones[:], pattern=[[1, 128]], compare_op=ALU.is_ge,
                                               fill=0.0, base=0, channel_multiplier=-1),
             reads=[self.ones_b], writes=[self.tri_b])
        P.op("pool", lambda e: e.affine_select(out=self.negm[:], in_=self.zeros[:], pattern=[[1, 128]], compare_op=ALU.is_ge,
                                               fill=NEGV, base=0, channel_multiplier=-1),
             reads=[self.zeros_b], writes=[self.negm_b])
        P.op("pool", lambda e: e.affine_select(out=self.ident[:], in_=self.ones[:], pattern=[[1, 128]], compare_op=ALU.is_equal,
                                               fill=0.0, base=0, channel_multiplier=-1),
             reads=[self.ones_b], writes=[self.ident_b])
        P.op("dve", lambda e: e.tensor_copy(out=self.identb[:], in_=self.ident[:]), reads=[self.ident_b], writes=[self.identb_b])

    def next_ps(self):
        banks = self.banks
        i = banks[self.ps_rr % len(banks)]
        self.ps_rr += 1
        return self.ps[i], self.psb[i]

    def next_pt(self):
        i = self.pt_rr % 8
        self.pt_rr += 1
        return self.pt[:, i * 128:(i + 1) * 128], self.ptb[i]


def conv_silu(P, raw, raw_b, cw, cw_b, j0, acc, acc_b, width, n_part=128, bias=None):
    P.op("dve", lambda e: e.tensor_scalar(out=acc[:n_part, :width], in0=raw[:n_part, 0:width], scalar1=cw[:n_part, j0:j0 + 1], scalar2=None, op0=ALU.mult),
         reads=[raw_b, cw_b], writes=[acc_b])
    for j in range(1, 4):
        P.op("dve", lambda e, j=j: e.scalar_tensor_tensor(out=acc[:n_part, :width], in0=raw[:n_part, j:j + width], scalar=cw[:n_part, j0 + j:j0 + j + 1],
                                                       in1=acc[:n_part, :width], op0=ALU.mult, op1=ALU.add),
             reads=[raw_b, cw_b, acc_b], writes=[acc_b])


def local_cumsum_cols(P, C, src, src_b, dst, dst_b, ncol):
    ps, ps_b = C.next_ps()
    P.op("pe", lambda e: e.matmul(ps[:, :ncol], C.tri[:, :], src, start=True, stop=True), reads=[src_b, C.tri_b], writes=[ps_b])
    P.op("dve", lambda e: e.tensor_copy(out=dst, in_=ps[:, :ncol]), reads=[ps_b], writes=[dst_b])


def decay_exp_tile(P, C, lcol_ap, lcol_b, bias_ap, bias_b, ET, ET_b, lfb, lfb_b, sdec=None, sdec_b=None, pd=None):
    P.op("dve", lambda e: e.tensor_scalar(out=lfb[:, :], in0=C.ones[:, :], scalar1=lcol_ap, scalar2=None, op0=ALU.mult),
         reads=[C.ones_b, lcol_b], writes=[lfb_b])
    pD, pD_b = C.next_ps() if pd is None else pd
    P.op("pe", lambda e: e.matmul(pD[:, :128], lfb[:, :], C.tri[:, :], start=True, stop=False), reads=[lfb_b, C.tri_b], writes=[pD_b])
    P.op("pe", lambda e: e.matmul(pD[:, :128], C.ident[:, :], C.negm[:, :], start=False, stop=True), reads=[C.ident_b, C.negm_b], writes=[pD_b])
    P.op("act", lambda e: e.activation(out=ET[:, :], in_=pD[:, :128], func=AF.Exp, bias=bias_ap), reads=[pD_b, bias_b], writes=[ET_b])
    if sdec is not None:
        P.op("act", lambda e: e.activation(out=sdec, in_=pD[:, 127:128], func=AF.Exp), reads=[pD_b], writes=[sdec_b])


def mlstm_part(P, C, S_len, pre=""):
    nch = S_len // 128
    nseg = max(1, S_len // SEG)
    seg = min(SEG, S_len)
    cps = seg // 128
    qk = P.dram_in(pre + "ml_qk", [2, 128, S_len], BF16)
    v = P.dram_in(pre + "ml_v", [S_len, 256], BF16)
    gif = P.dram_in(pre + "ml_if", [128, 2, nch])
    cw_d = P.dram_in(pre + "ml_cw", [128, 8])
    bias_d = P.dram_in(pre + "ml_bias", [128, 2])
    gn_d = P.dram_in(pre + "ml_gn", [128, 256])
    out = P.dram_out(pre + "ml_h", [S_len, 256], BF16)

    def T(name, shape, dt=F32):
        return P.sbuf(pre + name, shape, dt), Buf(pre + name)

    cw, cw_b = T("cw", [128, 8])
    bias, bias_b = T("bias", [128, 2])
    gn, gn_b = T("gn", [128, 256])
    gi, gi_b = T("gif", [128, 2, nch])
    P.dma(cw[:], cw_d, writes=[cw_b])
    P.dma(bias[:], bias_d, writes=[bias_b])
    P.dma(gn[:], gn_d, writes=[gn_b])
    P.dma(gi[:], gif, writes=[gi_b])
    lf, lf_b = T("lf", [128, nch])
    Bc, Bc_b = T("Bc", [128, nch])
    bc, bc_b = T("biascol", [128, nch])
    dc, dc_b = T("decaycol", [128, nch])
    nfb, nfb_b = T("nfb", [128, 1])
    P.op("dve", lambda e: e.tensor_scalar(out=nfb[:, :], in0=bias[:, 1:2], scalar1=-1.0, scalar2=None, op0=ALU.mult), reads=[bias_b], writes=[nfb_b])
    P.op("act", lambda e: e.activation(out=lf[:, :], in_=gi[:, 1, :], func=AF.Exp, bias=nfb[:, 0:1], scale=-1.0), reads=[gi_b, nfb_b], writes=[lf_b])
    P.op("act", lambda e: e.activation(out=lf[:, :], in_=lf[:, :], func=AF.Ln, bias=C.ones[:, 0:1], scale=1.0), reads=[lf_b, C.ones_b], writes=[lf_b])
    P.op("dve", lambda e: e.tensor_scalar(out=lf[:, :], in0=lf[:, :], scalar1=-1.0, scalar2=None, op0=ALU.mult), reads=[lf_b], writes=[lf_b])
    local_cumsum_cols(P, C, lf[:, :], lf_b, Bc[:, :], Bc_b, nch)
    P.op("dve", lambda e: e.scalar_tensor_tensor(out=bc[:, :], in0=gi[:, 0, :], scalar=bias[:, 0:1], in1=Bc[:, :], op0=ALU.add, op1=ALU.subtract),
         reads=[gi_b, bias_b, Bc_b], writes=[bc_b])
    P.op("act", lambda e: e.activation(out=dc[:, :], in_=Bc[:, :], func=AF.Exp), reads=[Bc_b], writes=[dc_b])

    raw = [T(f"raw{i}", [128, 2, seg + 3], BF16) for i in range(1)]
    acc, acc_b = T("acc", [128, seg])
    qT, qT_b = T("qT", [128, seg], BF16)
    kT, kT_b = T("kT", [128, seg], BF16)
    vx = [T(f"vx{i}", [128, cps, 257], BF16) for i in range(2)]
    for i in range(2):
        P.op("pool", lambda e, i=i: e.memset(vx[i][0][:, :, 256:257], 1.0), writes=[vx[i][1]])
    CT, CT_b = T("CT", [128, 257])
    CTb, CTb_b = T("CTb", [128, 257], BF16)
    P.op("dve", lambda e: e.memset(CT[:], 0.0), writes=[CT_b])
    P.op("pool", lambda e: e.memset(CTb[:], 0.0), writes=[CTb_b])
    ET, ET_b = T("ET", [128, 128])
    lfb, lfb_b = T("lfb", [128, 128])
    PTm, PTm_b = T("PTm", [128, 128], BF16)
    kw, kw_b = T("kw", [128, 128], BF16)
    sdec, sdec_b = T("sdec", [128, 1])
    tmpi, tmpi_b = T("tmpi", [128, 257])
    nd, nd_b = T("nd", [128, 257])
    sc, sc_b = T("sc", [128, 4])
    junk, junk_b = T("junk", [128, 256])
    ho = [T(f"ho{i}", [128, 256], BF16) for i in range(2)]
    v_v = v.rearrange("(c p) d -> p c d", p=128)
    out_v = out.rearrange("(c p) d -> p c d", p=128)
    for sg in range(nseg):
        rw, rw_b = raw[0]
        vxt, vx_b = vx[sg % 2]
        t0 = sg * seg
        if sg == 0:
            P.op("pool", lambda e, rw=rw: e.memset(rw[:, :, 0:3], 0.0), writes=[rw_b])
            for j in range(2):
                P.dma(rw[:, j, 3:seg + 3], qk[j, :, 0:seg], writes=[rw_b])
        else:
            for j in range(2):
                P.dma(rw[:, j, :], qk[j, :, t0 - 3:t0 + seg], writes=[rw_b])
        P.dma(vxt[:, :, 0:256], v_v[:, sg * cps:(sg + 1) * cps, :], writes=[vx_b])
        conv_silu(P, rw[:, 0, :], rw_b, cw, cw_b, 0, acc, acc_b, seg)
        P.op("act", lambda e: e.activation(out=acc[:, :], in_=acc[:, :], func=AF.Silu), reads=[acc_b], writes=[acc_b])
        P.op("dve", lambda e: e.tensor_scalar(out=qT[:, :], in0=acc[:, :], scalar1=float(128 ** -0.5), scalar2=None, op0=ALU.mult),
             reads=[acc_b], writes=[qT_b])
        conv_silu(P, rw[:, 1, :], rw_b, cw, cw_b, 4, acc, acc_b, seg)
        P.op("act", lambda e: e.activation(out=kT[:, :], in_=acc[:, :], func=AF.Silu), reads=[acc_b], writes=[kT_b])
        for cc in range(cps):
            c = sg * cps + cc
            cs = slice(cc * 128, (cc + 1) * 128)
            decay_exp_tile(P, C, lf[:, c:c + 1], lf_b, bc[:, c:c + 1], bc_b, ET, ET_b, lfb, lfb_b, sdec[:, 0:1], sdec_b)
            pS, pS_b = C.next_ps()
            P.op("pe", lambda e, cs=cs, pS=pS: e.matmul(pS[:, :128], kT[:, cs], qT[:, cs], start=True, stop=True), reads=[kT_b, qT_b], writes=[pS_b])
            P.op("dve", lambda e, pS=pS: e.tensor_tensor(out=PTm[:, :], in0=pS[:, :128], in1=ET[:, :], op=ALU.mult), reads=[pS_b, ET_b], writes=[PTm_b])
            pI, pI_b = C.next_ps()
            P.op("pe", lambda e, cc=cc, pI=pI, vxt=vxt: e.matmul(pI[:, :257], PTm[:, :], vxt[:, cc, :], start=True, stop=True),
                 reads=[PTm_b, vx_b], writes=[pI_b])
            pN, pN_b = C.next_ps()
            P.op("pe", lambda e, cs=cs, pN=pN: e.matmul(pN[:, :257], qT[:, cs], CTb[:, :], start=True, stop=True), reads=[qT_b, CTb_b], writes=[pN_b])
            P.op("act", lambda e, c=c, pN=pN: e.activation(out=tmpi[:, :], in_=pN[:, :257], func=AF.Copy, scale=dc[:, c:c + 1]),
                 reads=[pN_b, dc_b], writes=[tmpi_b])
            P.op("dve", lambda e, pI=pI: e.tensor_tensor(out=nd[:, :], in0=tmpi[:, :], in1=pI[:, :257], op=ALU.add), reads=[tmpi_b, pI_b], writes=[nd_b])
            P.op("act", lambda e: e.activation(out=sc[:, 0:1], in_=nd[:, 256:257], func=AF.Abs), reads=[nd_b], writes=[sc_b])
            P.op("dve", lambda e: e.tensor_scalar(out=sc[:, 0:1], in0=sc[:, 0:1], scalar1=1.0, scalar2=None, op0=ALU.max), reads=[sc_b], writes=[sc_b])
            P.op("dve", lambda e: e.reciprocal(out=sc[:, 0:1], in_=sc[:, 0:1]), reads=[sc_b], writes=[sc_b])
            P.op("act", lambda e: e.activation(out=junk[:, :], in_=nd[:, 0:256], func=AF.Square, accum_out=sc[:, 1:2]), reads=[nd_b, sc_b], writes=[junk_b, sc_b])
            P.op("dve", lambda e: e.tensor_tensor(out=sc[:, 2:3], in0=sc[:, 0:1], in1=sc[:, 0:1], op=ALU.mult), reads=[sc_b], writes=[sc_b])
            P.op("dve", lambda e: e.tensor_tensor(out=sc[:, 2:3], in0=sc[:, 2:3], in1=sc[:, 1:2], op=ALU.mult), reads=[sc_b], writes=[sc_b])
            P.op("act", lambda e: e.activation(out=sc[:, 2:3], in_=sc[:, 2:3], func=AF.Sqrt, bias=C.eps[:, 0:1], scale=1.0 / 256), reads=[sc_b, C.eps_b], writes=[sc_b])
            P.op("dve", lambda e: e.reciprocal(out=sc[:, 2:3], in_=sc[:, 2:3]), reads=[sc_b], writes=[sc_b])
            P.op("dve", lambda e: e.tensor_tensor(out=sc[:, 3:4], in0=sc[:, 2:3], in1=sc[:, 0:1], op=ALU.mult), reads=[sc_b], writes=[sc_b])
            hot, ho_b = ho[c % 2]
            P.op("dve", lambda e, hot=hot: e.scalar_tensor_tensor(out=hot[:, :], in0=nd[:, 0:256], scalar=sc[:, 3:4], in1=gn[:, :], op0=ALU.mult, op1=ALU.mult),
                 reads=[nd_b, sc_b, gn_b], writes=[ho_b])
            P.dma(out_v[:, c, :], hot[:, :], reads=[ho_b])
            pT, pT_b = C.next_pt()
            P.op("pe", lambda e, cs=cs, pT=pT: e.transpose(pT, kT[:, cs], C.identb[:, :]), reads=[kT_b, C.identb_b], writes=[pT_b])
            P.op("dve", lambda e, pT=pT: e.tensor_scalar(out=kw[:, :], in0=pT, scalar1=ET[:, 127:128], scalar2=None, op0=ALU.mult),
                 reads=[pT_b, ET_b], writes=[kw_b])
            pU, pU_b = C.next_ps()
            P.op("pe", lambda e, cc=cc, pU=pU, vxt=vxt: e.matmul(pU[:, :257], kw[:, :], vxt[:, cc, :], start=True, stop=True), reads=[kw_b, vx_b], writes=[pU_b])
            P.op("dve", lambda e, pU=pU: e.scalar_tensor_tensor(out=CT[:, :], in0=CT[:, :], scalar=sdec[:, 0:1], in1=pU[:, :257], op0=ALU.mult, op1=ALU.add),
                 reads=[CT_b, sdec_b, pU_b], writes=[CT_b])
            P.op("act", lambda e: e.activation(out=CTb[:, :], in_=CT[:, :], func=AF.Copy), reads=[CT_b], writes=[CTb_b])
            yield


def ssd_part(P, C, S_len, pre=""):
    nch = S_len // 128
    nseg = max(1, S_len // SEG)
    seg = min(SEG, S_len)
    cps = seg // 128
    raw_d = P.dram_in(pre + "ss_raw", [4, 128, S_len], BF16)
    dt_d = P.dram_in(pre + "ss_dt", [128, 4, nch])
    cw_d = P.dram_in(pre + "ss_cw", [128, 4, 4])
    cb_d = P.dram_in(pre + "ss_cb", [128, 4])
    par_d = P.dram_in(pre + "ss_par", [128, 3, 4])
    out = P.dram_out(pre + "ss_y", [S_len, 256], BF16)

    def T(name, shape, dt=F32):
        return P.sbuf(pre + name, shape, dt), Buf(pre + name)

    cw, cw_b = T("scw", [128, 16])
    cb, cb_b = T("scb", [128, 4])
    par, par_b = T("spar", [128, 3, 4])
    dtr, dtr_b = T("sdt", [128, 4, nch])
    P.dma(cw[:], cw_d.rearrange("p a b -> p (a b)"), writes=[cw_b])
    P.dma(cb[:], cb_d, writes=[cb_b])
    P.dma(par[:], par_d, writes=[par_b])
    P.dma(dtr[:], dt_d, writes=[dtr_b])
    dA, dA_b = T("sdA", [128, 4, nch])
    sg_, sg_b = T("ssegc", [128, 4, nch])
    dc, dc_b = T("sdecay", [128, 4, nch])
    An, An_b = T("sAn", [128, 4])
    for h in range(4):
        P.op("act", lambda e, h=h: e.activation(out=dtr[:, h, :], in_=dtr[:, h, :], func=AF.Exp, bias=par[:, 0, h:h + 1]), reads=[dtr_b, par_b], writes=[dtr_b])
    P.op("act", lambda e: e.activation(out=dtr[:, :, :], in_=dtr[:, :, :], func=AF.Ln, bias=C.ones[:, 0:1], scale=1.0), reads=[dtr_b, C.ones_b], writes=[dtr_b])
    P.op("act", lambda e: e.activation(out=An[:, :], in_=par[:, 1, :], func=AF.Exp), reads=[par_b], writes=[An_b])
    P.op("dve", lambda e: e.tensor_scalar(out=An[:, :], in0=An[:, :], scalar1=-1.0, scalar2=None, op0=ALU.mult), reads=[An_b], writes=[An_b])
    for h in range(4):
        P.op("dve", lambda e, h=h: e.tensor_scalar(out=dA[:, h, :], in0=dtr[:, h, :], scalar1=An[:, h:h + 1], scalar2=None, op0=ALU.mult),
             reads=[dtr_b, An_b], writes=[dA_b])
    local_cumsum_cols(P, C, dA[:, :, :].rearrange("p a b -> p (a b)"), dA_b, sg_[:, :, :].rearrange("p a b -> p (a b)"), sg_b, 4 * nch)
    P.op("act", lambda e: e.activation(out=dc[:, :, :], in_=sg_[:, :, :], func=AF.Exp), reads=[sg_b], writes=[dc_b])
    P.op("dve", lambda e: e.tensor_scalar(out=sg_[:, :, :], in0=sg_[:, :, :], scalar1=-1.0, scalar2=None, op0=ALU.mult), reads=[sg_b], writes=[sg_b])

    raw = [T(f"sraw{i}", [128, 4, seg + 3], BF16) for i in range(1)]
    acc, acc_b = T("sacc", [128, seg])
    fT = [T(f"sfT{i}", [128, seg], BF16) for i in range(4)]
    ST, ST_b = T("sST", [128, 256])
    STb, STb_b = T("sSTb", [128, 256], BF16)
    P.op("dve", lambda e: e.memset(ST[:], 0.0), writes=[ST_b])
    P.op("pool", lambda e: e.memset(STb[:], 0.0), writes=[STb_b])
    ETs = [T(f"sET{i}", [128, 128]) for i in range(4)]
    sdecs = [T(f"ssdec{i}", [128, 1]) for i in range(4)]
    lfb, lfb_b = T("slfb", [128, 128])
    PTm = [T(f"sPT{i}", [128, 128], BF16) for i in range(2)]
    xtm, xtm_b = T("sxtm", [128, 256], BF16)
    xdt, xdt_b = T("sxdt", [128, 256], BF16)
    xw, xw_b = T("sxw", [128, 256], BF16)
    btm, btm_b = T("sbtm", [128, 128], BF16)
    wcol, wcol_b = T("swcol", [128, 4])
    tmpi, tmpi_b = T("stmpi", [128, 256])
    y1, y1_b = T("sy1", [128, 256])
    yo = [T(f"syo{i}", [128, 256], BF16) for i in range(2)]
    out_v = out.rearrange("(c p) d -> p c d", p=128)
    for sgi in range(nseg):
        rw, rw_b = raw[0]
        t0 = sgi * seg
        if sgi == 0:
            P.op("pool", lambda e, rw=rw: e.memset(rw[:, :, 0:3], 0.0), writes=[rw_b])
            for j in range(4):
                P.dma(rw[:, j, 3:seg + 3], raw_d[j, :, 0:seg], writes=[rw_b])
        else:
            for j in range(4):
                P.dma(rw[:, j, :], raw_d[j, :, t0 - 3:t0 + seg], writes=[rw_b])
        for j in range(4):
            conv_silu(P, rw[:, j, :], rw_b, cw, cw_b, 4 * j, acc, acc_b, seg)
            P.op("act", lambda e, j=j: e.activation(out=fT[j][0][:, :], in_=acc[:, :], func=AF.Silu, bias=cb[:, j:j + 1]), reads=[acc_b, cb_b], writes=[fT[j][1]])
        for cc in range(cps):
            c = sgi * cps + cc
            cs = slice(cc * 128, (cc + 1) * 128)
            for j in range(2):
                pT, pT_b = C.next_pt()
                P.op("pe", lambda e, j=j, pT=pT, c=c, cs=cs: e.transpose(pT, fT[j][0][:, cs], C.identb[:, :]), reads=[fT[j][1], C.identb_b], writes=[pT_b])
                P.op("act", lambda e, j=j, pT=pT, c=c, cs=cs: e.activation(out=xtm[:, j * 128:(j + 1) * 128], in_=pT, func=AF.Copy), reads=[pT_b], writes=[xtm_b])
            pT, pT_b = C.next_pt()
            P.op("pe", lambda e, pT=pT, c=c, cs=cs: e.transpose(pT, fT[2][0][:, cs], C.identb[:, :]), reads=[fT[2][1], C.identb_b], writes=[pT_b])
            P.op("act", lambda e, pT=pT, c=c, cs=cs: e.activation(out=btm[:, :], in_=pT, func=AF.Copy), reads=[pT_b], writes=[btm_b])
            for h in range(4):
                hs = slice(h * 64, (h + 1) * 64)
                P.op("dve", lambda e, h=h, hs=hs, c=c, cs=cs: e.tensor_scalar(out=xdt[:, hs], in0=xtm[:, hs], scalar1=dtr[:, h, c:c + 1], scalar2=None, op0=ALU.mult),
                     reads=[xtm_b, dtr_b], writes=[xdt_b])
            bankA, bankA_b = C.next_ps()
            bankD, bankD_b = C.next_ps()
            bankI, bankI_b = C.next_ps()
            pSc, pSc_b = bankA[:, 0:128], bankA_b
            P.op("pe", lambda e, pSc=pSc, c=c, cs=cs: e.matmul(pSc[:, :128], fT[2][0][:, cs], fT[3][0][:, cs], start=True, stop=True), reads=[fT[2][1], fT[3][1]], writes=[pSc_b])
            pY, pY_b = bankA[:, 128:384], bankA_b
            for h in range(4):
                hs = slice(h * 64, (h + 1) * 64)
                ET, ET_b = ETs[h]
                sd, sd_b = sdecs[h]
                decay_exp_tile(P, C, dA[:, h, c:c + 1], dA_b, sg_[:, h, c:c + 1], sg_b, ET, ET_b, lfb, lfb_b, sd[:, 0:1], sd_b, pd=(bankD[:, h * 128:(h + 1) * 128], bankD_b))
                pm, pm_b = PTm[h % 2]
                P.op("dve", lambda e, pm=pm, ET=ET, pSc=pSc, c=c, cs=cs: e.tensor_tensor(out=pm[:, :], in0=pSc[:, :128], in1=ET[:, :], op=ALU.mult), reads=[pSc_b, ET_b], writes=[pm_b])
                P.op("pe", lambda e, pm=pm, hs=hs, pY=pY, c=c, cs=cs: e.matmul(pY[:, hs], pm[:, :], xdt[:, hs], start=True, stop=True), reads=[pm_b, xdt_b], writes=[pY_b])
                P.op("dve", lambda e, h=h, ET=ET, c=c, cs=cs: e.tensor_tensor(out=wcol[:, h:h + 1], in0=ET[:, 127:128], in1=dtr[:, h, c:c + 1], op=ALU.mult),
                     reads=[ET_b, dtr_b], writes=[wcol_b])
            pYi, pYi_b = bankI[:, 0:256], bankI_b
            P.op("pe", lambda e, pYi=pYi, c=c, cs=cs: e.matmul(pYi[:, :256], fT[3][0][:, cs], STb[:, :], start=True, stop=True), reads=[fT[3][1], STb_b], writes=[pYi_b])
            for h in range(4):
                hs = slice(h * 64, (h + 1) * 64)
                P.op("act", lambda e, h=h, hs=hs, pYi=pYi, c=c, cs=cs: e.activation(out=tmpi[:, hs], in_=pYi[:, hs], func=AF.Copy, scale=dc[:, h, c:c + 1]),
                     reads=[pYi_b, dc_b], writes=[tmpi_b])
            P.op("dve", lambda e, pY=pY, c=c, cs=cs: e.tensor_tensor(out=y1[:, :], in0=tmpi[:, :], in1=pY[:, :256], op=ALU.add), reads=[tmpi_b, pY_b], writes=[y1_b])
            yot, yo_b = yo[c % 2]
            for h in range(4):
                hs = slice(h * 64, (h + 1) * 64)
                P.op("dve", lambda e, h=h, hs=hs, yot=yot, c=c, cs=cs: e.scalar_tensor_tensor(out=yot[:, hs], in0=xtm[:, hs], scalar=par[:, 2, h:h + 1], in1=y1[:, hs],
                                                                             op0=ALU.mult, op1=ALU.add),
                     reads=[xtm_b, par_b, y1_b], writes=[yo_b])
            P.dma(out_v[:, c, :], yot[:, :], reads=[yo_b])
            for h in range(4):
                hs = slice(h * 64, (h + 1) * 64)
                P.op("dve", lambda e, h=h, hs=hs, c=c, cs=cs: e.tensor_scalar(out=xw[:, hs], in0=xtm[:, hs], scalar1=wcol[:, h:h + 1], scalar2=None, op0=ALU.mult),
                     reads=[xtm_b, wcol_b], writes=[xw_b])
            pU, pU_b = bankI[:, 256:512], bankI_b
            P.op("pe", lambda e, pU=pU, c=c, cs=cs: e.matmul(pU[:, :256], btm[:, :], xw[:, :], start=True, stop=True), reads=[btm_b, xw_b], writes=[pU_b])
            for h in range(4):
                hs = slice(h * 64, (h + 1) * 64)
                P.op("dve", lambda e, h=h, hs=hs, pU=pU, c=c, cs=cs: e.scalar_tensor_tensor(out=ST[:, hs], in0=ST[:, hs], scalar=sdecs[h][0][:, 0:1], in1=pU[:, hs],
                                                                           op0=ALU.mult, op1=ALU.add),
                     reads=[ST_b, sdecs[h][1], pU_b], writes=[ST_b])
            P.op("act", lambda e, c=c, cs=cs: e.activation(out=STb[:, :], in_=ST[:, :], func=AF.Copy), reads=[ST_b], writes=[STb_b])
            yield


def dsa_part(P, C, S_len, nqb, blk_sadm, pre="", debug=False):
    NIT = 25
    LO0, W0 = -128.0, 256.0
    dq_d = P.dram_in(pre + "d_q", [nqb, 128, 8, 128], BF16)
    diq_d = P.dram_in(pre + "d_iq", [nqb, 128, 8, 128], BF16)
    dw_d = P.dram_in(pre + "d_w", [128, nqb, 16])
    dlim_d = P.dram_in(pre + "d_lim", [128, nqb])
    kT_d = P.dram_in(pre + "d_kT", [128, 8, S_len], BF16)
    v_d = P.dram_in(pre + "d_v", [S_len, 1024], BF16)
    ki_d = P.dram_in(pre + "d_ki2", [128, S_len], BF16)
    out = P.dram_out(pre + "d_y", [nqb, 128, 1024], BF16)
    if debug:
        dbg_score = P.dram_out(pre + "dbg_score", [nqb, 128, max(blk_sadm)])
        dbg_lo = P.dram_out(pre + "dbg_lo", [nqb, 2, 128, 1])

    def T(name, shape, dt=F32):
        return P.sbuf(pre + name, shape, dt), Buf(pre + name)

    smax = max(blk_sadm)
    score, score_b = T("dscore", [128, smax])
    JW = min(2048, smax)
    junk, junk_b = T("djunk", [128, JW], BF16)
    kidx, kidx_b = T("dkidx", [128, 512])
    P.op("pool", lambda e: e.iota(kidx[:, :], pattern=[[1, 512]], base=0, channel_multiplier=0, allow_small_or_imprecise_dtypes=True), writes=[kidx_b])
    dw, dw_b = T("ddw", [128, nqb, 16])
    aw, aw_b = T("daw", [128, nqb, 16])
    sw, sw_b = T("dsw", [128, nqb, 16])
    lim, lim_b = T("dlim", [128, nqb])
    P.dma(dw[:], dw_d, writes=[dw_b])
    P.dma(lim[:], dlim_d, writes=[lim_b])
    P.op("act", lambda e: e.activation(out=aw[:, :, :], in_=dw[:, :, :], func=AF.Abs, scale=0.25), reads=[dw_b], writes=[aw_b])
    P.op("act", lambda e: e.activation(out=sw[:, :, :], in_=dw[:, :, :], func=AF.Sign), reads=[dw_b], writes=[sw_b])
    qb = [T(f"dqb{i}", [128, 8, 128], BF16) for i in range(2)]
    iqb = [T(f"diqb{i}", [128, 8, 128], BF16) for i in range(2)]
    kib = [T(f"dkib{i}", [128, 1024], BF16) for i in range(2)]
    rt = [T(f"drt{i}", [128, 1024]) for i in range(2)]
    KG = 256
    kbuf = [T(f"dkbuf{i}", [128, 8, KG], BF16) for i in range(2)]
    vbuf = [T(f"dvbuf{i}", [128, KG // 128, 8, 129], BF16) for i in range(2)]
    for i in range(2):
        P.op("pool", (lambda vb: (lambda e: e.memset(vb[:, :, :, 128:129], 1.0)))(vbuf[i][0]), writes=[vbuf[i][1]])
    ngm = [T(f"dngm{i}", [128, KG], BF16) for i in range(2)]
    ngT = [T(f"dngT{i}", [128, KG // 128, 128], BF16) for i in range(2)]
    pex = [T(f"dpex{i}", [128, 128], BF16) for i in range(4)]
    lo, lo_b = T("dlo", [128, 1])
    th, th_b = T("dth", [128, 1])
    cnt, cnt_b = T("dcnt", [128, 1])
    ge, ge_b = T("dge", [128, 1])
    nth, nth_b = T("dnth", [128, 1])
    sgs, sgs_b = T("dsgs", [128, 8])
    sg1, sg1_b = T("dsg1", [128, 1])
    junk2, junk2_b = T("djunk2", [128, JW], BF16)
    DVE_FRAC = 0.5
    pxw = [T(f"dpxw{i}", [128, 512], BF16) for i in range(3)]
    limk, limk_b = T("dlimk", [128, 1])
    rden, rden_b = T("drden", [128, 8])
    yo = [T(f"dyo{i}", [128, 1024], BF16) for i in range(2)]
    negc, negc_b = T("dnegc", [128, 1])
    P.op("dve", lambda e: e.memset(negc[:], -12.0), writes=[negc_b])
    acc_banks = [(C.ps[4], C.psb[4]), (C.ps[5], C.psb[5]), (C.ps[6], C.psb[6])]
    v_v = v_d.rearrange("s (h d) -> s h d", h=8)
    rr = {"ps": 0, "q": 0, "pex": 0, "kv": 0, "rt": 0, "ng": 0}

    qslots = [Buf(f"dqslot{i}") for i in range(4)]

    def rot_ps():
        i = rr["ps"] % 2
        rr["ps"] += 1
        return C.ps[i], C.psb[i]

    def rot_q():
        i = rr["q"] % 4
        rr["q"] += 1
        return C.ps[3][:, i * 128:(i + 1) * 128], qslots[i]

    def index_tile(b, kt0, kib_t, kib_bf, off, iq_t, iq_bf, width):
        for hh in range(16):
            j, half = hh // 2, hh % 2
            prt = slice(half * 64, half * 64 + 64)
            i2 = rr["ps"] % 2
            rr["ps"] += 1
            pr = C.ps03[:, i2 * 1024:i2 * 1024 + width]
            pr_bl = [C.psb[2 * i2], C.psb[2 * i2 + 1]]
            for sub in range(0, width, 512):
                P.op("pe", lambda e, pr=pr, prt=prt, j=j, sub=sub: e.matmul(pr[:, sub:sub + 512], iq_t[prt, j, :], kib_t[prt, off + sub:off + sub + 512], start=True, stop=True),
                     reads=[iq_bf, kib_bf], writes=pr_bl)
            r, r_b = rt[rr["rt"] % 2]
            rr["rt"] += 1
            P.op("act", lambda e, r=r, pr=pr, hh=hh: e.activation(out=r[:, :width], in_=pr[:, :], func=AF.Relu, scale=aw[:, b, hh:hh + 1]), reads=pr_bl + [aw_b], writes=[r_b])
            P.op("dve", lambda e, r=r, hh=hh: e.scalar_tensor_tensor(out=score[:, kt0:kt0 + width], in0=r[:, :width], scalar=sw[:, b, hh:hh + 1], in1=score[:, kt0:kt0 + width],
                                                                op0=ALU.mult, op1=ALU.add),
                 reads=[r_b, sw_b, score_b], writes=[score_b])

    def pen_tile(b, kt0):
        P.op("dve", lambda e: e.tensor_scalar(out=limk[:, :], in0=lim[:, b:b + 1], scalar1=float(-kt0), scalar2=None, op0=ALU.add), reads=[lim_b], writes=[limk_b])
        P.op("dve", lambda e: e.tensor_scalar(out=score[:, kt0:kt0 + 512], in0=kidx[:, :], scalar1=limk[:, 0:1], scalar2=-1.0e4, op0=ALU.is_ge, op1=ALU.mult),
             reads=[kidx_b, limk_b], writes=[score_b])

    def count_piece(a, bnd, first):
        if first:
            P.op("dve", lambda e: e.tensor_scalar(out=junk[:, :bnd - a], in0=score[:, a:bnd], scalar1=th[:, 0:1], scalar2=None, op0=ALU.is_ge, op1=ALU.add,
                                                  accum_out=cnt[:, 0:1]),
                 reads=[score_b, th_b], writes=[junk_b, cnt_b])
        else:
            P.op("dve", lambda e: e.tensor_scalar(out=junk[:, :bnd - a], in0=score[:, a:bnd], scalar1=th[:, 0:1], scalar2=cnt[:, 0:1], op0=ALU.is_ge, op1=ALU.add,
                                                  accum_out=cnt[:, 0:1]),
                 reads=[score_b, th_b, cnt_b], writes=[junk_b, cnt_b])

    def count_piece_act(a, bnd, col):
        P.op("act", lambda e: e.activation(out=junk2[:, :bnd - a], in_=score[:, a:bnd], func=AF.Sign, bias=th[:, 0:1], scale=-1.0, accum_out=sgs[:, col:col + 1]),
             reads=[score_b, th_b], writes=[junk2_b, sgs_b])

    def bisect_step(sadm, half):
        P.op("dve", lambda e: e.tensor_scalar(out=th[:, :], in0=lo[:, :], scalar1=float(half), scalar2=None, op0=ALU.add), reads=[lo_b], writes=[th_b])
        nd = min(sadm, max(512, int(sadm * DVE_FRAC) // 512 * 512))
        pieces = [(a, min(nd, a + JW)) for a in range(0, nd, JW)]
        for i, (a, bnd) in enumerate(pieces):
            count_piece(a, bnd, i == 0)
        apieces = [(a, min(sadm, a + JW)) for a in range(nd, sadm, JW)]
        for i, (a, bnd) in enumerate(apieces):
            count_piece_act(a, bnd, i)
        if apieces:
            na = len(apieces)
            if na > 1:
                P.op("dve", lambda e: e.tensor_reduce(out=sg1[:, 0:1], in_=sgs[:, 0:na], axis=AX.X, op=ALU.add), reads=[sgs_b], writes=[sg1_b])
                sgx, sgx_b = sg1, sg1_b
            else:
                sgx, sgx_b = sgs, sgs_b
            P.op("dve", lambda e: e.scalar_tensor_tensor(out=cnt[:, :], in0=sgx[:, 0:1], scalar=-0.5, in1=cnt[:, :], op0=ALU.mult, op1=ALU.add),
                 reads=[sgx_b, cnt_b], writes=[cnt_b])
            P.op("dve", lambda e: e.tensor_scalar(out=ge[:, :], in0=cnt[:, :], scalar1=256.0 - 0.5 * (sadm - nd), scalar2=float(half), op0=ALU.is_ge, op1=ALU.mult),
                 reads=[cnt_b], writes=[ge_b])
        else:
            P.op("dve", lambda e: e.tensor_scalar(out=ge[:, :], in0=cnt[:, :], scalar1=256.0, scalar2=float(half), op0=ALU.is_ge, op1=ALU.mult), reads=[cnt_b], writes=[ge_b])
        P.op("dve", lambda e: e.tensor_tensor(out=lo[:, :], in0=lo[:, :], in1=ge[:, :], op=ALU.add), reads=[lo_b, ge_b], writes=[lo_b])

    def bisect(sadm):
        P.op("dve", lambda e: e.memset(lo[:], LO0), writes=[lo_b])
        w = W0
        for it in range(NIT):
            bisect_step(sadm, w / 2)
            w = w / 2

    def attn_hgroup_tile(t, hg, kb, kb_b, vb, vb_b, nT, nT_b, q_t, q_bf, start, stop):
        i = rr["q"] % 2
        rr["q"] += 1
        pL, pL_b = C.ps[2 + i], C.psb[2 + i]
        P.op("pe", lambda e: e.matmul(pL[:, :].rearrange("p (h q) -> p h q", h=4), C.identb[:, :], nT[:, t, :].unsqueeze(1).broadcast_to([128, 4, 128]),
                                      start=True, stop=False, skip_group_check=True),
             reads=[C.identb_b, nT_b], writes=[pL_b])
        for h4 in range(4):
            h = hg * 4 + h4
            P.op("pe", lambda e, h=h, h4=h4: e.matmul(pL[:, h4 * 128:(h4 + 1) * 128], kb[:, h, t * 128:(t + 1) * 128], q_t[:, h, :], start=False, stop=(h4 == 3),
                                                  skip_group_check=True),
                 reads=[kb_b, q_bf], writes=[pL_b])
        px, px_b = pxw[rr["pex"] % 3]
        rr["pex"] += 1
        P.op("act", lambda e: e.activation(out=px[:, :], in_=pL[:, :], func=AF.Exp, bias=negc[:, 0:1]), reads=[pL_b, negc_b], writes=[px_b])
        for h4 in range(4):
            h = hg * 4 + h4
            ab, ab_b = acc_banks[h // 3]
            col = (h % 3) * 129
            P.op("pe", lambda e, h=h, h4=h4, ab=ab, col=col: e.matmul(ab[:, col:col + 129], px[:, h4 * 128:(h4 + 1) * 128], vb[:, t, h, :],
                                                                  start=(start and h % 3 == 0), stop=stop, skip_group_check=True),
                 reads=[px_b, vb_b], writes=[ab_b])

    def transpose_ng(t, ng, ng_b, nT, nT_b):
        pT, pT_b = C.next_pt()
        P.op("pe", lambda e: e.transpose(pT, ng[:, t * 128:(t + 1) * 128], C.identb[:, :]), reads=[ng_b, C.identb_b], writes=[pT_b])
        P.op("act", lambda e: e.activation(out=nT[:, t, :], in_=pT, func=AF.Copy), reads=[pT_b], writes=[nT_b])

    def attn_group(b, g0, first, last, q_t, q_bf):
        k = rr["kv"] % 2
        rr["kv"] += 1
        kb, kb_b = kbuf[k]
        vb, vb_b = vbuf[k]
        ng, ng_b = ngm[k]
        nT, nT_b = ngT[k]
        P.dma(kb[:, :, :], kT_d[:, :, g0:g0 + KG], writes=[kb_b])
        for t in range(KG // 128):
            P.dma(vb[:, t, :, 0:128], v_v[g0 + t * 128:g0 + (t + 1) * 128], writes=[vb_b])
        P.op("dve", lambda e: e.tensor_scalar(out=ng[:, :], in0=score[:, g0:g0 + KG], scalar1=lo[:, 0:1], scalar2=NEGV, op0=ALU.is_lt, op1=ALU.mult),
             reads=[score_b, lo_b], writes=[ng_b])
        nt = KG // 128
        for t in range(nt):
            transpose_ng(t, ng, ng_b, nT, nT_b)
        for t in range(nt):
            for hg in range(2):
                attn_hgroup_tile(t, hg, kb, kb_b, vb, vb_b, nT, nT_b, q_t, q_bf, first and t == 0, last and t == nt - 1)

    def finalize_head(h, yot, yo_b):
        ab, ab_b = acc_banks[h // 3]
        col = (h % 3) * 129
        P.op("dve", lambda e: e.reciprocal(out=rden[:, h:h + 1], in_=ab[:, col + 128:col + 129]), reads=[ab_b], writes=[rden_b])
        P.op("act", lambda e: e.activation(out=yot[:, h * 128:(h + 1) * 128], in_=ab[:, col:col + 128], func=AF.Copy, scale=rden[:, h:h + 1]),
             reads=[ab_b, rden_b], writes=[yo_b])

    def do_block(b):
        sadm = blk_sadm[b]
        q_t, q_bf = qb[b % 2]
        iq_t, iq_bf = iqb[b % 2]
        P.dma(q_t[:, :, :], dq_d[b], writes=[q_bf])
        P.dma(iq_t[:, :, :], diq_d[b], writes=[iq_bf])
        for kt0 in range(0, sadm, 512):
            pen_tile(b, kt0)
        for k0 in range(0, sadm, 1024):
            kw_ = min(1024, sadm - k0)
            kib_t, kib_bf = kib[(k0 // 1024) % 2]
            P.dma(kib_t[:, :kw_], ki_d[:, k0:k0 + kw_], writes=[kib_bf])
            index_tile(b, k0, kib_t, kib_bf, 0, iq_t, iq_bf, kw_)
            yield
        P.op("dve", lambda e: e.memset(lo[:], LO0), writes=[lo_b])
        w_ = W0
        for it in range(NIT):
            bisect_step(sadm, w_ / 2)
            w_ = w_ / 2
            yield
        if debug:
            P.dma(dbg_score[b, :, :sadm], score[:, :sadm], reads=[score_b])
            P.dma(dbg_lo[b, 0], lo[:, :], reads=[lo_b])
            P.dma(dbg_lo[b, 1], cnt[:, :], reads=[cnt_b])
        ng_ = sadm // KG
        for gi in range(ng_):
            attn_group(b, gi * KG, gi == 0, gi == ng_ - 1, q_t, q_bf)
            yield
        yot, yo_b = yo[b % 2]
        for h in range(8):
            finalize_head(h, yot, yo_b)
        P.dma(out[b], yot[:, :], reads=[yo_b])

    for b in range(nqb):
        yield from do_block(b)


def build_B(S_len, nqb, blk_sadm, interleave=False):
    P = Prog()
    C = CtxB(P)
    gm = mlstm_part(P, C, S_len)
    gs = ssd_part(P, C, S_len)
    gd = dsa_part(P, C, S_len, nqb, blk_sadm)
    if not interleave:
        for g in (gm, gs, gd):
            for _ in g:
                pass
        return P
    C.banks = (0, 1, 2, 3)
    n_units = sum(s_ // 512 + 27 + s_ // 256 for s_ in blk_sadm)
    n_steps = 2 * (S_len // 128)
    every = max(1, n_units // (n_steps + 1))
    b1 = [gm, gs]
    k = 0
    u = 0
    for _ in gd:
        u += 1
        if u % every == 0 and b1:
            g = b1[k % len(b1)]
            k += 1
            try:
                next(g)
            except StopIteration:
                b1.remove(g)
    for g in b1:
        for _ in g:
            pass
    return P


NMEM = 256


def setup_C(P, C, S, ntok, pre="", with_out=True):
    x1T = P.dram_in(pre + "x1T", [D, ntok])
    hmT = P.dram_in(pre + "hmT", [1024, ntok], BF16)
    ybT = P.dram_in(pre + "ybT", [1024, ntok], BF16)
    ysT = P.dram_in(pre + "ysT", [1024, ntok], BF16)
    gmix = P.dram_in(pre + "gmix", [128, KC])
    gssm = P.dram_in(pre + "gssm", [128, 8])
    winC = P.dram_in(pre + "winC", [D, 8192])
    wbr = P.dram_in(pre + "wbr", [3, 1024, D])
    wout = P.dram_in(pre + "wout", [D, D])
    gxa = P.dram_in(pre + "gxa", [128, KC])
    gxm = P.dram_in(pre + "gxm", [128, KC])
    gxq = P.dram_in(pre + "gxq", [128, 1])
    gxk = P.dram_in(pre + "gxk", [128, 1])
    wq = P.dram_in(pre + "wq", [D, 512])
    wkv = P.dram_in(pre + "wkv", [D, 1024])
    wo = P.dram_in(pre + "wo", [512, D])
    memT = P.dram_in(pre + "memT", [D, NMEM])
    g2 = P.dram_in(pre + "g2", [128, KC])
    wup = P.dram_in(pre + "wup", [D, 2 * DFF])
    wdn = P.dram_in(pre + "wdn", [DFF, D])
    xoT = P.dram_out(pre + "xoT", [D, ntok]) if with_out else None
    wup_bf = P.dram_tmp(pre + "wup_bf", [44, 128, KC, 256], BF16)
    wdn_bf = P.dram_tmp(pre + "wdn_bf", [16, 128, FC, 128], BF16)
    winC_bf = P.dram_tmp(pre + "winC_bf", [32, 128, KC, 256], BF16)
    wbr_bf = [P.dram_tmp(pre + f"wbr_bf{b}", [8, 128, 8, 256], BF16) for b in range(3)]
    wout_bf = P.dram_tmp(pre + "wout_bf", [8, 128, KC, 256], BF16)
    wq_bf = P.dram_tmp(pre + "wq_bf", [2, 128, KC, 256], BF16)
    wk_bf = P.dram_tmp(pre + "wk_bf", [2, 128, KC, 256], BF16)
    wo_bf = P.dram_tmp(pre + "wo_bf", [8, 128, 4, 256], BF16)
    S["tmp"] = [P.sbuf(f"tmpf{i}", [128, TT], F32) for i in range(2)]
    S["tmp_b"] = [Buf(f"tmpf{i}") for i in range(2)]
    S["tmp_rr"] = 0
    S["rstd2"] = P.sbuf("rstd2", [128, TT], F32)
    S["rstd2_b"] = Buf("rstd2")
    macc = [P.sbuf(f"macc{i}", [128, TT], F32) for i in range(2)]
    macc_b = [Buf(f"macc{i}") for i in range(2)]
    gsb = [P.sbuf(f"gsb{i}", [128, TT], F32) for i in range(2)]
    gsb_b = [Buf(f"gsb{i}") for i in range(2)]
    ytmp = P.sbuf("ytmp", [128, TT], BF16)
    ytmp_b = Buf("ytmp")
    kx = P.sbuf("kx", [128, 4, NMEM], BF16)
    kx_b = Buf("kx")
    vx = P.sbuf("vx", [128, 2, 512], BF16)
    vx_b = Buf("vx")
    mn = P.sbuf("mn", [128, KC, NMEM], BF16)
    mn_b = Buf("mn")
    onesb = P.sbuf("onesb", [128, 128], BF16)
    onesb_b = Buf("onesb")
    negc = P.sbuf("negc", [128, 1], F32)
    negc_b = Buf("negc")
    P.op("dve", lambda e: e.memset(onesb[:], 1.0), writes=[onesb_b])
    P.op("dve", lambda e: e.memset(negc[:], -12.0), writes=[negc_b])

    wup_b = cast_weight_tiled(P, wup, wup_bf, KC, 256, 44, "wup")
    wdn_b = cast_weight_tiled(P, wdn, wdn_bf, FC, 128, 16, "wdn")
    winC_b = cast_weight_tiled(P, winC, winC_bf, KC, 256, 32, "winC")
    wbr_b = [cast_weight_tiled(P, wbr[b], wbr_bf[b], 8, 256, 8, f"wbr{b}") for b in range(3)]
    wout_b = cast_weight_tiled(P, wout, wout_bf, KC, 256, 8, "wout")
    wq_b = cast_weight_tiled(P, wq, wq_bf, KC, 256, 2, "wq")
    wk_b = cast_weight_tiled(P, wkv[:, 0:512], wk_bf, KC, 256, 2, "wk")
    wo_b = cast_weight_tiled(P, wo, wo_bf, 4, 256, 8, "wo")
    gm_t, gm_b = load_vec(P, pre + "gmix", gmix, [128, KC])
    gs_t, gs_b = load_vec(P, pre + "gssm", gssm, [128, 8])
    gxa_t, gxa_b = load_vec(P, pre + "gxa", gxa, [128, KC])
    gxm_t, gxm_b = load_vec(P, pre + "gxm", gxm, [128, KC])
    gxq_t, gxq_b = load_vec(P, pre + "gxq", gxq, [128, 1])
    gxk_t, gxk_b = load_vec(P, pre + "gxk", gxk, [128, 1])
    g2_t, g2_b = load_vec(P, pre + "g2", g2, [128, KC])
    P.op("dve", lambda e: e.tensor_scalar(out=gxq_t[:, :], in0=gxq_t[:, :], scalar1=float(128 ** -0.5), scalar2=None, op0=ALU.mult),
         reads=[gxq_b], writes=[gxq_b])

    x_sb, x_b = S["x"], S["x_b"]
    h_sb, h_b = S["h"], S["h_b"]
    u_sb, u_b = S["u"], S["u_b"]
    wa, wab_b = S["wa"], S["wab_b"]
    wbufs = [(S["wa"][0], S["wab_b"][0]), (S["wa"][1], S["wab_b"][1]), (S["wb"][0], S["wbb_b"][0]), (S["wb"][1], S["wbb_b"][1])]
    rr = {"w": 0}

    def next_w():
        k = rr["w"] % 4
        rr["w"] += 1
        return wbufs[k]

    def mm_acc(ps, ps_b, lhs_fn, rhs_fn, nk, reads, m=128, n=TT):
        for kc in range(nk):
            P.op("pe", lambda e, kc=kc: e.matmul(ps[:m, :n], lhs_fn(kc), rhs_fn(kc), start=(kc == 0), stop=(kc == nk - 1)),
                 reads=reads(kc), writes=[ps_b])

    P.dma(x_sb[:, :, 0:NMEM], memT.rearrange("(c p) n -> p c n", p=128), writes=[x_b])
    rms_rstd(P, C, x_sb, x_b, S["rstd"], S["rstd_b"], S["sq"], S["sq_b"], KC, NMEM, D)
    for c in range(KC):
        P.op("dve", lambda e, c=c: e.scalar_tensor_tensor(out=mn[:, c, :], in0=x_sb[:, c, 0:NMEM], scalar=gxm_t[:, c:c + 1],
                                                        in1=S["rstd"][:, 0:NMEM], op0=ALU.mult, op1=ALU.mult),
             reads=[x_b, gxm_b, S["rstd_b"]], writes=[mn_b])
    P.dma(u_sb[:, 0:16, :], wkv.rearrange("(kc p) n -> p kc n", p=128)[:, :, 512:1024], writes=[u_b[i] for i in range(16)], eng="pool")
    for blk in range(2):
        w_t, w_bf = next_w()
        P.dma(w_t[:, :, :], wk_bf[blk], reads=[wk_b], writes=[w_bf])
        for s in range(2):
            hh = blk * 2 + s
            pk, pk_b = C.next_ps()
            mm_acc(pk, pk_b, lambda kc, w_t=w_t, s=s: w_t[:, kc, s * 128:(s + 1) * 128], lambda kc: mn[:, kc, :], KC,
                   lambda kc, w_bf=w_bf: [w_bf, mn_b], n=NMEM)
            _hn_small(P, C, S, pk, pk_b, gxk_t, gxk_b, kx, kx_b, hh)
    for mc in range(2):
        pv, pv_b = C.next_ps()
        mm_acc(pv, pv_b, lambda kc, mc=mc: mn[:, kc, mc * 128:(mc + 1) * 128], lambda kc: u_sb[:, kc, :], KC,
               lambda kc: [mn_b, u_b[kc]])
        P.op("act", lambda e, mc=mc, pv=pv: e.activation(out=vx[:, mc, :], in_=pv[:, :], func=AF.Copy), reads=[pv_b], writes=[vx_b])

    xin_v = x1T.rearrange("(c p) n -> p c n", p=128)
    xout_v = xoT.rearrange("(c p) n -> p c n", p=128) if with_out else None
    YA, YB, YC, MG = 0, 8, 16, 24
    def tile(t, store=True):
        tok = slice(t * TT, (t + 1) * TT)
        P.dma(x_sb[:, :, :], xin_v[:, :, tok], writes=[x_b])
        norm_to_h(P, C, S, gm_t, gm_b)
        P.dma(u_sb[:, YB:YB + 8, :], ybT.rearrange("(c p) n -> p c n", p=128)[:, :, tok], writes=[u_b[YB + j] for j in range(8)])
        pss, pss_b = C.ps[7], C.psb[7]
        for blk in range(8):
            w_t, w_bf = next_w()
            P.dma(w_t[:, :, :], winC_bf[blk], reads=[winC_b], writes=[w_bf])
            for s in range(2):
                j = (blk % 4) * 2 + s
                ps, ps_b = C.next_ps()
                mm_acc(ps, ps_b, lambda kc, w_t=w_t, s=s: w_t[:, kc, s * 128:(s + 1) * 128], lambda kc: h_sb[:, kc, :], KC,
                       lambda kc, w_bf=w_bf: [w_bf, h_b[kc]])
                src = hmT if blk < 4 else ysT
                P.dma(ytmp[:, :], src[j * 128:(j + 1) * 128, tok], writes=[ytmp_b])
                g_t, g_bf = gsb[j % 2], gsb_b[j % 2]
                if blk < 4:
                    P.op("act", lambda e, ps=ps, g_t=g_t: e.activation(out=g_t[:, :], in_=ps[:, :], func=AF.Sigmoid), reads=[ps_b], writes=[g_bf])
                    P.op("dve", lambda e, j=j, g_t=g_t: e.tensor_tensor(out=u_sb[:, YA + j, :], in0=g_t[:, :], in1=ytmp[:, :], op=ALU.mult),
                         reads=[g_bf, ytmp_b], writes=[u_b[YA + j]])
                else:
                    P.op("act", lambda e, ps=ps, g_t=g_t: e.activation(out=g_t[:, :], in_=ps[:, :], func=AF.Silu), reads=[ps_b], writes=[g_bf])
                    P.op("dve", lambda e, j=j, g_t=g_t: e.tensor_tensor(out=g_t[:, :], in0=g_t[:, :], in1=ytmp[:, :], op=ALU.mult),
                         reads=[g_bf, ytmp_b], writes=[g_bf])
                    P.op("pool", lambda e, j=j, g_t=g_t: e.tensor_copy(out=u_sb[:, YC + j, :], in_=g_t[:, :]), reads=[g_bf], writes=[u_b[YC + j]])
                    sq, sq_b = S["sq"][j % 2], S["sq_b"][j % 2]
                    P.op("act", lambda e, g_t=g_t, sq=sq: e.activation(out=sq[:, :], in_=g_t[:, :], func=AF.Square), reads=[g_bf], writes=[sq_b])
                    P.op("pe", lambda e, j=j, sq=sq: e.matmul(pss[:, :], C.ones[:], sq[:, :], start=(j == 0), stop=(j == 7)),
                         reads=[sq_b, C.ones_b], writes=[pss_b])
        P.op("act", lambda e: e.activation(out=S["rstd2"][:, :], in_=pss[:, :], func=AF.Sqrt, bias=C.eps[:, 0:1], scale=1.0 / 1024),
             reads=[pss_b, C.ones_b], writes=[S["rstd2_b"]])
        P.op("dve", lambda e: e.reciprocal(out=S["rstd2"][:, :], in_=S["rstd2"][:, :]), reads=[S["rstd2_b"]], writes=[S["rstd2_b"]])
        for j in range(8):
            P.op("dve", lambda e, j=j: e.scalar_tensor_tensor(out=u_sb[:, YC + j, :], in0=u_sb[:, YC + j, :], scalar=gs_t[:, j:j + 1], in1=S["rstd2"][:, :],
                                                            op0=ALU.mult, op1=ALU.mult),
                 reads=[u_b[YC + j], gs_b, S["rstd2_b"]], writes=[u_b[YC + j]])
        for ip in range(8):
            for b in range(3):
                wg_t, wg_bf = next_w()
                P.dma(wg_t[:, :, :], winC_bf[8 + b * 8 + ip], reads=[winC_b], writes=[wg_bf])
                wb_t, wb_bf = next_w()
                P.dma(wb_t[:, 0:8, :], wbr_bf[b][ip], reads=[wbr_b[b]], writes=[wb_bf])
                yoff = (YA, YB, YC)[b]
                for s in range(2):
                    i = ip * 2 + s
                    pg, pg_b = C.next_ps()
                    mm_acc(pg, pg_b, lambda kc, wg_t=wg_t, s=s: wg_t[:, kc, s * 128:(s + 1) * 128], lambda kc: h_sb[:, kc, :], KC,
                           lambda kc, wg_bf=wg_bf: [wg_bf, h_b[kc]])
                    pb, pb_b = C.next_ps()
                    mm_acc(pb, pb_b, lambda kc, wb_t=wb_t, s=s: wb_t[:, kc, s * 128:(s + 1) * 128], lambda kc, yoff=yoff: u_sb[:, yoff + kc, :], 8,
                           lambda kc, wb_bf=wb_bf, yoff=yoff: [wb_bf, u_b[yoff + kc]])
                    g_t, g_bf = gsb[s], gsb_b[s]
                    P.op("act", lambda e, pg=pg, g_t=g_t: e.activation(out=g_t[:, :], in_=pg[:, :], func=AF.Sigmoid), reads=[pg_b], writes=[g_bf])
                    if b == 0:
                        P.op("dve", lambda e, pb=pb, g_t=g_t, s=s: e.tensor_tensor(out=macc[s][:, :], in0=g_t[:, :], in1=pb[:, :], op=ALU.mult),
                             reads=[g_bf, pb_b], writes=[macc_b[s]])
                    else:
                        P.op("dve", lambda e, pb=pb, g_t=g_t: e.tensor_tensor(out=g_t[:, :], in0=g_t[:, :], in1=pb[:, :], op=ALU.mult),
                             reads=[g_bf, pb_b], writes=[g_bf])
                        if b == 1:
                            P.op("pool", lambda e, g_t=g_t, s=s: e.tensor_tensor(out=macc[s][:, :], in0=macc[s][:, :], in1=g_t[:, :], op=ALU.add),
                                 reads=[g_bf, macc_b[s]], writes=[macc_b[s]])
                        else:
                            P.op("pool", lambda e, g_t=g_t, s=s, i=i: e.tensor_tensor(out=u_sb[:, MG + i, :], in0=macc[s][:, :], in1=g_t[:, :], op=ALU.add),
                                 reads=[g_bf, macc_b[s]], writes=[u_b[MG + i]])
        for blk in range(8):
            w_t, w_bf = next_w()
            P.dma(w_t[:, :, :], wout_bf[blk], reads=[wout_b], writes=[w_bf])
            for s in range(2):
                i = blk * 2 + s
                ps, ps_b = C.next_ps()
                mm_acc(ps, ps_b, lambda kc, w_t=w_t, s=s: w_t[:, kc, s * 128:(s + 1) * 128], lambda kc: u_sb[:, MG + kc, :], KC,
                       lambda kc, w_bf=w_bf: [w_bf, u_b[MG + kc]])
                P.op("dve", lambda e, i=i, ps=ps: e.tensor_tensor(out=x_sb[:, i, :], in0=x_sb[:, i, :], in1=ps[:, :], op=ALU.add),
                     reads=[ps_b, x_b], writes=[x_b])
        norm_to_h(P, C, S, gxa_t, gxa_b)
        QX, OX, PX = 0, 4, 8
        for blk in range(2):
            w_t, w_bf = next_w()
            P.dma(w_t[:, :, :], wq_bf[blk], reads=[wq_b], writes=[w_bf])
            for s in range(2):
                hh = blk * 2 + s
                ps, ps_b = C.next_ps()
                mm_acc(ps, ps_b, lambda kc, w_t=w_t, s=s: w_t[:, kc, s * 128:(s + 1) * 128], lambda kc: h_sb[:, kc, :], KC,
                       lambda kc, w_bf=w_bf: [w_bf, h_b[kc]])
                head_norm(P, C, S, ps, ps_b, 128, gxq_t[:, 0:1], gxq_b, u_sb[:, QX + hh, :], u_b[QX + hh])
        for hh in range(4):
            for mc in range(2):
                psc, psc_b = C.next_ps()
                P.op("pe", lambda e, hh=hh, mc=mc, psc=psc: e.matmul(psc[:, :], kx[:, hh, mc * 128:(mc + 1) * 128], u_sb[:, QX + hh, :], start=True, stop=True),
                     reads=[kx_b, u_b[QX + hh]], writes=[psc_b])
                P.op("act", lambda e, mc=mc, psc=psc: e.activation(out=u_sb[:, PX + mc, :], in_=psc[:, :], func=AF.Exp, bias=negc[:, 0:1]),
                     reads=[psc_b, negc_b], writes=[u_b[PX + mc]])
            po, po_b = C.next_ps()
            pd, pd_b = C.next_ps()
            for mc in range(2):
                P.op("pe", lambda e, hh=hh, mc=mc, po=po: e.matmul(po[:, :], vx[:, mc, hh * 128:(hh + 1) * 128], u_sb[:, PX + mc, :], start=(mc == 0), stop=(mc == 1)),
                     reads=[vx_b, u_b[PX + mc]], writes=[po_b])
            for mc in range(2):
                P.op("pe", lambda e, mc=mc, pd=pd: e.matmul(pd[:, :], onesb[:, :], u_sb[:, PX + mc, :], start=(mc == 0), stop=(mc == 1)),
                     reads=[onesb_b, u_b[PX + mc]], writes=[pd_b])
            P.op("dve", lambda e, pd=pd: e.reciprocal(out=S["rstd2"][:, :], in_=pd[:, :]), reads=[pd_b], writes=[S["rstd2_b"]])
            P.op("dve", lambda e, hh=hh, po=po: e.tensor_tensor(out=u_sb[:, OX + hh, :], in0=po[:, :], in1=S["rstd2"][:, :], op=ALU.mult),
                 reads=[po_b, S["rstd2_b"]], writes=[u_b[OX + hh]])
        for blk in range(8):
            w_t, w_bf = next_w()
            P.dma(w_t[:, 0:4, :], wo_bf[blk], reads=[wo_b], writes=[w_bf])
            for s in range(2):
                i = blk * 2 + s
                ps, ps_b = C.next_ps()
                mm_acc(ps, ps_b, lambda kc, w_t=w_t, s=s: w_t[:, kc, s * 128:(s + 1) * 128], lambda kc: u_sb[:, OX + kc, :], 4,
                       lambda kc, w_bf=w_bf: [w_bf, u_b[OX + kc]])
                P.op("dve", lambda e, i=i, ps=ps: e.tensor_tensor(out=x_sb[:, i, :], in0=x_sb[:, i, :], in1=ps[:, :], op=ALU.add),
                     reads=[ps_b, x_b], writes=[x_b])
        ffn_tile(P, C, S, g2_t, g2_b, wup_bf, wup_b, wdn_bf, wdn_b)
        if store:
            P.dma(xout_v[:, :, tok], x_sb[:, :, :], reads=[x_b])

    return dict(locals())


def build_C(ntok):
    P = Prog()
    C = Ctx(P)
    S = alloc_shared(P)
    Cc = setup_C(P, C, S, ntok, pre="c_")
    for t in range(ntok // TT):
        Cc["tile"](t, True)
    return P


def build_CA(ntok):
    P = Prog()
    C = Ctx(P)
    S = alloc_shared(P)
    Cc = setup_C(P, C, S, ntok, pre="c_", with_out=False)
    A = setup_A(P, C, S, ntok, pre="a_", with_x_in=False, resident_kv=False)
    for t in range(ntok // TT):
        Cc["tile"](t, False)
        tile_A(P, C, S, A, t, ntok)
    return P


def _hn_small(P, C, S, pk, pk_b, g_t, g_b, kx, kx_b, hh):
    t, t_b = next_tmp(S)
    sq, sq_b = S["sq"][0], S["sq_b"][0]
    n = NMEM
    P.op("act", lambda e: e.activation(out=t[:, :n], in_=pk[:, :n], func=AF.Copy), reads=[pk_b], writes=[t_b])
    P.op("act", lambda e: e.activation(out=sq[:, :n], in_=pk[:, :n], func=AF.Square), reads=[pk_b], writes=[sq_b])
    p2, p2_b = C.next_ps()
    P.op("pe", lambda e: e.matmul(p2[:, :n], C.ones[:, :], sq[:, :n], start=True, stop=True), reads=[sq_b, C.ones_b], writes=[p2_b])
    r, r_b = S["rstd2"], S["rstd2_b"]
    P.op("act", lambda e: e.activation(out=r[:, :n], in_=p2[:, :n], func=AF.Sqrt, bias=C.eps[:, 0:1], scale=1.0 / 128), reads=[p2_b, C.ones_b], writes=[r_b])
    P.op("dve", lambda e: e.reciprocal(out=r[:, :n], in_=r[:, :n]), reads=[r_b], writes=[r_b])
    P.op("dve", lambda e: e.scalar_tensor_tensor(out=kx[:, hh, :], in0=t[:, :n], scalar=g_t[:, 0:1], in1=r[:, :n], op0=ALU.mult, op1=ALU.mult),
         reads=[t_b, r_b, g_b], writes=[kx_b])


B_SZ, SEQ, NCORE = 2, 16384, 8
NTOK = 4096
NQB = 32
BLK_SADM = [512 * (i + 1) for i in range(NQB)]
_PROGS = {}


def _prog(name):
    if name not in _PROGS:
        if name == "A":
            _PROGS[name] = build_A(NTOK).emit()
        elif name == "B":
            _PROGS[name] = build_B(SEQ, NQB, BLK_SADM).emit()
        elif name == "CA":
            _PROGS[name] = build_CA(NTOK).emit()
        else:
            _PROGS[name] = build_C(NTOK).emit()
    return _PROGS[name]


def _pc(v, c):
    return np.ascontiguousarray(np.asarray(v, np.float32).reshape(c, -1).T)


def _rep(v):
    v = np.asarray(v, np.float32)
    return np.ascontiguousarray(np.broadcast_to(v[None], (128,) + v.shape))


def _run(nc, in_maps):
    res = run_bass_kernel_spmd(nc, in_maps, core_ids=list(range(NCORE)))
    return res.results


def kernel(x, mem, ffn1_norm, ffn1_w_up, ffn1_w_down, mix_norm, w_in,
           ml_conv, ml_i_bias, ml_f_bias, ml_out_norm,
           dsa_q_norm, dsa_k_norm, dsa_kv_norm, dsa_w_uk, dsa_w_uv, idx_k_norm,
           ssm_conv, ssm_conv_b, ssm_dt_bias, ssm_a_log, ssm_d, ssm_norm,
           w_branch, w_out, xa_norm, xa_mem_norm, xa_wq, xa_wkv, xa_q_norm, xa_k_norm, xa_wo,
           ffn2_norm, ffn2_w_up, ffn2_w_down):
    A_ = np.ascontiguousarray
    f32 = np.float32
    x = np.asarray(x, f32)
    mem = np.asarray(mem, f32)
    main, small, later = perm_cols_A()
    nch = SEQ // 128
    xT = [A_(x[c // 4, (c % 4) * NTOK:(c % 4 + 1) * NTOK, :].T) for c in range(NCORE)]

    def a_inputs(l):
        w_in_l = np.asarray(w_in[l], f32)
        d = {"g1": _pc(ffn1_norm[l], 16), "wup": A_(ffn1_w_up[l]), "wdn": A_(ffn1_w_down[l]), "gmix": _pc(mix_norm[l], 16),
             "winA": A_(w_in_l[:, main]), "winS": A_(w_in_l[:, small]),
             "gq": A_(np.asarray(dsa_q_norm[l], f32).reshape(128, 1)), "gk": A_(np.asarray(dsa_k_norm[l], f32).reshape(128, 1)),
             "gkv": _pc(dsa_kv_norm[l], 4), "gik": A_(np.asarray(idx_k_norm[l], f32).reshape(64, 1)),
             "wuk": A_(dsa_w_uk[l]), "wuv": A_(dsa_w_uv[l])}
        return {"a_" + k: v for k, v in d.items()}

    def c_inputs(l):
        w_in_l = np.asarray(w_in[l], f32)
        d = {"gmix": _pc(mix_norm[l], 16), "gssm": _pc(ssm_norm[l], 8), "winC": A_(w_in_l[:, later]), "wbr": A_(w_branch[l]), "wout": A_(w_out[l]),
             "gxa": _pc(xa_norm[l], 16), "gxm": _pc(xa_mem_norm[l], 16),
             "gxq": A_(np.asarray(xa_q_norm[l], f32).reshape(128, 1)), "gxk": A_(np.asarray(xa_k_norm[l], f32).reshape(128, 1)),
             "wq": A_(xa_wq[l]), "wkv": A_(xa_wkv[l]), "wo": A_(xa_wo[l]), "g2": _pc(ffn2_norm[l], 16),
             "wup": A_(ffn2_w_up[l]), "wdn": A_(ffn2_w_down[l])}
        return {"c_" + k: v for k, v in d.items()}

    comA = a_inputs(0)
    resA = _run(_prog("A"), [dict(comA, a_xT=xT[c]) for c in range(NCORE)])
    del comA
    for l in range(2):
        x1T = [r["a_x1T"] for r in resA]
        in_B = []
        for b in range(B_SZ):
            PT = np.concatenate([resA[b * 4 + q]["a_PT"] for q in range(4)], axis=1)
            kiT = np.concatenate([resA[b * 4 + q]["a_kiT"] for q in range(4)], axis=1)
            smT = np.concatenate([resA[b * 4 + q]["a_smT"] for q in range(4)], axis=1)
            d_kT = A_(PT[24 * 128:32 * 128].reshape(8, 128, SEQ).transpose(1, 0, 2))
            d_v = A_(PT[32 * 128:40 * 128].T)
            d_ki2 = A_(np.concatenate([kiT, kiT], axis=0))
            qd = PT[16 * 128:24 * 128].reshape(8, 128, nch, 128)
            qi = PT[40 * 128:48 * 128].reshape(8, 128, nch, 128)
            iw = smT[8:24].reshape(16, nch, 128)
            for g in range(4):
                blks = np.arange(NQB) * 4 + g
                pos = blks[None, :] * 128 + np.arange(128)[:, None]
                lim = ((pos // 64 + 1) * 64).astype(f32)
                gr = g // 2
                chs = [np.arange(256 * g, 256 * g + 128), np.arange(256 * g + 128, 256 * g + 256),
                       np.arange(1024 + 128 * gr, 1024 + 128 * gr + 128), np.arange(1280 + 128 * gr, 1280 + 128 * gr + 128)]
                conv_s = np.asarray(ssm_conv[l], f32)
                convb_s = np.asarray(ssm_conv_b[l], f32)
                conv_m = np.asarray(ml_conv[l], f32)
                m = {
                    "ml_qk": A_(np.stack([PT[g * 128:(g + 1) * 128], PT[512 + g * 128:512 + (g + 1) * 128]])),
                    "ml_v": A_(PT[1024 + g * 256:1024 + (g + 1) * 256].T),
                    "ml_if": A_(np.stack([smT[g].reshape(nch, 128).T, smT[4 + g].reshape(nch, 128).T], axis=1)),
                    "ml_cw": A_(np.concatenate([conv_m[:, g * 128:(g + 1) * 128].T, conv_m[:, 512 + g * 128:512 + (g + 1) * 128].T], axis=1)),
                    "ml_bias": _rep(np.array([np.asarray(ml_i_bias[l], f32)[g], np.asarray(ml_f_bias[l], f32)[g]], f32)),
                    "ml_gn": _rep(np.asarray(ml_out_norm[l], f32)[g * 256:(g + 1) * 256]),
                    "ss_raw": A_(np.stack([PT[(48 + 2 * g) * 128:(49 + 2 * g) * 128], PT[(49 + 2 * g) * 128:(50 + 2 * g) * 128],
                                           PT[(56 + gr) * 128:(57 + gr) * 128], PT[(58 + gr) * 128:(59 + gr) * 128]])),
                    "ss_dt": A_(np.stack([smT[24 + 4 * g + h].reshape(nch, 128).T for h in range(4)], axis=1)),
                    "ss_cw": A_(np.stack([conv_s[:, ch].T for ch in chs], axis=1)),
                    "ss_cb": A_(np.stack([convb_s[ch] for ch in chs], axis=1)),
                    "ss_par": _rep(np.stack([np.asarray(ssm_dt_bias[l], f32)[4 * g:4 * g + 4], np.asarray(ssm_a_log[l], f32)[4 * g:4 * g + 4],
                                             np.asarray(ssm_d[l], f32)[4 * g:4 * g + 4]])),
                    "d_q": A_(qd[:, :, blks, :].transpose(2, 1, 0, 3)),
                    "d_iq": A_(qi[:, :, blks, :].transpose(2, 1, 0, 3)),
                    "d_w": A_(iw[:, blks, :].transpose(2, 1, 0)),
                    "d_lim": A_(lim),
                    "d_kT": d_kT, "d_v": d_v, "d_ki2": d_ki2,
                }
                in_B.append(m)
        del resA
        resB = _run(_prog("B"), in_B)
        del in_B
        commonC = c_inputs(l)
        if l == 0:
            commonC.update(a_inputs(1))
        in_C = []
        for b in range(B_SZ):
            hm = np.concatenate([resB[b * 4 + g]["ml_h"] for g in range(4)], axis=1)
            ys = np.concatenate([resB[b * 4 + g]["ss_y"] for g in range(4)], axis=1)
            yb = np.empty((nch, 128, 1024), dtype=resB[0]["d_y"].dtype)
            for g in range(4):
                yb[np.arange(NQB) * 4 + g] = resB[b * 4 + g]["d_y"]
            yb = yb.reshape(SEQ, 1024)
            memT = A_(mem[b].T)
            for q in range(4):
                tok = slice(q * NTOK, (q + 1) * NTOK)
                in_C.append(dict(commonC, c_x1T=x1T[b * 4 + q], c_hmT=A_(hm[tok].T), c_ybT=A_(yb[tok].T), c_ysT=A_(ys[tok].T), c_memT=memT))
        del resB
        if l == 0:
            resA = _run(_prog("CA"), in_C)
        else:
            resC = _run(_prog("C"), in_C)
            xT = [r["c_xoT"] for r in resC]
        del in_C, commonC
    out = np.empty((B_SZ, SEQ, D), f32)
    for c in range(NCORE):
        out[c // 4, (c % 4) * NTOK:(c % 4 + 1) * NTOK, :] = xT[c].T
    return out
```
